# Optimizing a Trainium2 kernel written in Bass

```python
import math
import jax, jax.numpy as jnp
from jax import lax
import numpy as np

D_MODEL = 1024
BATCH = 2
SEQ = 8192
DEPTH = 4

N_MIXERS = 2
Q_BLOCK = 128
EPS = 1e-6
POS_OFFSET_MAX = 4096

MLA_HEADS = 16
MLA_NOPE = 64
MLA_ROPE = 32
MLA_V = 64
MLA_Q_RANK = 384
MLA_KV_RANK = 256
ROPE_BASE = 10000.0
MLA_GATE = MLA_HEADS * MLA_V
MLA_IN = MLA_Q_RANK + MLA_KV_RANK + MLA_ROPE + MLA_GATE

DIFF_HD = 64
DIFF_HEADS = D_MODEL // (2 * DIFF_HD)
DIFF_W = DIFF_HEADS * 2 * DIFF_HD
DIFF_IN = 4 * DIFF_W

N_MLA = (DEPTH + 1) // 2
N_DIFF = DEPTH // 2

kernel_name = "hybrid_mla_diffattn_adaln_encoder"


def rmsnorm(x, g):
    xf = x.astype(jnp.float32)
    y = xf * lax.rsqrt(jnp.mean(xf * xf, axis=-1, keepdims=True) + EPS)
    return (y * g.astype(jnp.float32)).astype(x.dtype)


def blockwise(fn, *arrs):
    b, s = arrs[0].shape[:2]
    nb = s // Q_BLOCK
    blocks = tuple(a.reshape((b, nb, Q_BLOCK) + a.shape[2:]).swapaxes(0, 1) for a in arrs)
    out = lax.map(fn, blocks)
    return out.swapaxes(0, 1).reshape((b, s) + out.shape[3:])


def rope_tables(positions):
    inv = ROPE_BASE ** (-jnp.arange(0, MLA_ROPE, 2, dtype=jnp.float32) / MLA_ROPE)
    ang = positions.astype(jnp.float32)[..., None] * inv
    return jnp.cos(ang), jnp.sin(ang)


def apply_rope(x, cos, sin):
    half = x.shape[-1] // 2
    x1, x2 = x[..., :half], x[..., half:]
    cos = cos.astype(x.dtype)
    sin = sin.astype(x.dtype)
    return jnp.concatenate([x1 * cos - x2 * sin, x1 * sin + x2 * cos], axis=-1)


def mla_mixer(h, positions, w_in, q_norm_g, w_q_up, kv_norm_g, w_kv_up, w_o):
    b, s, _ = h.shape
    proj = h @ w_in
    o1 = MLA_Q_RANK
    o2 = o1 + MLA_KV_RANK
    o3 = o2 + MLA_ROPE
    q_lat, kv_lat, k_rope, gate = proj[..., :o1], proj[..., o1:o2], proj[..., o2:o3], proj[..., o3:]
    q = (rmsnorm(q_lat, q_norm_g) @ w_q_up).reshape(b, s, MLA_HEADS, MLA_NOPE + MLA_ROPE)
    kv = (rmsnorm(kv_lat, kv_norm_g) @ w_kv_up).reshape(b, s, MLA_HEADS, MLA_NOPE + MLA_V)
    k_nope, v = kv[..., :MLA_NOPE], kv[..., MLA_NOPE:]
    cos, sin = rope_tables(positions)
    scale = (MLA_NOPE + MLA_ROPE) ** -0.5
    q_nope = q[..., :MLA_NOPE] * scale
    q_rope = apply_rope(q[..., MLA_NOPE:], cos[:, :, None], sin[:, :, None]) * scale
    k_rope = apply_rope(k_rope, cos, sin)

    def block(args):
        qn, qr = args
        sc = (jnp.einsum('bqhd,bkhd->bhqk', qn, k_nope)
              + jnp.einsum('bqhd,bkd->bhqk', qr, k_rope))
        p = jax.nn.softmax(sc.astype(jnp.float32), axis=-1)
        return jnp.einsum('bhqk,bkhd->bqhd', p.astype(v.dtype), v)

    o = blockwise(block, q_nope, q_rope)
    return (o.reshape(b, s, MLA_GATE) * jax.nn.silu(gate)) @ w_o


def diff_mixer(h, positions, w_in, lq1, lk1, lq2, lk2, head_g, w_o, lambda_init):
    b, s, _ = h.shape
    proj = h @ w_in
    q, k, v, gate = (proj[..., i * DIFF_W:(i + 1) * DIFF_W] for i in range(4))
    q = q.reshape(b, s, DIFF_HEADS, 2, DIFF_HD) * (DIFF_HD ** -0.5)
    k = k.reshape(b, s, DIFF_HEADS, 2, DIFF_HD)
    v = v.reshape(b, s, DIFF_HEADS, 2 * DIFF_HD)
    lam = (jnp.exp(jnp.sum(lq1.astype(jnp.float32) * lk1.astype(jnp.float32)))
           - jnp.exp(jnp.sum(lq2.astype(jnp.float32) * lk2.astype(jnp.float32)))
           + lambda_init)
    slopes = jnp.exp2(-8.0 * jnp.arange(1, DIFF_HEADS + 1, dtype=jnp.float32) / DIFF_HEADS)

    def block(args):
        qb, pb = args
        sc = jnp.einsum('bqhcd,bkhcd->bhcqk', qb, k).astype(jnp.float32)
        dist = jnp.abs(pb[:, :, None] - positions[:, None, :]).astype(jnp.float32)
        bias = -slopes[None, :, None, None, None] * dist[:, None, None]
        p = jax.nn.softmax(sc + bias, axis=-1)
        a = p[:, :, 0] - lam * p[:, :, 1]
        return jnp.einsum('bhqk,bkhd->bqhd', a.astype(v.dtype), v)

    o = blockwise(block, q, positions)
    o = rmsnorm(o, head_g) * (1.0 - lambda_init)
    return (o.reshape(b, s, DIFF_W) * jax.nn.silu(gate)) @ w_o


def setup_inputs(seed: int = 0) -> dict:
    key = jax.random.key(seed)
    ks = jax.random.split(key, 24)
    f32 = jnp.float32
    D = D_MODEL

    def nrm(k, shape, fan_in):
        return jax.random.normal(k, shape, f32) * (fan_in ** -0.5)

    def gain(k, shape):
        return 1.0 + 0.02 * jax.random.normal(k, shape, f32)

    x = jax.random.normal(ks[0], (BATCH, SEQ, D), f32)
    c = jax.random.normal(ks[1], (BATCH, D), f32)
    offs = jax.random.randint(ks[2], (BATCH, 1), 0, POS_OFFSET_MAX, dtype=jnp.int32)
    positions = (offs + jnp.arange(SEQ, dtype=jnp.int32)[None, :]).astype(jnp.int32)
    return {
        "x": x,
        "c": c,
        "positions": positions,
        "ada_w": nrm(ks[3], (DEPTH, D, 3 * D), D),
        "ada_b": 0.02 * jax.random.normal(ks[4], (DEPTH, 3 * D), f32),
        "norm_g": gain(ks[5], (DEPTH, D)),
        "mla_w_in": nrm(ks[6], (N_MLA, D, MLA_IN), D),
        "mla_q_norm_g": gain(ks[7], (N_MLA, MLA_Q_RANK)),
        "mla_w_q_up": nrm(ks[8], (N_MLA, MLA_Q_RANK, MLA_HEADS * (MLA_NOPE + MLA_ROPE)), MLA_Q_RANK),
        "mla_kv_norm_g": gain(ks[9], (N_MLA, MLA_KV_RANK)),
        "mla_w_kv_up": nrm(ks[10], (N_MLA, MLA_KV_RANK, MLA_HEADS * (MLA_NOPE + MLA_V)), MLA_KV_RANK),
        "mla_w_o": nrm(ks[11], (N_MLA, MLA_GATE, D), MLA_GATE),
        "diff_w_in": nrm(ks[12], (N_DIFF, D, DIFF_IN), D),
        "diff_lq1": 0.1 * jax.random.normal(ks[13], (N_DIFF, DIFF_HD), f32),
        "diff_lk1": 0.1 * jax.random.normal(ks[14], (N_DIFF, DIFF_HD), f32),
        "diff_lq2": 0.1 * jax.random.normal(ks[15], (N_DIFF, DIFF_HD), f32),
        "diff_lk2": 0.1 * jax.random.normal(ks[16], (N_DIFF, DIFF_HD), f32),
        "diff_head_g": gain(ks[17], (N_DIFF, 2 * DIFF_HD)),
        "diff_w_o": nrm(ks[18], (N_DIFF, DIFF_W, D), DIFF_W),
        "final_g": gain(ks[19], (D,)),
    }


def reference(x, c, positions, ada_w, ada_b, norm_g,
              mla_w_in, mla_q_norm_g, mla_w_q_up, mla_kv_norm_g, mla_w_kv_up, mla_w_o,
              diff_w_in, diff_lq1, diff_lk1, diff_lq2, diff_lk2, diff_head_g, diff_w_o,
              final_g):
    c_act = jax.nn.silu(c)
    for i in range(DEPTH):
        mod = c_act @ ada_w[i] + ada_b[i]
        shift, scale, gate = mod[:, :D_MODEL], mod[:, D_MODEL:2 * D_MODEL], mod[:, 2 * D_MODEL:]
        h = rmsnorm(x, norm_g[i]) * (1.0 + scale[:, None, :]) + shift[:, None, :]
        j = i // N_MIXERS
        if i % N_MIXERS == 0:
            y = mla_mixer(h, positions, mla_w_in[j], mla_q_norm_g[j], mla_w_q_up[j],
                          mla_kv_norm_g[j], mla_w_kv_up[j], mla_w_o[j])
        else:
            lambda_init = 0.8 - 0.6 * math.exp(-0.3 * i)
            y = diff_mixer(h, positions, diff_w_in[j], diff_lq1[j], diff_lk1[j],
                           diff_lq2[j], diff_lk2[j], diff_head_g[j], diff_w_o[j], lambda_init)
        x = x + gate[:, None, :] * y
    return rmsnorm(x, final_g)
```

```python
import math
import numpy as np
import concourse.bass as bass
import concourse.mybir as mybir
from concourse.bass_utils import run_bass_kernel_spmd

F32 = mybir.dt.float32
BF16 = mybir.dt.bfloat16
I32 = mybir.dt.int32
U16 = mybir.dt.uint16
AF = mybir.ActivationFunctionType
ALU = mybir.AluOpType
AX = mybir.AxisListType

D = 1024
S = 8192
T = 2048
NCORE = 8
EPS = 1e-6
MLA_IN = 1696
LAYERS = [0, 1, 2, 3]
GROUPS = [[0, 1, 2, 3], [4, 5, 6, 7]]


import os


class StopBuild(Exception):
    pass


def ckpt(n):
    if int(os.environ.get("KSTOP", "99")) == n:
        raise StopBuild()


class Ctx:
    def __init__(self, nc):
        self.nc = nc
        self.E = {"pe": nc.tensor, "act": nc.scalar, "dve": nc.vector, "pool": nc.gpsimd, "sp": nc.sync}
        self.sems = {}
        self.cnt = {}
        self.cur = {}
        self.epoch = 0
        for e in ("pe", "act", "dve", "pool"):
            self._new_csem(e)
        self.NR = 8
        self.ring = {}
        for q in ("sp", "pool", "act"):
            self.ring[q] = 0
            for i in range(self.NR):
                k = f"d_{q}_{i}"
                self.sems[k] = nc.alloc_semaphore(k)
                self.cnt[k] = 0
        self.ccn = 0
        for i in range(12):
            k = f"cc_{i}"
            self.sems[k] = nc.alloc_semaphore(k)
            self.cnt[k] = 0
        self.seen = {e: {} for e in self.E}
        self.lastw = {}
        self.readers = {}
        self.ninst = 0

    def _new_csem(self, e):
        k = f"c_{e}_{self.epoch}"
        self.sems[k] = self.nc.alloc_semaphore(k)
        self.cnt[k] = 0
        self.cur[e] = k

    def new_epoch(self):
        self.barrier()
        self.epoch += 1
        for e in ("pe", "act", "dve", "pool"):
            self._new_csem(e)

    def _need(self, eng, ev):
        if ev is None:
            return
        key, val = ev
        if val <= 0 or self.seen[eng].get(key, 0) >= val:
            return
        self.E[eng].wait_ge(self.sems[key], val)
        self.seen[eng][key] = val
        self.ninst += 1

    def _deps(self, eng, reads, writes):
        own = self.cur.get(eng)
        for t in reads:
            ev = self.lastw.get(t)
            if ev is not None and not (eng == "pe" and ev[0] == own):
                self._need(eng, ev)
        for t in writes:
            ev = self.lastw.get(t)
            if ev is not None and not (eng == "pe" and ev[0] == own):
                self._need(eng, ev)
            for k, v in self.readers.get(t, {}).items():
                if not (eng == "pe" and k == own):
                    self._need(eng, (k, v))

    def _record(self, ev, reads, writes):
        for t in writes:
            self.lastw[t] = ev
            self.readers[t] = {}
        for t in reads:
            d = self.readers.setdefault(t, {})
            if d.get(ev[0], 0) < ev[1]:
                d[ev[0]] = ev[1]

    def op(self, eng, fn, reads=(), writes=(), inc=True):
        excl = [t for t in reads if t.startswith("bank") or t in ("S0", "S1", "S2")]
        if excl:
            writes = list(writes) + excl
        self._deps(eng, reads, writes)
        ins = fn(self.E[eng])
        self.ninst += 1
        key = self.cur[eng]
        if inc:
            self.cnt[key] += 1
            ins.then_inc(self.sems[key], 1)
            ev = (key, self.cnt[key])
        else:
            ev = (key, self.cnt[key] + 1)
        self._record(ev, reads, writes)
        return ev

    def dma(self, q, out, in_, reads=(), writes=()):
        self._deps(q, reads, writes)
        i = self.ring[q]
        self.ring[q] = (i + 1) % self.NR
        key = f"d_{q}_{i}"
        self._need(q, (key, self.cnt[key]))
        ins = self.E[q].dma_start(out=out, in_=in_)
        self.ninst += 1
        self.cnt[key] += 16
        ins.then_inc(self.sems[key], 16)
        ev = (key, self.cnt[key])
        self._record(ev, reads, writes)
        return ev

    def collective(self, in_ap, out_ap, reads=(), writes=()):
        q = "pool"
        if os.environ.get("KNOCC"):
            return None
        self._deps(q, reads, writes)
        key = f"cc_{self.ccn % 12}"
        self.ccn += 1
        self._need(q, (key, self.cnt[key]))
        ins = self.E[q].collective_compute("AllGather", ALU.bypass, replica_groups=GROUPS,
                                          ins=[in_ap], outs=[out_ap])
        self.ninst += 1
        self.cnt[key] += 1
        ins.then_inc(self.sems[key], 1)
        ev = (key, self.cnt[key])
        self._record(ev, reads, writes)
        return ev

    def barrier(self):
        for eng in self.E:
            for key, c in self.cnt.items():
                if c > 0 and not key.startswith("cc_"):
                    self._need(eng, (key, c))
        keep = {t: ev for t, ev in self.lastw.items() if ev[0].startswith("cc_")}
        self.lastw.clear()
        self.readers.clear()
        self.lastw.update(keep)


def build_program(layers):
    nc = bass.Bass("TRN2", target_bir_lowering=False)
    dt = nc.dram_tensor

    x_in = dt("xT_in", [D, T], F32, kind="ExternalInput")
    y_out = dt("yT_out", [D, T], F32, kind="ExternalOutput")
    c_lay = dt("c_lay", [128, 8], F32, kind="ExternalInput")
    pos_own = dt("pos_own", [128, T], I32, kind="ExternalInput")
    pos_all = dt("pos_all", [128, 64], I32, kind="ExternalInput")
    pos_c0 = dt("pos_c0", [128, 1], I32, kind="ExternalInput")
    ada_w = dt("ada_w", [4, D, 3 * D], F32, kind="ExternalInput")
    ada_b = dt("ada_b_lay", [4, 128, 24], F32, kind="ExternalInput")
    norm_g = dt("norm_g_lay", [4, 128, 8], F32, kind="ExternalInput")
    final_g = dt("final_g_lay", [128, 8], F32, kind="ExternalInput")
    m_w_in = dt("m_w_in", [2, D, MLA_IN], F32, kind="ExternalInput")
    m_w_rot = dt("m_w_rot", [2, D, 32], F32, kind="ExternalInput")
    m_gq = dt("m_gq", [2, 128, 3], F32, kind="ExternalInput")
    m_gkv = dt("m_gkv", [2, 128, 2], F32, kind="ExternalInput")
    m_wq = dt("m_wq", [2, 384, 16 * 96], F32, kind="ExternalInput")
    m_wqr = dt("m_wqr", [2, 384, 16 * 32], F32, kind="ExternalInput")
    m_wk = dt("m_wk", [2, 256, 1024], F32, kind="ExternalInput")
    m_wv = dt("m_wv", [2, 256, 1024], F32, kind="ExternalInput")
    m_wo = dt("m_wo", [2, D, D], F32, kind="ExternalInput")
    d_w_in = dt("d_w_in", [2, D, 4096], F32, kind="ExternalInput")
    d_wo = dt("d_wo", [2, D, D], F32, kind="ExternalInput")
    d_hg = dt("d_hg", [128, 2], F32, kind="ExternalInput")
    d_lq = dt("d_lq", [2, 4, 128, 64], F32, kind="ExternalInput")
    k_ind2 = dt("k_ind2", [128, 8 * 16], F32, kind="ExternalInput")
    k_indh = dt("k_indh", [128, 16 * 16], F32, kind="ExternalInput")
    k_invf = dt("k_invf", [128, 1], F32, kind="ExternalInput")
    k_sgn = dt("k_sgn", [128, 1], F32, kind="ExternalInput")

    xs = dt("xs", [D, T], F32)
    cosd = dt("cosd", [128, T], F32)
    sind = dt("sind", [128, T], F32)
    posd = dt("posd", [128, T], F32)
    nbd = dt("nbd", [4, 64], F32)
    ttd = dt("ttd", [256, 128, 512], U16)
    qs = [dt(f"qs{l}", [16, 128, T], BF16) for l in range(4)]
    kown = [[dt(f"kown{l}_{g}", [256, T], BF16) for g in range(4)] for l in range(4)]
    kall = [[dt(f"kall{l}_{g}", [1024, T], BF16) for g in range(4)] for l in range(4)]
    vown = [[dt(f"vown{l}_{g}", [T, 256], BF16) for g in range(4)] for l in range(4)]
    vall = [[dt(f"vall{l}_{g}", [4 * T, 256], BF16) for g in range(4)] for l in range(4)]
    smown = [dt(f"smown{l}", [32, T], BF16) for l in range(4)]
    small = [dt(f"small{l}", [128, T], BF16) for l in range(4)]
    stown = [dt(f"stown{l}", [16, 8], F32) for l in range(4)]
    stall = [dt(f"stall{l}", [64, 8], F32) for l in range(4)]

    sb = nc.alloc_sbuf_tensor if hasattr(nc, "alloc_sbuf_tensor") else None

    def sbt(name, shape, dtype):
        cm = nc.sbuf_tensor(name, shape, dtype)
        return cm.__enter__()

    def pst(name, shape, dtype):
        cm = nc.psum_tensor(name, shape, dtype)
        return cm.__enter__()

    ARENA_N = 200 * 1024 // 2
    arena = sbt("arena", [128, ARENA_N], BF16)

    def carve(off_bytes, nelem, dtype):
        a = arena[:, off_bytes // 2: off_bytes // 2 + (nelem * (2 if dtype in (BF16, U16) else 4)) // 2]
        if dtype != BF16:
            a = a.bitcast(dtype)
        return a

    ones_bf = sbt("ones_bf", [128, 128], BF16)
    ident_unused = None
    cact = sbt("cact", [128, 8], BF16)
    ctmp = sbt("ctmp", [128, 8], F32)
    mods = sbt("mods", [128, 4 * 24], F32)
    adab = sbt("adab", [128, 4 * 24], F32)
    Acoef = sbt("Acoef", [128, 4 * 8], F32)
    gnorm = sbt("gnorm", [128, 4 * 8], F32)
    fgl = sbt("fgl", [128, 8], F32)
    gq = sbt("gq", [128, 2 * 3], F32)
    gkv = sbt("gkv", [128, 2 * 2], F32)
    hgs = sbt("hgs", [128, 2], F32)
    lamt = sbt("lamt", [128, 8], F32)
    lqt = sbt("lqt", [128, 4 * 64], F32)
    lqp = sbt("lqp", [128, 64], F32)
    neglam = sbt("neglam", [128, 2], F32)
    ind2f = sbt("ind2f", [128, 128], F32)
    indhf = sbt("indhf", [128, 256], F32)
    ind2 = sbt("ind2", [128, 128], BF16)
    indh = sbt("indh", [128, 256], BF16)
    invf = sbt("invf", [128, 1], F32)
    sgn = sbt("sgn", [128, 1], F32)
    epst = sbt("epst", [128, 1], F32)
    zerot = sbt("zerot", [128, 1], F32)
    pkrel = sbt("pkrel", [128, 64], F32)
    pki = sbt("pki", [128, 64], I32)
    negpk = sbt("negpk", [128, 64], F32)
    c0i = sbt("c0i", [128, 1], I32)
    c0f = sbt("c0f", [128, 1], F32)
    qmx = sbt("qmx", [16, 8], F32)
    kmx = sbt("kmx", [16, 8], F32)
    qmax = sbt("qmax", [16, 4], F32)
    stl = sbt("stl", [16, 32], F32)
    kmax = sbt("kmax", [16, 1], F32)
    nb16 = sbt("nb16", [16, 4], F32)
    negb = sbt("negb", [128, 64], F32)
    negbh = sbt("negbh", [128, 32], F32)

    PS = [pst(f"ps{i}", [128, 1024], F32) for i in range(4)]

    def bank(i):
        return PS[i // 2][:, (i % 2) * 512:(i % 2) * 512 + 512]

    def btok(i):
        return f"bank{i}"

    blk = nc.Block()
    blk.__enter__()
    cx = Ctx(nc)

    def v3(ap, k):
        return ap.rearrange("p (k t) -> p k t", k=k)

    xT = v3(carve(0, 8 * T, F32), 8)
    gog = v3(carve(65536, 8 * T, BF16), 8)
    hT = v3(carve(98304, 8 * T, BF16), 8)
    wbig = carve(131072, 8192, BF16)
    wb = [carve(131072, 4096, BF16), carve(139264, 4096, BF16)]
    sqb = v3(carve(147456, 8 * 512, BF16), 8)
    t1 = [carve(155648, 512, F32), carve(157696, 512, F32)]
    rstd = carve(159744, 512, F32)
    qn = v3(carve(161792, 3 * T, BF16), 3)
    kvn = v3(carve(174080, 2 * T, BF16), 2)
    lat = v3(carve(182272, 5 * 512, F32), 5)
    krope = carve(182272, T, BF16)
    kropesq = carve(182272 + 4096, T, BF16)
    cosb = carve(192512, 512, F32)
    sinb = carve(194560, 512, F32)
    stg = [carve(196608 + i * 1024, 512, BF16) for i in range(4)]
    vst = [carve(200704, 1024, BF16), carve(202752, 1024, BF16)]
    Kt = [carve(0, 8192, BF16), carve(16384, 8192, BF16)]
    Vt = [carve(32768, 8192, BF16).rearrange("p (k c) -> p k c", k=64),
          carve(49152, 8192, BF16).rearrange("p (k c) -> p k c", k=64)]
    qt = [carve(98304 + i * 4096, T, BF16) for i in range(4)]
    Pt = [carve(114688 + i * 2048, 1024, BF16) for i in range(3)]
    tmpf = [carve(120832 + i * 4096, 1024, F32) for i in range(2)] + [carve(154624, 1024, F32)]
    ttu = [carve(158720 + i * 1024, 512, U16) for i in range(6)]
    posbc = carve(133120, T, F32)
    rinv = [carve(141312 + i * 2048, 512, F32) for i in range(2)]
    osb = [carve(145408 + i * 2048, 512, F32) for i in range(2)]
    sqo = carve(149504, 512, BF16)
    rr = carve(150528, 512, F32)
    zer512 = carve(152576, 512, F32)
    dd = [carve(154624 + i * 2048, 512, F32) for i in range(2)]

    def tcs(tc):
        return slice(tc * 512, (tc + 1) * 512)

    cx.op("dve", lambda e: e.memset(ones_bf[:], 1.0), writes=["ones"])
    cx.op("dve", lambda e: e.memset(epst[:], EPS), writes=["eps"])
    cx.op("dve", lambda e: e.memset(zerot[:], 0.0), writes=["zero"])
    cx.dma("sp", ctmp[:], c_lay[:, :], writes=["ctmp"])
    cx.op("act", lambda e: e.activation(out=cact[:], in_=ctmp[:], func=AF.Silu), reads=["ctmp"], writes=["cact"])
    cx.dma("sp", adab[:].rearrange("p (l j) -> p l j", l=4), ada_b.ap().rearrange("l p j -> p l j"), writes=["adab"])
    cx.dma("sp", gnorm[:].rearrange("p (l j) -> p l j", l=4), norm_g.ap().rearrange("l p j -> p l j"), writes=["gnorm"])
    cx.dma("sp", fgl[:], final_g[:, :], writes=["fgl"])
    cx.dma("sp", gq[:].rearrange("p (l j) -> p l j", l=2), m_gq.ap().rearrange("l p j -> p l j"), writes=["gq"])
    cx.dma("sp", gkv[:].rearrange("p (l j) -> p l j", l=2), m_gkv.ap().rearrange("l p j -> p l j"), writes=["gkv"])
    cx.dma("sp", hgs[:], d_hg[:, :], writes=["hgs"])
    cx.dma("sp", ind2f[:], k_ind2[:, :], writes=["ind2f"])
    cx.dma("sp", indhf[:], k_indh[:, :], writes=["indhf"])
    cx.op("dve", lambda e: e.tensor_copy(out=ind2[:], in_=ind2f[:]), reads=["ind2f"], writes=["ind2"])
    cx.op("dve", lambda e: e.tensor_copy(out=indh[:], in_=indhf[:]), reads=["indhf"], writes=["indh"])
    cx.dma("sp", invf[:], k_invf[:, :], writes=["invf"])
    cx.dma("sp", sgn[:], k_sgn[:, :], writes=["sgn"])
    cx.dma("sp", pki[:], pos_all[:, :], writes=["pki"])
    cx.dma("sp", c0i[:], pos_c0[:, :], writes=["c0i"])
    cx.op("dve", lambda e: e.tensor_copy(out=c0f[:], in_=c0i[:]), reads=["c0i"], writes=["c0f"])
    cx.op("dve", lambda e: e.tensor_copy(out=pkrel[:], in_=pki[:]), reads=["pki"], writes=["pkrel"])
    cx.op("dve", lambda e: e.tensor_scalar(out=pkrel[:], in0=pkrel[:], scalar1=c0f[:, 0:1], scalar2=None,
                                           op0=ALU.subtract), reads=["pkrel", "c0f"], writes=["pkrel"])
    cx.op("dve", lambda e: e.tensor_scalar(out=negpk[:], in0=pkrel[:], scalar1=-1.0, scalar2=None, op0=ALU.mult),
          reads=["pkrel"], writes=["negpk"])
    for j in range(2):
        l = 2 * j + 1
        lam_init = 0.8 - 0.6 * math.exp(-0.3 * l)
        cx.dma("sp", lqt[:].rearrange("p (a d) -> p a d", a=4), d_lq[j].rearrange("a p d -> p a d"), writes=["lqt"])
        for a in range(2):
            cx.op("dve", lambda e: e.tensor_tensor(out=lqp[:], in0=lqt[:, (2 * a) * 64:(2 * a + 1) * 64],
                                                   in1=lqt[:, (2 * a + 1) * 64:(2 * a + 2) * 64], op=ALU.mult),
                  reads=["lqt"], writes=["lqp"])
            cx.op("dve", lambda e: e.tensor_reduce(out=lamt[:, a:a + 1], in_=lqp[:], axis=AX.X, op=ALU.add),
                  reads=["lqp"], writes=["lamt"])
        cx.op("act", lambda e: e.activation(out=lamt[:, 2:4], in_=lamt[:, 0:2], func=AF.Exp), reads=["lamt"], writes=["lamt"])
        cx.op("dve", lambda e: e.tensor_tensor(out=lamt[:, 4:5], in0=lamt[:, 3:4], in1=lamt[:, 2:3], op=ALU.subtract),
              reads=["lamt"], writes=["lamt"])
        cx.op("dve", lambda e: e.tensor_scalar(out=neglam[:, j:j + 1], in0=lamt[:, 4:5], scalar1=-lam_init, scalar2=None,
                                               op0=ALU.add), reads=["lamt"], writes=["neglam"])
        cx.op("dve", lambda e: e.tensor_scalar(out=hgs[:, j:j + 1], in0=hgs[:, j:j + 1], scalar1=(1.0 - lam_init),
                                               scalar2=None, op0=ALU.mult), reads=["hgs"], writes=["hgs"])

    pi = 0
    for l in layers:
        for p6 in range(6):
            w = v3(wb[pi % 2], 8)
            tok = f"wb{pi % 2}"
            cx.dma("pool", w, ada_w[l][:, p6 * 512:(p6 + 1) * 512].rearrange("(k p) c -> p k c", p=128), writes=[tok])
            for jj in range(4):
                j = p6 * 4 + jj
                for k in range(8):
                    cx.op("pe", lambda e: e.matmul(bank(0)[:, j:j + 1], w[:, k, jj * 128:(jj + 1) * 128], cact[:, k:k + 1],
                                                   start=(k == 0), stop=(k == 7)),
                          reads=[tok, "cact"], writes=[btok(0)], inc=(k == 7))
            pi += 1
        cx.op("dve", lambda e: e.tensor_tensor(out=mods[:, l * 24:(l + 1) * 24], in0=bank(0)[:, 0:24],
                                               in1=adab[:, l * 24:(l + 1) * 24], op=ALU.add),
              reads=[btok(0), "adab"], writes=["mods"])
        cx.op("dve", lambda e: e.scalar_tensor_tensor(out=Acoef[:, l * 8:(l + 1) * 8], in0=mods[:, l * 24 + 8:l * 24 + 16],
                                                     scalar=1.0, in1=gnorm[:, l * 8:(l + 1) * 8], op0=ALU.add, op1=ALU.mult),
              reads=["mods", "gnorm"], writes=["Acoef"])

    for k in range(8):
        cx.dma("sp", xT[:, k, :], x_in[k * 128:(k + 1) * 128, :], writes=[f"xT{k}"])

    TWO_PI = 2.0 * math.pi
    MAGIC = 12582912.0
    pint = carve(98304, 512, I32)
    pf = carve(98304 + 2048, 512, F32)
    ya = carve(98304 + 4096, 512, F32)
    yb = carve(98304 + 6144, 512, F32)
    yc = carve(98304 + 8192, 512, F32)
    for tc in range(4):
        cx.dma("sp", pint, pos_own[:, tcs(tc)], writes=["pint"])
        cx.op("dve", lambda e: e.tensor_copy(out=pf, in_=pint), reads=["pint"], writes=["pf"])
        cx.op("dve", lambda e: e.tensor_scalar(out=ya, in0=pf, scalar1=c0f[:, 0:1], scalar2=None, op0=ALU.subtract),
              reads=["pf", "c0f"], writes=["ya"])
        cx.dma("sp", posd[:, tcs(tc)], ya, reads=["ya"], writes=["posd"])
        for which in range(2):
            off = 0.0 if which == 0 else 0.25
            cx.op("dve", lambda e: e.tensor_scalar(out=ya, in0=pf, scalar1=invf[:, 0:1], scalar2=1.0 / TWO_PI,
                                                   op0=ALU.mult, op1=ALU.mult), reads=["pf", "invf"], writes=["ya"])
            if which == 1:
                cx.op("dve", lambda e: e.tensor_scalar(out=ya, in0=ya, scalar1=off, scalar2=None, op0=ALU.add),
                      reads=["ya"], writes=["ya"])
            cx.op("dve", lambda e: e.tensor_scalar(out=yb, in0=ya, scalar1=MAGIC, scalar2=None, op0=ALU.add),
                  reads=["ya"], writes=["yb"])
            cx.op("dve", lambda e: e.tensor_scalar(out=yb, in0=yb, scalar1=MAGIC, scalar2=None, op0=ALU.subtract),
                  reads=["yb"], writes=["yb"])
            cx.op("dve", lambda e: e.tensor_tensor(out=yc, in0=ya, in1=yb, op=ALU.subtract), reads=["ya", "yb"], writes=["yc"])
            cx.op("dve", lambda e: e.tensor_scalar(out=yc, in0=yc, scalar1=0.49999, scalar2=-0.49999, op0=ALU.min, op1=ALU.max),
                  reads=["yc"], writes=["yc"])
            cx.op("act", lambda e: e.activation(out=yb, in_=yc, func=AF.Sin, scale=TWO_PI), reads=["yc"], writes=["yb"])
            if which == 0:
                cx.op("dve", lambda e: e.tensor_scalar(out=yb, in0=yb, scalar1=sgn[:, 0:1], scalar2=None, op0=ALU.mult),
                      reads=["yb", "sgn"], writes=["yb"])
                cx.dma("sp", sind[:, tcs(tc)], yb, reads=["yb"], writes=["sind"])
            else:
                cx.dma("sp", cosd[:, tcs(tc)], yb, reads=["yb"], writes=["cosd"])
    cx.barrier()

    if any(l % 2 == 1 for l in layers):
        posb_t = carve(65536, T, F32)
        u16st = [carve(65536 + (2 + i) * 4096, 512, U16) for i in range(4)]
        cx.dma("sp", posb_t, posd[:, :], writes=["gog0", "gog1"])
        for kb in range(64):
            for tc in range(4):
                i = kb * 4 + tc
                cx.op("act", lambda e: e.activation(out=u16st[i % 4], in_=posb_t[:, tcs(tc)], func=AF.Abs,
                                                    bias=negpk[:, kb:kb + 1], scale=1.0),
                      reads=["gog0", "gog1", "negpk"], writes=[f"gog{2 + i % 4}"])
                cx.dma("sp", ttd[i], u16st[i % 4], reads=[f"gog{2 + i % 4}"], writes=["ttd"])

    def norm_stats(tc, nfeat, src_chunks, src_tokens, dst_rstd, bk):
        n = len(src_chunks)
        for i, (ap, tok) in enumerate(zip(src_chunks, src_tokens)):
            cx.op("pe", lambda e: e.matmul(bank(bk), ones_bf[:, :], ap, start=(i == 0), stop=(i == n - 1)),
                  reads=[tok, "ones"], writes=[btok(bk)], inc=(i == n - 1))
        cx.op("act", lambda e: e.activation(out=dst_rstd, in_=bank(bk), func=AF.Sqrt, bias=epst[:, 0:1], scale=1.0 / nfeat),
              reads=[btok(bk), "eps"], writes=["rstd"])
        cx.op("dve", lambda e: e.reciprocal(out=dst_rstd, in_=dst_rstd), reads=["rstd"], writes=["rstd"])

    def make_hT(l):
        for tc in range(4):
            cx.op("act", lambda e: e.activation(out=sqb[:, :, :], in_=xT[:, :, tcs(tc)], func=AF.Square),
                  reads=[f"xT{k}" for k in range(8)], writes=["sqb"])
            norm_stats(tc, 1024, [sqb[:, k, :] for k in range(8)], ["sqb"] * 8, rstd, 0)
            for k in range(8):
                tb = t1[k % 2]
                cx.op("dve", lambda e: e.tensor_tensor(out=tb, in0=xT[:, k, tcs(tc)], in1=rstd, op=ALU.mult),
                      reads=[f"xT{k}", "rstd"], writes=[f"t1{k % 2}"])
                cx.op("act", lambda e: e.activation(out=hT[:, k, tcs(tc)], in_=tb, func=AF.Identity,
                                                    bias=mods[:, l * 24 + k:l * 24 + k + 1],
                                                    scale=Acoef[:, l * 8 + k:l * 8 + k + 1]),
                      reads=[f"t1{k % 2}", "mods", "Acoef"], writes=[f"hT{tc}"])

    def spill_x():
        for k in range(8):
            cx.dma("sp", xs[k * 128:(k + 1) * 128, :], xT[:, k, :], reads=[f"xT{k}"], writes=["xs"])

    def load_w(dst, src_ap, tok):
        cx.dma("pool", dst, src_ap, writes=[tok])

    bkc = [0]

    def nextbank():
        bkc[0] = (bkc[0] + 1) % 6
        return bkc[0] + 2

    def proj(M, lhs_list, rhs_list, reads):
        bk = nextbank()
        n = len(lhs_list)
        for i in range(n):
            cx.op("pe", lambda e: e.matmul(bank(bk)[0:M, :], lhs_list[i], rhs_list[i], start=(i == 0), stop=(i == n - 1)),
                  reads=reads, writes=[btok(bk)], inc=(i == n - 1))
        return bk

    def p_phase_mla(l):
        j = l // 2
        make_hT(l)
        spill_x()
        scale = 96.0 ** -0.5
        ckpt(21)
        wl = wbig[:, 0:8 * 640].rearrange("p (k c) -> p k c", k=8)
        load_w(wl, m_w_in[j][:, 0:640].rearrange("(k p) c -> p k c", p=128), "wbig")
        for tc in range(4):
            for ci in range(5):
                bk = proj(128, [wl[:, k, ci * 128:(ci + 1) * 128] for k in range(8)], [hT[:, k, tcs(tc)] for k in range(8)],
                          ["wbig", f"hT{tc}"])
                cx.op("dve", lambda e: e.tensor_copy(out=lat[:, ci, :], in_=bank(bk)), reads=[btok(bk)], writes=[f"lat{ci}"])
                cx.op("act", lambda e: e.activation(out=sqb[:, ci, :], in_=bank(bk), func=AF.Square),
                      reads=[btok(bk)], writes=[f"sqb{ci}"])
            norm_stats(tc, 384, [sqb[:, ci, :] for ci in range(3)], [f"sqb{ci}" for ci in range(3)], rstd, 0)
            for ci in range(3):
                cx.op("dve", lambda e: e.scalar_tensor_tensor(out=qn[:, ci, tcs(tc)], in0=lat[:, ci, :],
                                                             scalar=gq[:, j * 3 + ci:j * 3 + ci + 1], in1=rstd,
                                                             op0=ALU.mult, op1=ALU.mult),
                      reads=[f"lat{ci}", "rstd", "gq"], writes=["qn"])
            norm_stats(tc, 256, [sqb[:, 3 + ci, :] for ci in range(2)], [f"sqb{3 + ci}" for ci in range(2)], rstd, 1)
            for ci in range(2):
                cx.op("dve", lambda e: e.scalar_tensor_tensor(out=kvn[:, ci, tcs(tc)], in0=lat[:, 3 + ci, :],
                                                             scalar=gkv[:, j * 2 + ci:j * 2 + ci + 1], in1=rstd,
                                                             op0=ALU.mult, op1=ALU.mult),
                      reads=[f"lat{3 + ci}", "rstd", "gkv"], writes=["kvn"])
        ckpt(22)
        wr = wbig[:, 0:8 * 64].rearrange("p (k c) -> p k c", k=8)
        cx.dma("pool", wr[:, :, 0:32], m_w_in[j][:, 640:672].rearrange("(k p) c -> p k c", p=128), writes=["wbig"])
        cx.dma("pool", wr[:, :, 32:64], m_w_rot[j].rearrange("(k p) c -> p k c", p=128), writes=["wbig2"],
               reads=["wbig"])
        for tc in range(4):
            cx.dma("sp", cosb, cosd[:, tcs(tc)], writes=["cosb"])
            cx.dma("sp", sinb, sind[:, tcs(tc)], writes=["sinb"])
            b1 = proj(32, [wr[:, k, 0:32] for k in range(8)], [hT[:, k, tcs(tc)] for k in range(8)], ["wbig", "wbig2", f"hT{tc}"])
            b2 = proj(32, [wr[:, k, 32:64] for k in range(8)], [hT[:, k, tcs(tc)] for k in range(8)], ["wbig", "wbig2", f"hT{tc}"])
            cx.op("dve", lambda e: e.tensor_tensor(out=t1[0][0:32, :], in0=bank(b1)[0:32, :], in1=cosb[0:32, :], op=ALU.mult),
                  reads=[btok(b1), "cosb"], writes=["t10"])
            cx.op("dve", lambda e: e.tensor_tensor(out=t1[1][0:32, :], in0=bank(b2)[0:32, :], in1=sinb[0:32, :], op=ALU.mult),
                  reads=[btok(b2), "sinb"], writes=["t11"])
            cx.op("pool", lambda e: e.tensor_tensor(out=krope[0:32, tcs(tc)], in0=t1[0][0:32, :], in1=t1[1][0:32, :], op=ALU.add),
                  reads=["t10", "t11"], writes=["krope"])
            cx.op("act", lambda e: e.activation(out=kropesq[0:32, tcs(tc)], in_=krope[0:32, tcs(tc)], func=AF.Square),
                  reads=["krope"], writes=["kropesq"])
        cx.dma("sp", smown[l][:, :], krope[0:32, :], reads=["krope"], writes=["smown"])
        si = 0
        cx.op("dve", lambda e: e.memset(qmx[:], 0.0), writes=["qmx"])
        cx.op("dve", lambda e: e.memset(kmx[:], 0.0), writes=["kmx"])
        def st_gate():
            nonlocal si
            for pc in range(2):
                load_w(v3(wb[pc], 8), m_w_in[j][:, 672 + pc * 512:672 + (pc + 1) * 512].rearrange("(k p) c -> p k c", p=128),
                       "wbig" if pc == 0 else "wbig2")
            for pc in range(2):
                w = v3(wb[pc], 8)
                for ci in range(4):
                    for tc in range(4):
                        bk = proj(128, [w[:, k, ci * 128:(ci + 1) * 128] for k in range(8)],
                                  [hT[:, k, tcs(tc)] for k in range(8)], ["wbig", "wbig2", f"hT{tc}"])
                        cx.op("act", lambda e: e.activation(out=gog[:, pc * 4 + ci, tcs(tc)], in_=bank(bk), func=AF.Silu),
                              reads=[btok(bk)], writes=[f"gog{pc * 4 + ci}"])

        def st_qup():
            nonlocal si
            wq = wbig[:, 0:3 * 1536].rearrange("p (k c) -> p k c", k=3)
            wqr = wbig[:, 3 * 1536:3 * 1536 + 3 * 512].rearrange("p (k c) -> p k c", k=3)
            cx.dma("pool", wq, m_wq[j].rearrange("(k p) c -> p k c", p=128), writes=["wbig"], reads=["wbig2"])
            cx.dma("pool", wqr, m_wqr[j].rearrange("(k p) c -> p k c", p=128), writes=["wbig2"], reads=["wbig"])
            cx.collective(stown[l].ap().opt(), stall[l].ap().opt(), reads=["stown"], writes=["stall"])
            cx.collective(smown[l].ap().opt(), small[l].ap().opt(), reads=["smown"], writes=["small"])
            for g in range(4):
                cx.collective(kown[l][g].ap().opt(), kall[l][g].ap().opt(), reads=[f"kown{g}"], writes=[f"kall{g}"])
                cx.collective(vown[l][g].ap().opt(), vall[l][g].ap().opt(), reads=[f"vown{g}"], writes=[f"vall{g}"])
            deferred = []
            stgx = [(stg[i], f"stg{i}") for i in range(4)] + [(vst[0][:, 0:512], "vst0"), (vst[0][:, 512:1024], "vst0"),
                                                               (vst[1][:, 0:512], "vst1"), (vst[1][:, 512:1024], "vst1")]
            for tc in range(4):
                cx.dma("sp", cosb, cosd[:, tcs(tc)], writes=["cosb"])
                cx.dma("sp", sinb, sind[:, tcs(tc)], writes=["sinb"])
                for h in range(16):
                    b1 = proj(96, [wq[:, k, h * 96:(h + 1) * 96] for k in range(3)], [qn[:, k, tcs(tc)] for k in range(3)],
                              ["wbig", "wbig2", "qn"])
                    b2 = proj(32, [wqr[:, k, h * 32:(h + 1) * 32] for k in range(3)], [qn[:, k, tcs(tc)] for k in range(3)],
                              ["wbig", "wbig2", "qn"])
                    while deferred:
                        deferred.pop(0)()
                    st, stok = stgx[si % 8]
                    si += 1
                    cx.op("act", lambda e: e.activation(out=st[0:64, :], in_=bank(b1)[0:64, :], func=AF.Copy, scale=scale),
                          reads=[btok(b1)], writes=[stok])
                    cx.op("dve", lambda e: e.scalar_tensor_tensor(out=t1[0][64:96, :], in0=bank(b1)[64:96, :], scalar=scale,
                                                                 in1=cosb[64:96, :], op0=ALU.mult, op1=ALU.mult),
                          reads=[btok(b1), "cosb"], writes=["t10"])
                    cx.op("dve", lambda e: e.scalar_tensor_tensor(out=t1[1][64:96, :], in0=bank(b2)[0:32, :], scalar=scale,
                                                                 in1=sinb[0:32, :], op0=ALU.mult, op1=ALU.mult),
                          reads=[btok(b2), "sinb"], writes=["t11"])
                    cx.op("dve", lambda e: e.tensor_tensor(out=st[64:96, :], in0=t1[0][64:96, :], in1=t1[1][64:96, :], op=ALU.add),
                          reads=["t10", "t11"], writes=[stok])
                    sq = sqb[:, h % 8, :]
                    cx.op("act", lambda e: e.activation(out=sq[0:96, :], in_=st[0:96, :], func=AF.Square),
                          reads=[stok], writes=[f"sqb{h % 8}"])

                    def ind_mm(h=h, sq=sq):
                        cx.op("pe", lambda e: e.matmul(bank(1)[0:16, :], indh[0:96, h * 16:(h + 1) * 16], sq[0:96, :],
                                                       start=(h == 0), stop=(h == 15)),
                              reads=[f"sqb{h % 8}", "indh"], writes=[btok(1)], inc=(h == 15))
                    deferred.append(ind_mm)
                    cx.dma("sp", qs[l][h][0:96, tcs(tc)], st[0:96, :], reads=[stok], writes=["qs"])
                while deferred:
                    deferred.pop(0)()
                cx.op("dve", lambda e: e.tensor_reduce(out=qmx[:, tc:tc + 1], in_=bank(1)[0:16, :], axis=AX.X, op=ALU.max),
                      reads=[btok(1)], writes=["qmx"])
            cx.op("dve", lambda e: e.tensor_copy(out=qmax[:], in_=qmx[:, 0:4]), reads=["qmx"], writes=["qmax"])

        def st_kv():
            nonlocal si
            wk = wbig[:, 0:2 * 1024].rearrange("p (k c) -> p k c", k=2)
            wv = wbig[:, 2048:2048 + 2 * 1024].rearrange("p (k c) -> p k c", k=2)
            cx.dma("pool", wk, m_wk[j].rearrange("(k p) c -> p k c", p=128), writes=["wbig"], reads=["wbig2"])
            cx.dma("pool", wv, m_wv[j].rearrange("(k p) c -> p k c", p=128), writes=["wbig2"], reads=["wbig"])
            kdef = []
            for tc in range(4):
                for hp in range(8):
                    bk = proj(128, [wk[:, k, hp * 128:(hp + 1) * 128] for k in range(2)], [kvn[:, k, tcs(tc)] for k in range(2)],
                              ["wbig", "wbig2", "kvn"])
                    while kdef:
                        kdef.pop(0)()
                    st = stg[si % 4]
                    stok = f"stg{si % 4}"
                    si += 1
                    cx.op("act", lambda e: e.activation(out=st, in_=bank(bk), func=AF.Copy), reads=[btok(bk)], writes=[stok])
                    sq = sqb[:, hp, :]
                    cx.op("act", lambda e: e.activation(out=sq, in_=bank(bk), func=AF.Square), reads=[btok(bk)],
                          writes=[f"sqb{hp}"])
                    def ind_mm(hp=hp, sq=sq):
                        cx.op("pe", lambda e: e.matmul(bank(1)[0:16, :], ind2[:, hp * 16:(hp + 1) * 16], sq, start=(hp == 0), stop=False),
                              reads=[f"sqb{hp}", "ind2"], writes=[btok(1)], inc=False)
                    kdef.append(ind_mm)
                    g = hp // 2
                    cx.dma("sp", kown[l][g][(hp % 2) * 128:(hp % 2) * 128 + 128, tcs(tc)], st, reads=[stok], writes=[f"kown{g}"])
                while kdef:
                    kdef.pop(0)()
                cx.op("pe", lambda e: e.matmul(bank(1)[0:16, :], ones_bf[0:32, 0:16], kropesq[0:32, tcs(tc)], start=False, stop=True),
                      reads=["kropesq", "ones"], writes=[btok(1)], inc=True)
                cx.op("dve", lambda e: e.tensor_reduce(out=kmx[:, tc:tc + 1], in_=bank(1)[0:16, :], axis=AX.X, op=ALU.max),
                      reads=[btok(1)], writes=["kmx"])
            cx.dma("sp", stown[l][:, :], kmx[:], reads=["kmx"], writes=["stown"])
            for tb in range(16):
                vs = vst[tb % 2]
                vtok = f"vst{tb % 2}"
                for half in range(2):
                    bk = proj(128, [kvn[:, k, tb * 128:(tb + 1) * 128] for k in range(2)],
                              [wv[:, k, half * 512:(half + 1) * 512] for k in range(2)], ["wbig", "wbig2", "kvn"])
                    cx.op("act" if half == 0 else "dve",
                          (lambda e: e.activation(out=vs[:, half * 512:(half + 1) * 512], in_=bank(bk), func=AF.Copy)) if half == 0
                          else (lambda e: e.tensor_copy(out=vs[:, half * 512:(half + 1) * 512], in_=bank(bk))),
                          reads=[btok(bk)], writes=[vtok])
                for g in range(4):
                    cx.dma("sp", vown[l][g][tb * 128:(tb + 1) * 128, :], vs[:, g * 256:(g + 1) * 256], reads=[vtok],
                           writes=[f"vown{g}"])

        st_kv()
        st_gate()
        st_qup()


    def p_phase_diff(l):
        j = l // 2
        make_hT(l)
        spill_x()
        cx.op("dve", lambda e: e.memset(qmx[:], 0.0), writes=["qmx"])
        cx.op("dve", lambda e: e.memset(kmx[:], 0.0), writes=["kmx"])
        for i in range(4):
            cx.op("pool", lambda e: e.memset(stg[i], 0.0), writes=[f"stg{i}"])
        si = 0
        order = [2, 3, 4, 5, 0, 1, 6, 7]

        def issue_load(pidx):
            pcc = order[pidx]
            load_w(v3(wb[pidx % 2], 8), d_w_in[j][:, pcc * 512:(pcc + 1) * 512].rearrange("(k p) c -> p k c", p=128),
                   f"wb{pidx % 2}")
            if pidx == 7:
                cx.collective(stown[l].ap().opt(), stall[l].ap().opt(), reads=["stown"], writes=["stall"])
                for g in range(4):
                    cx.collective(kown[l][g].ap().opt(), kall[l][g].ap().opt(), reads=[f"kown{g}"], writes=[f"kall{g}"])
                    cx.collective(vown[l][g].ap().opt(), vall[l][g].ap().opt(), reads=[f"vown{g}"], writes=[f"vall{g}"])

        issue_load(0)
        for pidx, pc in enumerate(order):
            w = v3(wb[pidx % 2], 8)
            wtok = f"wb{pidx % 2}"
            if pidx + 1 < 8:
                issue_load(pidx + 1)
            kind = pc // 2
            ddef = []
            if kind in (0, 1):
                mx = qmx if kind == 0 else kmx
                mtok = "qmx" if kind == 0 else "kmx"
                for tc in range(4):
                    for hh in range(4):
                        h = (pc % 2) * 4 + hh
                        bk = proj(128, [w[:, k, hh * 128:(hh + 1) * 128] for k in range(8)],
                                  [hT[:, k, tcs(tc)] for k in range(8)], [wtok, f"hT{tc}"])
                        while ddef:
                            ddef.pop(0)()
                        sq = sqb[:, hh, :]
                        if kind == 0:
                            s0, s1 = stg[(si % 2) * 2], stg[(si % 2) * 2 + 1]
                            t0, t1k = f"stg{(si % 2) * 2}", f"stg{(si % 2) * 2 + 1}"
                            si += 1
                            cx.op("act", lambda e: e.activation(out=s0[0:64, :], in_=bank(bk)[0:64, :], func=AF.Copy, scale=0.125),
                                  reads=[btok(bk)], writes=[t0])
                            cx.op("act", lambda e: e.activation(out=s1[64:128, :], in_=bank(bk)[64:128, :], func=AF.Copy, scale=0.125),
                                  reads=[btok(bk)], writes=[t1k])
                            cx.op("act", lambda e: e.activation(out=sq, in_=bank(bk), func=AF.Square, scale=0.125),
                                  reads=[btok(bk)], writes=[f"sqb{hh}"])
                            cx.dma("sp", qs[l][2 * h][:, tcs(tc)], s0, reads=[t0], writes=["qs"])
                            cx.dma("sp", qs[l][2 * h + 1][:, tcs(tc)], s1, reads=[t1k], writes=["qs"])
                        else:
                            st = vst[si % 2][:, 0:512]
                            stok = f"vst{si % 2}"
                            si += 1
                            cx.op("act", lambda e: e.activation(out=st, in_=bank(bk), func=AF.Copy), reads=[btok(bk)], writes=[stok])
                            cx.op("act", lambda e: e.activation(out=sq, in_=bank(bk), func=AF.Square),
                                  reads=[btok(bk)], writes=[f"sqb{hh}"])
                            g = h // 2
                            cx.dma("sp", kown[l][g][(h % 2) * 128:(h % 2) * 128 + 128, tcs(tc)], st, reads=[stok],
                                   writes=[f"kown{g}"])
                        def ind_mm(h=h, hh=hh, sq=sq):
                            cx.op("pe", lambda e: e.matmul(bank(1)[0:16, :], ind2[:, h * 16:(h + 1) * 16], sq,
                                                           start=(hh == 0), stop=(hh == 3)),
                                  reads=[f"sqb{hh}", "ind2"], writes=[btok(1)], inc=(hh == 3))
                        ddef.append(ind_mm)
                    while ddef:
                        ddef.pop(0)()
                    col = (pc % 2) * 4 + tc
                    cx.op("dve", lambda e: e.tensor_reduce(out=mx[:, col:col + 1], in_=bank(1)[0:16, :], axis=AX.X, op=ALU.max),
                          reads=[btok(1)], writes=[mtok])
                if pc == 1:
                    cx.op("dve", lambda e: e.tensor_tensor(out=qmax[:], in0=qmx[:, 0:4], in1=qmx[:, 4:8], op=ALU.max),
                          reads=["qmx"], writes=["qmax"])
                if pc == 3:
                    cx.dma("sp", stown[l][:, :], kmx[:], reads=["kmx"], writes=["stown"])
            elif kind == 2:
                for tb in range(16):
                    bk = proj(128, [hT[:, k, tb * 128:(tb + 1) * 128] for k in range(8)], [w[:, k, :] for k in range(8)],
                              [wtok, f"hT{tb // 4}"])
                    vs = vst[tb % 2][:, 0:512]
                    vtok = f"vst{tb % 2}"
                    cx.op("act" if tb % 2 == 0 else "dve",
                          (lambda e: e.activation(out=vs, in_=bank(bk), func=AF.Copy)) if tb % 2 == 0
                          else (lambda e: e.tensor_copy(out=vs, in_=bank(bk))),
                          reads=[btok(bk)], writes=[vtok])
                    for gg in range(2):
                        g = (pc % 2) * 2 + gg
                        cx.dma("sp", vown[l][g][tb * 128:(tb + 1) * 128, :], vs[:, gg * 256:(gg + 1) * 256], reads=[vtok],
                               writes=[f"vown{g}"])
            else:
                for ci in range(4):
                    for tc in range(4):
                        bk = proj(128, [w[:, k, ci * 128:(ci + 1) * 128] for k in range(8)],
                                  [hT[:, k, tcs(tc)] for k in range(8)], [wtok, f"hT{tc}"])
                        c8 = (pc % 2) * 4 + ci
                        cx.op("act", lambda e: e.activation(out=gog[:, c8, tcs(tc)], in_=bank(bk), func=AF.Silu),
                              reads=[btok(bk)], writes=[f"gog{c8}"])

    def a_prologue(l):
        cx.dma("sp", stl[:].rearrange("m (r c) -> m r c", r=4), stall[l].ap().rearrange("(r m) c -> m r c", r=4),
               reads=["stall"], writes=["stl"])
        cx.op("dve", lambda e: e.tensor_reduce(out=kmax[:], in_=stl[:], axis=AX.X, op=ALU.max), reads=["stl"], writes=["kmax"])
        cx.op("dve", lambda e: e.tensor_scalar(out=nb16[:], in0=qmax[:], scalar1=kmax[:, 0:1], scalar2=None, op0=ALU.mult),
              reads=["qmax", "kmax"], writes=["nb16"])
        cx.op("act", lambda e: e.activation(out=nb16[:], in_=nb16[:], func=AF.Sqrt), reads=["nb16"], writes=["nb16"])
        cx.op("dve", lambda e: e.tensor_scalar(out=nb16[:], in0=nb16[:], scalar1=-1.02, scalar2=None, op0=ALU.mult),
              reads=["nb16"], writes=["nb16"])
        cx.dma("sp", nbd[l].rearrange("(m c) -> m c", m=16), nb16[:], reads=["nb16"], writes=["nbd"])
        cx.dma("sp", negb[:], nbd[l:l + 1, :].partition_broadcast(128), reads=["nbd"], writes=["negb"])

    def a_phase_mla(l):
        a_prologue(l)
        for b in range(2):
            cx.op("dve", lambda e: e.memset(Vt[b][:, :, 64:128], 1.0), writes=[f"Vt{b}"])
            for r in range(4):
                cx.dma("sp", Kt[b][64:96, r * T:(r + 1) * T], small[l][r * 32:(r + 1) * 32, :], reads=["small"],
                       writes=[f"Kr{b}"])

        def loads(h):
            b = h % 2
            g, hh = h // 4, h % 4
            cx.dma("sp", qt[b][0:96, :], qs[l][h][0:96, :], reads=["qs"], writes=[f"qt{b}"])
            for r in range(4):
                cx.dma("sp", Kt[b][0:64, r * T:(r + 1) * T], kall[l][g][r * 256 + hh * 64:r * 256 + hh * 64 + 64, :],
                       reads=[f"kall{g}"], writes=[f"Kt{b}"])
            for r in range(4):
                cx.dma("sp", Vt[b][:, r * 16:(r + 1) * 16, 0:64],
                       vall[l][g][r * T:(r + 1) * T, hh * 64:(hh + 1) * 64].rearrange("(k p) c -> p k c", p=128),
                       reads=[f"vall{g}"], writes=[f"Vt{b}"])

        loads(0)
        it = 0
        for h in range(16):
            if h + 1 < 16:
                loads(h + 1)
            b = h % 2
            for tc in range(4):
                ab = 4 + (it % 2)
                it += 1
                NU = 32
                bias_ap = negb[:, h * 4 + tc:h * 4 + tc + 1]

                SB = [PS[0], PS[1], PS[3]]

                def qk(u):
                    for kk in range(2):
                        kb = 2 * u + kk
                        cx.op("pe", lambda e: e.matmul(SB[u % 3][:, kk * 512:(kk + 1) * 512], Kt[b][0:96, kb * 128:(kb + 1) * 128],
                                                       qt[b][0:96, tcs(tc)], start=True, stop=True),
                              reads=[f"Kt{b}", f"Kr{b}", f"qt{b}"], writes=[f"S{u % 3}"], inc=(kk == 1))

                def ex(u):
                    cx.op("act", lambda e: e.activation(out=Pt[u % 3], in_=SB[u % 3][:, :], func=AF.Exp, bias=bias_ap, scale=1.0),
                          reads=[f"S{u % 3}", "negb"], writes=[f"P{u % 3}"])

                def pv(u):
                    for kk in range(2):
                        kb = 2 * u + kk
                        cx.op("pe", lambda e: e.matmul(bank(ab), Vt[b][:, kb, :], Pt[u % 3][:, kk * 512:(kk + 1) * 512],
                                                       start=(kb == 0), stop=(kb == 63)),
                              reads=[f"Vt{b}", f"P{u % 3}"], writes=[btok(ab)], inc=(kk == 1))

                qk(0)
                qk(1)
                for u in range(NU):
                    if u + 2 < NU:
                        qk(u + 2)
                    ex(u)
                    pv(u)
                ri = rinv[it % 2]
                ob = osb[it % 2]
                pb = (h % 2) * 64
                cx.op("dve", lambda e: e.reciprocal(out=ri[0:64, :], in_=bank(ab)[64:128, :]), reads=[btok(ab)],
                      writes=[f"rinv{it % 2}"])
                cx.op("dve", lambda e: e.tensor_tensor(out=ob[pb:pb + 64, :], in0=bank(ab)[0:64, :], in1=ri[0:64, :], op=ALU.mult),
                      reads=[btok(ab), f"rinv{it % 2}"], writes=[f"osb{it % 2}"])
                cx.op("dve", lambda e: e.tensor_tensor(out=gog[pb:pb + 64, h // 2, tcs(tc)], in0=ob[pb:pb + 64, :],
                                                        in1=gog[pb:pb + 64, h // 2, tcs(tc)], op=ALU.mult),
                      reads=[f"osb{it % 2}", f"gog{h // 2}"], writes=[f"gog{h // 2}"])

    def a_phase_diff(l):
        j = l // 2
        a_prologue(l)
        nv = negb[:].rearrange("p (h c t) -> p h c t", h=8, c=2)
        cx.op("dve", lambda e: e.tensor_tensor(out=negbh[:].rearrange("p (h t) -> p h t", h=8), in0=nv[:, :, 0, :],
                                               in1=nv[:, :, 1, :], op=ALU.min), reads=["negb"], writes=["negbh"])

        def loads(h):
            b = h % 2
            g = h // 2
            for c in range(2):
                cx.dma("sp", qt[2 * b + c], qs[l][2 * h + c][:, :], reads=["qs"], writes=[f"qt{2 * b + c}"])
            for r in range(4):
                cx.dma("sp", Kt[b][:, r * T:(r + 1) * T], kall[l][g][r * 256 + (h % 2) * 128:r * 256 + (h % 2) * 128 + 128, :],
                       reads=[f"kall{g}"], writes=[f"Kt{b}"])
            for r in range(4):
                cx.dma("sp", Vt[b][:, r * 16:(r + 1) * 16, :],
                       vall[l][g][r * T:(r + 1) * T, (h % 2) * 128:(h % 2) * 128 + 128].rearrange("(k p) c -> p k c", p=128),
                       reads=[f"vall{g}"], writes=[f"Vt{b}"])

        loads(0)
        it = 0
        for h in range(8):
            if h + 1 < 8:
                loads(h + 1)
            b = h % 2
            slope = 2.0 ** (-(h + 1))
            for tc in range(4):
                it += 1
                NU = 64

                def qk(u):
                    for c in range(2):
                        cx.op("pe", lambda e: e.matmul(bank((u % 2) * 2 + c), Kt[b][c * 64:(c + 1) * 64, u * 128:(u + 1) * 128],
                                                       qt[2 * b + c][c * 64:(c + 1) * 64, tcs(tc)], start=True, stop=True),
                              reads=[f"Kt{b}", f"qt{2 * b + c}"], writes=[f"S{u % 2}"], inc=(c == 1))

                def bias(u):
                    cx.dma("sp", ttu[u % 6], ttd[u * 4 + tc], writes=[f"tt{u % 6}"])

                def stt(u):
                    tm = tmpf[u % 3]
                    cx.op("dve", lambda e: e.scalar_tensor_tensor(
                        out=tm.rearrange("p (c n) -> p c n", c=2), in0=ttu[u % 6].unsqueeze(1).to_broadcast([128, 2, 512]),
                        scalar=-slope, in1=PS[u % 2][:, :].rearrange("p (c n) -> p c n", c=2), op0=ALU.mult, op1=ALU.add),
                          reads=[f"tt{u % 6}", f"S{u % 2}"], writes=[f"tm{u % 3}"])

                def exa(u):
                    cx.op("act", lambda e: e.activation(out=Pt[u % 3], in_=tmpf[u % 3], func=AF.Exp,
                                                        bias=negbh[:, h * 4 + tc:h * 4 + tc + 1], scale=1.0),
                          reads=[f"tm{u % 3}", "negbh"], writes=[f"P{u % 3}"])

                def pv(u):
                    for c in range(2):
                        cx.op("pe", lambda e: e.matmul(bank(4 + c), Vt[b][:, u, :], Pt[u % 3][:, c * 512:(c + 1) * 512],
                                                       start=(u == 0), stop=(u == 63)),
                              reads=[f"Vt{b}", f"P{u % 3}"], writes=[btok(4 + c)], inc=False)
                    for c in range(2):
                        cx.op("pe", lambda e: e.matmul(bank(6 + c), ones_bf[:, :], Pt[u % 3][:, c * 512:(c + 1) * 512],
                                                       start=(u == 0), stop=(u == 63)),
                              reads=["ones", f"P{u % 3}"], writes=[btok(6 + c)], inc=(c == 1))

                qk(0)
                qk(1)
                for i5 in range(5):
                    bias(i5)
                for u in range(NU):
                    stt(u)
                    if u + 2 < NU:
                        qk(u + 2)
                    if u + 5 < NU:
                        bias(u + 5)
                    exa(u)
                    pv(u)
                for c in range(2):
                    cx.op("dve", lambda e: e.reciprocal(out=rinv[c], in_=bank(6 + c)), reads=[btok(6 + c)], writes=[f"rinv{c}"])
                    cx.op("dve", lambda e: e.tensor_tensor(out=osb[c], in0=bank(4 + c), in1=rinv[c], op=ALU.mult),
                          reads=[btok(4 + c), f"rinv{c}"], writes=[f"osb{c}"])
                cx.op("dve", lambda e: e.scalar_tensor_tensor(out=osb[0], in0=osb[1], scalar=neglam[:, j:j + 1], in1=osb[0],
                                                             op0=ALU.mult, op1=ALU.add),
                      reads=["osb0", "osb1", "neglam"], writes=["osb0"])
                cx.op("act", lambda e: e.activation(out=sqo, in_=osb[0], func=AF.Square), reads=["osb0"], writes=["sqo"])
                cx.op("pe", lambda e: e.matmul(bank(6), ones_bf[:, :], sqo, start=True, stop=True), reads=["sqo", "ones"],
                      writes=[btok(6)])
                cx.op("act", lambda e: e.activation(out=rr, in_=bank(6), func=AF.Sqrt, bias=epst[:, 0:1], scale=1.0 / 128),
                      reads=[btok(6), "eps"], writes=["rr"])
                cx.op("dve", lambda e: e.reciprocal(out=rr, in_=rr), reads=["rr"], writes=["rr"])
                cx.op("dve", lambda e: e.scalar_tensor_tensor(out=osb[1], in0=osb[0], scalar=hgs[:, j:j + 1], in1=rr,
                                                             op0=ALU.mult, op1=ALU.mult),
                      reads=["osb0", "rr", "hgs"], writes=["osb1"])
                cx.op("dve", lambda e: e.tensor_tensor(out=gog[:, h, tcs(tc)], in0=osb[1], in1=gog[:, h, tcs(tc)], op=ALU.mult),
                      reads=["osb1", f"gog{h}"], writes=[f"gog{h}"])

    def out_proj(l, w_o):
        j = l // 2
        for k in range(8):
            cx.dma("sp", xT[:, k, :], xs[k * 128:(k + 1) * 128, :], reads=["xs"], writes=[f"xT{k}"])
        for pc in range(2):
            load_w(v3(wb[pc], 8), w_o[j][:, pc * 512:(pc + 1) * 512].rearrange("(k p) c -> p k c", p=128), f"wb{pc}")
        for pc in range(2):
            w = v3(wb[pc], 8)
            wtok = f"wb{pc}"
            for jj in range(4):
                jc = pc * 4 + jj
                for tc in range(4):
                    bk = proj(128, [w[:, k, jj * 128:(jj + 1) * 128] for k in range(8)], [gog[:, k, tcs(tc)] for k in range(8)],
                              [wtok] + [f"gog{k}" for k in range(8)])
                    cx.op("dve", lambda e: e.scalar_tensor_tensor(out=xT[:, jc, tcs(tc)], in0=bank(bk),
                                                                 scalar=mods[:, l * 24 + 16 + jc:l * 24 + 17 + jc],
                                                                 in1=xT[:, jc, tcs(tc)], op0=ALU.mult, op1=ALU.add),
                          reads=[btok(bk), "mods", f"xT{jc}"], writes=[f"xT{jc}"])

    def final_norm():
        for tc in range(4):
            cx.op("act", lambda e: e.activation(out=sqb[:, :, :], in_=xT[:, :, tcs(tc)], func=AF.Square),
                  reads=[f"xT{k}" for k in range(8)], writes=["sqb"])
            norm_stats(tc, 1024, [sqb[:, k, :] for k in range(8)], ["sqb"] * 8, rstd, 0)
            for k in range(8):
                tb = t1[k % 2]
                cx.op("dve", lambda e: e.scalar_tensor_tensor(out=tb, in0=xT[:, k, tcs(tc)], scalar=fgl[:, k:k + 1], in1=rstd,
                                                             op0=ALU.mult, op1=ALU.mult),
                      reads=[f"xT{k}", "rstd", "fgl"], writes=[f"t1{k % 2}"])
                cx.dma("sp", y_out[k * 128:(k + 1) * 128, tcs(tc)], tb, reads=[f"t1{k % 2}"], writes=["yout"])

    def main_body():
        ckpt(1)
        for li, l in enumerate(layers):
            if l % 2 == 0:
                p_phase_mla(l)
                cx.barrier()
                ckpt(2)
                a_prologue_only = int(os.environ.get("KSTOP", "99")) == 3
                if a_prologue_only:
                    a_prologue(l)
                    ckpt(3)
                a_phase_mla(l)
                cx.barrier()
                ckpt(4)
                out_proj(l, m_wo)
            else:
                p_phase_diff(l)
                cx.barrier()
                ckpt(2)
                a_phase_diff(l)
                cx.barrier()
                ckpt(4)
                out_proj(l, d_wo)
            if li + 1 < len(layers):
                cx.new_epoch()
        final_norm()

    try:
        main_body()
    except StopBuild:
        pass
    cx.barrier()
    blk.__exit__(None, None, None)
    return nc, cx


def host_inputs(inp):
    f32 = np.float32
    x = np.asarray(inp["x"], f32)
    c = np.asarray(inp["c"], f32)
    pos = np.asarray(inp["positions"], np.int32)

    def lay(v, n):
        return np.ascontiguousarray(np.asarray(v, f32).reshape(n, 128).T)

    shared = {}
    shared["ada_w"] = np.ascontiguousarray(np.asarray(inp["ada_w"], f32))
    shared["ada_b_lay"] = np.stack([lay(inp["ada_b"][l], 24) for l in range(4)])
    shared["norm_g_lay"] = np.stack([lay(inp["norm_g"][l], 8) for l in range(4)])
    shared["final_g_lay"] = lay(inp["final_g"], 8)
    w_in = np.asarray(inp["mla_w_in"], f32)
    shared["m_w_in"] = np.ascontiguousarray(w_in)
    rot_cols = list(range(656, 672)) + list(range(640, 656))
    shared["m_w_rot"] = np.ascontiguousarray(w_in[:, :, rot_cols])
    shared["m_gq"] = np.stack([lay(inp["mla_q_norm_g"][j], 3) for j in range(2)])
    shared["m_gkv"] = np.stack([lay(inp["mla_kv_norm_g"][j], 2) for j in range(2)])
    wq = np.asarray(inp["mla_w_q_up"], f32).reshape(2, 384, 16, 96)
    shared["m_wq"] = np.ascontiguousarray(wq.reshape(2, 384, 16 * 96))
    wq_rot = np.concatenate([wq[..., 80:96], wq[..., 64:80]], axis=-1)
    shared["m_wqr"] = np.ascontiguousarray(wq_rot.reshape(2, 384, 16 * 32))
    wkv = np.asarray(inp["mla_w_kv_up"], f32).reshape(2, 256, 16, 128)
    shared["m_wk"] = np.ascontiguousarray(wkv[..., 0:64].reshape(2, 256, 1024))
    shared["m_wv"] = np.ascontiguousarray(wkv[..., 64:128].reshape(2, 256, 1024))
    shared["m_wo"] = np.ascontiguousarray(np.asarray(inp["mla_w_o"], f32))
    shared["d_w_in"] = np.ascontiguousarray(np.asarray(inp["diff_w_in"], f32))
    shared["d_wo"] = np.ascontiguousarray(np.asarray(inp["diff_w_o"], f32))
    shared["d_hg"] = np.ascontiguousarray(np.asarray(inp["diff_head_g"], f32).reshape(2, 128).T)
    lq = np.stack([np.stack([np.broadcast_to(np.asarray(inp[n], f32)[j][None, :], (128, 64))
                             for n in ("diff_lq1", "diff_lk1", "diff_lq2", "diff_lk2")]) for j in range(2)])
    shared["d_lq"] = np.ascontiguousarray(lq)
    ind2 = np.zeros((128, 8, 16), f32)
    for hp in range(8):
        ind2[0:64, hp, 2 * hp] = 1.0
        ind2[64:128, hp, 2 * hp + 1] = 1.0
    shared["k_ind2"] = ind2.reshape(128, 128)
    indh = np.zeros((128, 16, 16), f32)
    for h in range(16):
        indh[:, h, h] = 1.0
    shared["k_indh"] = indh.reshape(128, 256)
    inv = (10000.0 ** (-np.arange(0, 32, 2, dtype=np.float32) / 32)).astype(f32)
    p = np.arange(128)
    shared["k_invf"] = inv[p % 16].reshape(128, 1).astype(f32)
    shared["k_sgn"] = np.where((p % 32) < 16, -1.0, 1.0).reshape(128, 1).astype(f32)

    maps = []
    for core in range(NCORE):
        b, r = core // 4, core % 4
        m = dict(shared)
        m["xT_in"] = np.ascontiguousarray(x[b, r * T:(r + 1) * T, :].T)
        m["c_lay"] = lay(c[b], 8)
        m["pos_own"] = np.ascontiguousarray(np.broadcast_to(pos[b, r * T:(r + 1) * T][None, :], (128, T))).astype(np.int32)
        m["pos_all"] = np.ascontiguousarray(pos[b].reshape(64, 128).T).astype(np.int32)
        m["pos_c0"] = np.full((128, 1), pos[b, 0], np.int32)
        maps.append(m)
    return maps


_CACHE = {}


def kernel(**inputs):
    key = tuple(LAYERS)
    if key not in _CACHE:
        _CACHE[key] = build_program(LAYERS)
    nc, cx = _CACHE[key]
    maps = host_inputs(inputs)
    res = run_bass_kernel_spmd(nc, maps, core_ids=list(range(NCORE)))
    out = np.empty((2, S, D), np.float32)
    for core in range(NCORE):
        b, r = core // 4, core % 4
        out[b, r * T:(r + 1) * T, :] = res.results[core]["yT_out"].T
    return out
```

```python
import math
import numpy as np
import concourse.bass as bass
import concourse.mybir as mybir
from concourse.bass_utils import run_bass_kernel_spmd

F32 = mybir.dt.float32
BF16 = mybir.dt.bfloat16
I32 = mybir.dt.int32
AF = mybir.ActivationFunctionType
ALU = mybir.AluOpType
AX = mybir.AxisListType

D = 1024
S = 8192
T = 2048
NCORE = 8
EPS = 1e-6
MLA_IN = 1696
LAYERS = [0, 1, 2, 3]
GROUPS = [[0, 1, 2, 3], [4, 5, 6, 7]]


import os


class StopBuild(Exception):
    pass


def ckpt(n):
    if int(os.environ.get("KSTOP", "99")) == n:
        raise StopBuild()


class Ctx:
    def __init__(self, nc):
        self.nc = nc
        self.E = {"pe": nc.tensor, "act": nc.scalar, "dve": nc.vector, "pool": nc.gpsimd, "sp": nc.sync}
        self.sems = {}
        self.cnt = {}
        self.cur = {}
        self.epoch = 0
        for e in ("pe", "act", "dve", "pool"):
            self._new_csem(e)
        self.NR = 8
        self.ring = {}
        for q in ("sp", "pool", "act"):
            self.ring[q] = 0
            for i in range(self.NR):
                k = f"d_{q}_{i}"
                self.sems[k] = nc.alloc_semaphore(k)
                self.cnt[k] = 0
        self.ccn = 0
        for i in range(12):
            k = f"cc_{i}"
            self.sems[k] = nc.alloc_semaphore(k)
            self.cnt[k] = 0
        self.seen = {e: {} for e in self.E}
        self.lastw = {}
        self.readers = {}
        self.ninst = 0

    def _new_csem(self, e):
        k = f"c_{e}_{self.epoch}"
        self.sems[k] = self.nc.alloc_semaphore(k)
        self.cnt[k] = 0
        self.cur[e] = k

    def new_epoch(self):
        self.barrier()
        self.epoch += 1
        for e in ("pe", "act", "dve", "pool"):
            self._new_csem(e)

    def _need(self, eng, ev):
        if ev is None:
            return
        key, val = ev
        if val <= 0 or self.seen[eng].get(key, 0) >= val:
            return
        self.E[eng].wait_ge(self.sems[key], val)
        self.seen[eng][key] = val
        self.ninst += 1

    def _deps(self, eng, reads, writes):
        own = self.cur.get(eng)
        for t in reads:
            ev = self.lastw.get(t)
            if ev is not None and not (eng == "pe" and ev[0] == own):
                self._need(eng, ev)
        for t in writes:
            ev = self.lastw.get(t)
            if ev is not None and not (eng == "pe" and ev[0] == own):
                self._need(eng, ev)
            for k, v in self.readers.get(t, {}).items():
                if not (eng == "pe" and k == own):
                    self._need(eng, (k, v))

    def _record(self, ev, reads, writes):
        for t in writes:
            self.lastw[t] = ev
            self.readers[t] = {}
        for t in reads:
            d = self.readers.setdefault(t, {})
            if d.get(ev[0], 0) < ev[1]:
                d[ev[0]] = ev[1]

    def op(self, eng, fn, reads=(), writes=(), inc=True):
        excl = [t for t in reads if t.startswith("bank") or t in ("S0", "S1", "S2")]
        if excl:
            writes = list(writes) + excl
        self._deps(eng, reads, writes)
        ins = fn(self.E[eng])
        self.ninst += 1
        key = self.cur[eng]
        if inc:
            self.cnt[key] += 1
            ins.then_inc(self.sems[key], 1)
            ev = (key, self.cnt[key])
        else:
            ev = (key, self.cnt[key] + 1)
        self._record(ev, reads, writes)
        return ev

    def dma(self, q, out, in_, reads=(), writes=()):
        self._deps(q, reads, writes)
        i = self.ring[q]
        self.ring[q] = (i + 1) % self.NR
        key = f"d_{q}_{i}"
        self._need(q, (key, self.cnt[key]))
        ins = self.E[q].dma_start(out=out, in_=in_)
        self.ninst += 1
        self.cnt[key] += 16
        ins.then_inc(self.sems[key], 16)
        ev = (key, self.cnt[key])
        self._record(ev, reads, writes)
        return ev

    def collective(self, in_ap, out_ap, reads=(), writes=()):
        q = "pool"
        if os.environ.get("KNOCC"):
            return None
        self._deps(q, reads, writes)
        key = f"cc_{self.ccn % 12}"
        self.ccn += 1
        self._need(q, (key, self.cnt[key]))
        ins = self.E[q].collective_compute("AllGather", ALU.bypass, replica_groups=GROUPS,
                                          ins=[in_ap], outs=[out_ap])
        self.ninst += 1
        self.cnt[key] += 1
        ins.then_inc(self.sems[key], 1)
        ev = (key, self.cnt[key])
        self._record(ev, reads, writes)
        return ev

    def barrier(self):
        for eng in self.E:
            for key, c in self.cnt.items():
                if c > 0 and not key.startswith("cc_"):
                    self._need(eng, (key, c))
        keep = {t: ev for t, ev in self.lastw.items() if ev[0].startswith("cc_")}
        self.lastw.clear()
        self.readers.clear()
        self.lastw.update(keep)


def build_program(layers):
    nc = bass.Bass("TRN2", target_bir_lowering=False)
    dt = nc.dram_tensor

    x_in = dt("xT_in", [D, T], F32, kind="ExternalInput")
    y_out = dt("yT_out", [D, T], F32, kind="ExternalOutput")
    c_lay = dt("c_lay", [128, 8], F32, kind="ExternalInput")
    pos_own = dt("pos_own", [128, T], I32, kind="ExternalInput")
    pos_all = dt("pos_all", [128, 64], I32, kind="ExternalInput")
    pos_c0 = dt("pos_c0", [128, 1], I32, kind="ExternalInput")
    ada_w = dt("ada_w", [4, D, 3 * D], F32, kind="ExternalInput")
    ada_b = dt("ada_b_lay", [4, 128, 24], F32, kind="ExternalInput")
    norm_g = dt("norm_g_lay", [4, 128, 8], F32, kind="ExternalInput")
    final_g = dt("final_g_lay", [128, 8], F32, kind="ExternalInput")
    m_w_in = dt("m_w_in", [2, D, MLA_IN], F32, kind="ExternalInput")
    m_w_rot = dt("m_w_rot", [2, D, 32], F32, kind="ExternalInput")
    m_gq = dt("m_gq", [2, 128, 3], F32, kind="ExternalInput")
    m_gkv = dt("m_gkv", [2, 128, 2], F32, kind="ExternalInput")
    m_wq = dt("m_wq", [2, 384, 16 * 96], F32, kind="ExternalInput")
    m_wqr = dt("m_wqr", [2, 384, 16 * 32], F32, kind="ExternalInput")
    m_wk = dt("m_wk", [2, 256, 1024], F32, kind="ExternalInput")
    m_wv = dt("m_wv", [2, 256, 1024], F32, kind="ExternalInput")
    m_wo = dt("m_wo", [2, D, D], F32, kind="ExternalInput")
    d_w_in = dt("d_w_in", [2, D, 4096], F32, kind="ExternalInput")
    d_wo = dt("d_wo", [2, D, D], F32, kind="ExternalInput")
    d_hg = dt("d_hg", [128, 2], F32, kind="ExternalInput")
    d_lq = dt("d_lq", [2, 4, 128, 64], F32, kind="ExternalInput")
    k_ind2 = dt("k_ind2", [128, 8 * 16], F32, kind="ExternalInput")
    k_indh = dt("k_indh", [128, 16 * 16], F32, kind="ExternalInput")
    k_invf = dt("k_invf", [128, 1], F32, kind="ExternalInput")
    k_sgn = dt("k_sgn", [128, 1], F32, kind="ExternalInput")

    xs = dt("xs", [D, T], F32)
    cosd = dt("cosd", [128, T], F32)
    sind = dt("sind", [128, T], F32)
    posd = dt("posd", [128, T], F32)
    nbd = dt("nbd", [4, 64], F32)
    qs = [dt(f"qs{l}", [16, 128, T], BF16) for l in range(4)]
    kown = [[dt(f"kown{l}_{g}", [256, T], BF16) for g in range(4)] for l in range(4)]
    kall = [[dt(f"kall{l}_{g}", [1024, T], BF16) for g in range(4)] for l in range(4)]
    vown = [[dt(f"vown{l}_{g}", [T, 256], BF16) for g in range(4)] for l in range(4)]
    vall = [[dt(f"vall{l}_{g}", [4 * T, 256], BF16) for g in range(4)] for l in range(4)]
    smown = [dt(f"smown{l}", [32, T], BF16) for l in range(4)]
    small = [dt(f"small{l}", [128, T], BF16) for l in range(4)]
    stown = [dt(f"stown{l}", [16, 8], F32) for l in range(4)]
    stall = [dt(f"stall{l}", [64, 8], F32) for l in range(4)]

    sb = nc.alloc_sbuf_tensor if hasattr(nc, "alloc_sbuf_tensor") else None

    def sbt(name, shape, dtype):
        cm = nc.sbuf_tensor(name, shape, dtype)
        return cm.__enter__()

    def pst(name, shape, dtype):
        cm = nc.psum_tensor(name, shape, dtype)
        return cm.__enter__()

    ARENA_N = 200 * 1024 // 2
    arena = sbt("arena", [128, ARENA_N], BF16)

    def carve(off_bytes, nelem, dtype):
        a = arena[:, off_bytes // 2: off_bytes // 2 + (nelem * (2 if dtype == BF16 else 4)) // 2]
        if dtype != BF16:
            a = a.bitcast(dtype)
        return a

    ones_bf = sbt("ones_bf", [128, 128], BF16)
    ident_unused = None
    cact = sbt("cact", [128, 8], BF16)
    ctmp = sbt("ctmp", [128, 8], F32)
    mods = sbt("mods", [128, 4 * 24], F32)
    adab = sbt("adab", [128, 4 * 24], F32)
    Acoef = sbt("Acoef", [128, 4 * 8], F32)
    gnorm = sbt("gnorm", [128, 4 * 8], F32)
    fgl = sbt("fgl", [128, 8], F32)
    gq = sbt("gq", [128, 2 * 3], F32)
    gkv = sbt("gkv", [128, 2 * 2], F32)
    hgs = sbt("hgs", [128, 2], F32)
    lamt = sbt("lamt", [128, 8], F32)
    lqt = sbt("lqt", [128, 4 * 64], F32)
    lqp = sbt("lqp", [128, 64], F32)
    neglam = sbt("neglam", [128, 2], F32)
    ind2f = sbt("ind2f", [128, 128], F32)
    indhf = sbt("indhf", [128, 256], F32)
    ind2 = sbt("ind2", [128, 128], BF16)
    indh = sbt("indh", [128, 256], BF16)
    invf = sbt("invf", [128, 1], F32)
    sgn = sbt("sgn", [128, 1], F32)
    epst = sbt("epst", [128, 1], F32)
    zerot = sbt("zerot", [128, 1], F32)
    pkrel = sbt("pkrel", [128, 64], F32)
    pki = sbt("pki", [128, 64], I32)
    negpk = sbt("negpk", [128, 64], F32)
    c0i = sbt("c0i", [128, 1], I32)
    c0f = sbt("c0f", [128, 1], F32)
    qmx = sbt("qmx", [16, 8], F32)
    kmx = sbt("kmx", [16, 8], F32)
    qmax = sbt("qmax", [16, 4], F32)
    stl = sbt("stl", [16, 32], F32)
    kmax = sbt("kmax", [16, 1], F32)
    nb16 = sbt("nb16", [16, 4], F32)
    negb = sbt("negb", [128, 64], F32)
    negbh = sbt("negbh", [128, 32], F32)

    PS = [pst(f"ps{i}", [128, 1024], F32) for i in range(4)]

    def bank(i):
        return PS[i // 2][:, (i % 2) * 512:(i % 2) * 512 + 512]

    def btok(i):
        return f"bank{i}"

    blk = nc.Block()
    blk.__enter__()
    cx = Ctx(nc)

    def v3(ap, k):
        return ap.rearrange("p (k t) -> p k t", k=k)

    xT = v3(carve(0, 8 * T, F32), 8)
    gog = v3(carve(65536, 8 * T, BF16), 8)
    hT = v3(carve(98304, 8 * T, BF16), 8)
    wbig = carve(131072, 8192, BF16)
    wb = [carve(131072, 4096, BF16), carve(139264, 4096, BF16)]
    sqb = v3(carve(147456, 8 * 512, BF16), 8)
    t1 = [carve(155648, 512, F32), carve(157696, 512, F32)]
    rstd = carve(159744, 512, F32)
    qn = v3(carve(161792, 3 * T, BF16), 3)
    kvn = v3(carve(174080, 2 * T, BF16), 2)
    lat = v3(carve(182272, 5 * 512, F32), 5)
    krope = carve(182272, T, BF16)
    kropesq = carve(182272 + 4096, T, BF16)
    cosb = carve(192512, 512, F32)
    sinb = carve(194560, 512, F32)
    stg = [carve(196608 + i * 1024, 512, BF16) for i in range(4)]
    vst = [carve(200704, 1024, BF16), carve(202752, 1024, BF16)]
    Kt = [carve(0, 8192, BF16), carve(16384, 8192, BF16)]
    Vt = [carve(32768, 8192, BF16).rearrange("p (k c) -> p k c", k=64),
          carve(49152, 8192, BF16).rearrange("p (k c) -> p k c", k=64)]
    qt = [carve(98304 + i * 4096, T, BF16) for i in range(4)]
    Pt = [carve(114688 + i * 2048, 1024, BF16) for i in range(3)]
    tmpf = [carve(120832 + i * 4096, 1024, F32) for i in range(2)] + [carve(154624, 1024, F32)]
    tt = [carve(129024, 512, F32), carve(131072, 512, F32), carve(158720, 512, F32), carve(160768, 512, F32)]
    posbc = carve(133120, T, F32)
    rinv = [carve(141312 + i * 2048, 512, F32) for i in range(2)]
    osb = [carve(145408 + i * 2048, 512, F32) for i in range(2)]
    sqo = carve(149504, 512, BF16)
    rr = carve(150528, 512, F32)
    zer512 = carve(152576, 512, F32)
    dd = [carve(154624 + i * 2048, 512, F32) for i in range(2)]

    def tcs(tc):
        return slice(tc * 512, (tc + 1) * 512)

    cx.op("dve", lambda e: e.memset(ones_bf[:], 1.0), writes=["ones"])
    cx.op("dve", lambda e: e.memset(epst[:], EPS), writes=["eps"])
    cx.op("dve", lambda e: e.memset(zerot[:], 0.0), writes=["zero"])
    cx.dma("sp", ctmp[:], c_lay[:, :], writes=["ctmp"])
    cx.op("act", lambda e: e.activation(out=cact[:], in_=ctmp[:], func=AF.Silu), reads=["ctmp"], writes=["cact"])
    cx.dma("sp", adab[:].rearrange("p (l j) -> p l j", l=4), ada_b.ap().rearrange("l p j -> p l j"), writes=["adab"])
    cx.dma("sp", gnorm[:].rearrange("p (l j) -> p l j", l=4), norm_g.ap().rearrange("l p j -> p l j"), writes=["gnorm"])
    cx.dma("sp", fgl[:], final_g[:, :], writes=["fgl"])
    cx.dma("sp", gq[:].rearrange("p (l j) -> p l j", l=2), m_gq.ap().rearrange("l p j -> p l j"), writes=["gq"])
    cx.dma("sp", gkv[:].rearrange("p (l j) -> p l j", l=2), m_gkv.ap().rearrange("l p j -> p l j"), writes=["gkv"])
    cx.dma("sp", hgs[:], d_hg[:, :], writes=["hgs"])
    cx.dma("sp", ind2f[:], k_ind2[:, :], writes=["ind2f"])
    cx.dma("sp", indhf[:], k_indh[:, :], writes=["indhf"])
    cx.op("dve", lambda e: e.tensor_copy(out=ind2[:], in_=ind2f[:]), reads=["ind2f"], writes=["ind2"])
    cx.op("dve", lambda e: e.tensor_copy(out=indh[:], in_=indhf[:]), reads=["indhf"], writes=["indh"])
    cx.dma("sp", invf[:], k_invf[:, :], writes=["invf"])
    cx.dma("sp", sgn[:], k_sgn[:, :], writes=["sgn"])
    cx.dma("sp", pki[:], pos_all[:, :], writes=["pki"])
    cx.dma("sp", c0i[:], pos_c0[:, :], writes=["c0i"])
    cx.op("dve", lambda e: e.tensor_copy(out=c0f[:], in_=c0i[:]), reads=["c0i"], writes=["c0f"])
    cx.op("dve", lambda e: e.tensor_copy(out=pkrel[:], in_=pki[:]), reads=["pki"], writes=["pkrel"])
    cx.op("dve", lambda e: e.tensor_scalar(out=pkrel[:], in0=pkrel[:], scalar1=c0f[:, 0:1], scalar2=None,
                                           op0=ALU.subtract), reads=["pkrel", "c0f"], writes=["pkrel"])
    cx.op("dve", lambda e: e.tensor_scalar(out=negpk[:], in0=pkrel[:], scalar1=-1.0, scalar2=None, op0=ALU.mult),
          reads=["pkrel"], writes=["negpk"])
    for j in range(2):
        l = 2 * j + 1
        lam_init = 0.8 - 0.6 * math.exp(-0.3 * l)
        cx.dma("sp", lqt[:].rearrange("p (a d) -> p a d", a=4), d_lq[j].rearrange("a p d -> p a d"), writes=["lqt"])
        for a in range(2):
            cx.op("dve", lambda e: e.tensor_tensor(out=lqp[:], in0=lqt[:, (2 * a) * 64:(2 * a + 1) * 64],
                                                   in1=lqt[:, (2 * a + 1) * 64:(2 * a + 2) * 64], op=ALU.mult),
                  reads=["lqt"], writes=["lqp"])
            cx.op("dve", lambda e: e.tensor_reduce(out=lamt[:, a:a + 1], in_=lqp[:], axis=AX.X, op=ALU.add),
                  reads=["lqp"], writes=["lamt"])
        cx.op("act", lambda e: e.activation(out=lamt[:, 2:4], in_=lamt[:, 0:2], func=AF.Exp), reads=["lamt"], writes=["lamt"])
        cx.op("dve", lambda e: e.tensor_tensor(out=lamt[:, 4:5], in0=lamt[:, 3:4], in1=lamt[:, 2:3], op=ALU.subtract),
              reads=["lamt"], writes=["lamt"])
        cx.op("dve", lambda e: e.tensor_scalar(out=neglam[:, j:j + 1], in0=lamt[:, 4:5], scalar1=-lam_init, scalar2=None,
                                               op0=ALU.add), reads=["lamt"], writes=["neglam"])
        cx.op("dve", lambda e: e.tensor_scalar(out=hgs[:, j:j + 1], in0=hgs[:, j:j + 1], scalar1=(1.0 - lam_init),
                                               scalar2=None, op0=ALU.mult), reads=["hgs"], writes=["hgs"])

    pi = 0
    for l in layers:
        for p6 in range(6):
            w = v3(wb[pi % 2], 8)
            tok = f"wb{pi % 2}"
            cx.dma("pool", w, ada_w[l][:, p6 * 512:(p6 + 1) * 512].rearrange("(k p) c -> p k c", p=128), writes=[tok])
            for jj in range(4):
                j = p6 * 4 + jj
                for k in range(8):
                    cx.op("pe", lambda e: e.matmul(bank(0)[:, j:j + 1], w[:, k, jj * 128:(jj + 1) * 128], cact[:, k:k + 1],
                                                   start=(k == 0), stop=(k == 7)),
                          reads=[tok, "cact"], writes=[btok(0)], inc=(k == 7))
            pi += 1
        cx.op("dve", lambda e: e.tensor_tensor(out=mods[:, l * 24:(l + 1) * 24], in0=bank(0)[:, 0:24],
                                               in1=adab[:, l * 24:(l + 1) * 24], op=ALU.add),
              reads=[btok(0), "adab"], writes=["mods"])
        cx.op("dve", lambda e: e.scalar_tensor_tensor(out=Acoef[:, l * 8:(l + 1) * 8], in0=mods[:, l * 24 + 8:l * 24 + 16],
                                                     scalar=1.0, in1=gnorm[:, l * 8:(l + 1) * 8], op0=ALU.add, op1=ALU.mult),
              reads=["mods", "gnorm"], writes=["Acoef"])

    for k in range(8):
        cx.dma("sp", xT[:, k, :], x_in[k * 128:(k + 1) * 128, :], writes=[f"xT{k}"])

    TWO_PI = 2.0 * math.pi
    MAGIC = 12582912.0
    pint = carve(98304, 512, I32)
    pf = carve(98304 + 2048, 512, F32)
    ya = carve(98304 + 4096, 512, F32)
    yb = carve(98304 + 6144, 512, F32)
    yc = carve(98304 + 8192, 512, F32)
    for tc in range(4):
        cx.dma("sp", pint, pos_own[:, tcs(tc)], writes=["pint"])
        cx.op("dve", lambda e: e.tensor_copy(out=pf, in_=pint), reads=["pint"], writes=["pf"])
        cx.op("dve", lambda e: e.tensor_scalar(out=ya, in0=pf, scalar1=c0f[:, 0:1], scalar2=None, op0=ALU.subtract),
              reads=["pf", "c0f"], writes=["ya"])
        cx.dma("sp", posd[:, tcs(tc)], ya, reads=["ya"], writes=["posd"])
        for which in range(2):
            off = 0.0 if which == 0 else 0.25
            cx.op("dve", lambda e: e.tensor_scalar(out=ya, in0=pf, scalar1=invf[:, 0:1], scalar2=1.0 / TWO_PI,
                                                   op0=ALU.mult, op1=ALU.mult), reads=["pf", "invf"], writes=["ya"])
            if which == 1:
                cx.op("dve", lambda e: e.tensor_scalar(out=ya, in0=ya, scalar1=off, scalar2=None, op0=ALU.add),
                      reads=["ya"], writes=["ya"])
            cx.op("dve", lambda e: e.tensor_scalar(out=yb, in0=ya, scalar1=MAGIC, scalar2=None, op0=ALU.add),
                  reads=["ya"], writes=["yb"])
            cx.op("dve", lambda e: e.tensor_scalar(out=yb, in0=yb, scalar1=MAGIC, scalar2=None, op0=ALU.subtract),
                  reads=["yb"], writes=["yb"])
            cx.op("dve", lambda e: e.tensor_tensor(out=yc, in0=ya, in1=yb, op=ALU.subtract), reads=["ya", "yb"], writes=["yc"])
            cx.op("dve", lambda e: e.tensor_scalar(out=yc, in0=yc, scalar1=0.49999, scalar2=-0.49999, op0=ALU.min, op1=ALU.max),
                  reads=["yc"], writes=["yc"])
            cx.op("act", lambda e: e.activation(out=yb, in_=yc, func=AF.Sin, scale=TWO_PI), reads=["yc"], writes=["yb"])
            if which == 0:
                cx.op("dve", lambda e: e.tensor_scalar(out=yb, in0=yb, scalar1=sgn[:, 0:1], scalar2=None, op0=ALU.mult),
                      reads=["yb", "sgn"], writes=["yb"])
                cx.dma("sp", sind[:, tcs(tc)], yb, reads=["yb"], writes=["sind"])
            else:
                cx.dma("sp", cosd[:, tcs(tc)], yb, reads=["yb"], writes=["cosd"])
    cx.barrier()

    def norm_stats(tc, nfeat, src_chunks, src_tokens, dst_rstd, bk):
        n = len(src_chunks)
        for i, (ap, tok) in enumerate(zip(src_chunks, src_tokens)):
            cx.op("pe", lambda e: e.matmul(bank(bk), ones_bf[:, :], ap, start=(i == 0), stop=(i == n - 1)),
                  reads=[tok, "ones"], writes=[btok(bk)], inc=(i == n - 1))
        cx.op("act", lambda e: e.activation(out=dst_rstd, in_=bank(bk), func=AF.Sqrt, bias=epst[:, 0:1], scale=1.0 / nfeat),
              reads=[btok(bk), "eps"], writes=["rstd"])
        cx.op("dve", lambda e: e.reciprocal(out=dst_rstd, in_=dst_rstd), reads=["rstd"], writes=["rstd"])

    def make_hT(l):
        for tc in range(4):
            cx.op("act", lambda e: e.activation(out=sqb[:, :, :], in_=xT[:, :, tcs(tc)], func=AF.Square),
                  reads=[f"xT{k}" for k in range(8)], writes=["sqb"])
            norm_stats(tc, 1024, [sqb[:, k, :] for k in range(8)], ["sqb"] * 8, rstd, 0)
            for k in range(8):
                tb = t1[k % 2]
                cx.op("dve", lambda e: e.tensor_tensor(out=tb, in0=xT[:, k, tcs(tc)], in1=rstd, op=ALU.mult),
                      reads=[f"xT{k}", "rstd"], writes=[f"t1{k % 2}"])
                cx.op("act", lambda e: e.activation(out=hT[:, k, tcs(tc)], in_=tb, func=AF.Identity,
                                                    bias=mods[:, l * 24 + k:l * 24 + k + 1],
                                                    scale=Acoef[:, l * 8 + k:l * 8 + k + 1]),
                      reads=[f"t1{k % 2}", "mods", "Acoef"], writes=[f"hT{tc}"])

    def spill_x():
        for k in range(8):
            cx.dma("sp", xs[k * 128:(k + 1) * 128, :], xT[:, k, :], reads=[f"xT{k}"], writes=["xs"])

    def load_w(dst, src_ap, tok):
        cx.dma("pool", dst, src_ap, writes=[tok])

    bkc = [0]

    def nextbank():
        bkc[0] = (bkc[0] + 1) % 6
        return bkc[0] + 2

    def proj(M, lhs_list, rhs_list, reads):
        bk = nextbank()
        n = len(lhs_list)
        for i in range(n):
            cx.op("pe", lambda e: e.matmul(bank(bk)[0:M, :], lhs_list[i], rhs_list[i], start=(i == 0), stop=(i == n - 1)),
                  reads=reads, writes=[btok(bk)], inc=(i == n - 1))
        return bk

    def p_phase_mla(l):
        j = l // 2
        make_hT(l)
        spill_x()
        scale = 96.0 ** -0.5
        ckpt(21)
        wl = wbig[:, 0:8 * 640].rearrange("p (k c) -> p k c", k=8)
        load_w(wl, m_w_in[j][:, 0:640].rearrange("(k p) c -> p k c", p=128), "wbig")
        for tc in range(4):
            for ci in range(5):
                bk = proj(128, [wl[:, k, ci * 128:(ci + 1) * 128] for k in range(8)], [hT[:, k, tcs(tc)] for k in range(8)],
                          ["wbig", f"hT{tc}"])
                cx.op("dve", lambda e: e.tensor_copy(out=lat[:, ci, :], in_=bank(bk)), reads=[btok(bk)], writes=[f"lat{ci}"])
                cx.op("act", lambda e: e.activation(out=sqb[:, ci, :], in_=bank(bk), func=AF.Square),
                      reads=[btok(bk)], writes=[f"sqb{ci}"])
            norm_stats(tc, 384, [sqb[:, ci, :] for ci in range(3)], [f"sqb{ci}" for ci in range(3)], rstd, 0)
            for ci in range(3):
                cx.op("dve", lambda e: e.scalar_tensor_tensor(out=qn[:, ci, tcs(tc)], in0=lat[:, ci, :],
                                                             scalar=gq[:, j * 3 + ci:j * 3 + ci + 1], in1=rstd,
                                                             op0=ALU.mult, op1=ALU.mult),
                      reads=[f"lat{ci}", "rstd", "gq"], writes=["qn"])
            norm_stats(tc, 256, [sqb[:, 3 + ci, :] for ci in range(2)], [f"sqb{3 + ci}" for ci in range(2)], rstd, 1)
            for ci in range(2):
                cx.op("dve", lambda e: e.scalar_tensor_tensor(out=kvn[:, ci, tcs(tc)], in0=lat[:, 3 + ci, :],
                                                             scalar=gkv[:, j * 2 + ci:j * 2 + ci + 1], in1=rstd,
                                                             op0=ALU.mult, op1=ALU.mult),
                      reads=[f"lat{3 + ci}", "rstd", "gkv"], writes=["kvn"])
        ckpt(22)
        wr = wbig[:, 0:8 * 64].rearrange("p (k c) -> p k c", k=8)
        cx.dma("pool", wr[:, :, 0:32], m_w_in[j][:, 640:672].rearrange("(k p) c -> p k c", p=128), writes=["wbig"])
        cx.dma("pool", wr[:, :, 32:64], m_w_rot[j].rearrange("(k p) c -> p k c", p=128), writes=["wbig2"],
               reads=["wbig"])
        for tc in range(4):
            cx.dma("sp", cosb, cosd[:, tcs(tc)], writes=["cosb"])
            cx.dma("sp", sinb, sind[:, tcs(tc)], writes=["sinb"])
            b1 = proj(32, [wr[:, k, 0:32] for k in range(8)], [hT[:, k, tcs(tc)] for k in range(8)], ["wbig", "wbig2", f"hT{tc}"])
            b2 = proj(32, [wr[:, k, 32:64] for k in range(8)], [hT[:, k, tcs(tc)] for k in range(8)], ["wbig", "wbig2", f"hT{tc}"])
            cx.op("dve", lambda e: e.tensor_tensor(out=t1[0][0:32, :], in0=bank(b1)[0:32, :], in1=cosb[0:32, :], op=ALU.mult),
                  reads=[btok(b1), "cosb"], writes=["t10"])
            cx.op("dve", lambda e: e.tensor_tensor(out=t1[1][0:32, :], in0=bank(b2)[0:32, :], in1=sinb[0:32, :], op=ALU.mult),
                  reads=[btok(b2), "sinb"], writes=["t11"])
            cx.op("pool", lambda e: e.tensor_tensor(out=krope[0:32, tcs(tc)], in0=t1[0][0:32, :], in1=t1[1][0:32, :], op=ALU.add),
                  reads=["t10", "t11"], writes=["krope"])
            cx.op("act", lambda e: e.activation(out=kropesq[0:32, tcs(tc)], in_=krope[0:32, tcs(tc)], func=AF.Square),
                  reads=["krope"], writes=["kropesq"])
        cx.dma("sp", smown[l][:, :], krope[0:32, :], reads=["krope"], writes=["smown"])
        si = 0
        cx.op("dve", lambda e: e.memset(qmx[:], 0.0), writes=["qmx"])
        cx.op("dve", lambda e: e.memset(kmx[:], 0.0), writes=["kmx"])
        def st_gate():
            nonlocal si
            for pc in range(2):
                load_w(v3(wb[pc], 8), m_w_in[j][:, 672 + pc * 512:672 + (pc + 1) * 512].rearrange("(k p) c -> p k c", p=128),
                       "wbig" if pc == 0 else "wbig2")
            for pc in range(2):
                w = v3(wb[pc], 8)
                for ci in range(4):
                    for tc in range(4):
                        bk = proj(128, [w[:, k, ci * 128:(ci + 1) * 128] for k in range(8)],
                                  [hT[:, k, tcs(tc)] for k in range(8)], ["wbig", "wbig2", f"hT{tc}"])
                        cx.op("act", lambda e: e.activation(out=gog[:, pc * 4 + ci, tcs(tc)], in_=bank(bk), func=AF.Silu),
                              reads=[btok(bk)], writes=[f"gog{pc * 4 + ci}"])

        def st_qup():
            nonlocal si
            wq = wbig[:, 0:3 * 1536].rearrange("p (k c) -> p k c", k=3)
            wqr = wbig[:, 3 * 1536:3 * 1536 + 3 * 512].rearrange("p (k c) -> p k c", k=3)
            cx.dma("pool", wq, m_wq[j].rearrange("(k p) c -> p k c", p=128), writes=["wbig"], reads=["wbig2"])
            cx.dma("pool", wqr, m_wqr[j].rearrange("(k p) c -> p k c", p=128), writes=["wbig2"], reads=["wbig"])
            cx.collective(stown[l].ap().opt(), stall[l].ap().opt(), reads=["stown"], writes=["stall"])
            cx.collective(smown[l].ap().opt(), small[l].ap().opt(), reads=["smown"], writes=["small"])
            for g in range(4):
                cx.collective(kown[l][g].ap().opt(), kall[l][g].ap().opt(), reads=[f"kown{g}"], writes=[f"kall{g}"])
                cx.collective(vown[l][g].ap().opt(), vall[l][g].ap().opt(), reads=[f"vown{g}"], writes=[f"vall{g}"])
            deferred = []
            stgx = [(stg[i], f"stg{i}") for i in range(4)] + [(vst[0][:, 0:512], "vst0"), (vst[0][:, 512:1024], "vst0"),
                                                               (vst[1][:, 0:512], "vst1"), (vst[1][:, 512:1024], "vst1")]
            for tc in range(4):
                cx.dma("sp", cosb, cosd[:, tcs(tc)], writes=["cosb"])
                cx.dma("sp", sinb, sind[:, tcs(tc)], writes=["sinb"])
                for h in range(16):
                    b1 = proj(96, [wq[:, k, h * 96:(h + 1) * 96] for k in range(3)], [qn[:, k, tcs(tc)] for k in range(3)],
                              ["wbig", "wbig2", "qn"])
                    b2 = proj(32, [wqr[:, k, h * 32:(h + 1) * 32] for k in range(3)], [qn[:, k, tcs(tc)] for k in range(3)],
                              ["wbig", "wbig2", "qn"])
                    while deferred:
                        deferred.pop(0)()
                    st, stok = stgx[si % 8]
                    si += 1
                    cx.op("act", lambda e: e.activation(out=st[0:64, :], in_=bank(b1)[0:64, :], func=AF.Copy, scale=scale),
                          reads=[btok(b1)], writes=[stok])
                    cx.op("dve", lambda e: e.scalar_tensor_tensor(out=t1[0][64:96, :], in0=bank(b1)[64:96, :], scalar=scale,
                                                                 in1=cosb[64:96, :], op0=ALU.mult, op1=ALU.mult),
                          reads=[btok(b1), "cosb"], writes=["t10"])
                    cx.op("dve", lambda e: e.scalar_tensor_tensor(out=t1[1][64:96, :], in0=bank(b2)[0:32, :], scalar=scale,
                                                                 in1=sinb[0:32, :], op0=ALU.mult, op1=ALU.mult),
                          reads=[btok(b2), "sinb"], writes=["t11"])
                    cx.op("dve", lambda e: e.tensor_tensor(out=st[64:96, :], in0=t1[0][64:96, :], in1=t1[1][64:96, :], op=ALU.add),
                          reads=["t10", "t11"], writes=[stok])
                    sq = sqb[:, h % 8, :]
                    cx.op("act", lambda e: e.activation(out=sq[0:96, :], in_=st[0:96, :], func=AF.Square),
                          reads=[stok], writes=[f"sqb{h % 8}"])

                    def ind_mm(h=h, sq=sq):
                        cx.op("pe", lambda e: e.matmul(bank(1)[0:16, :], indh[0:96, h * 16:(h + 1) * 16], sq[0:96, :],
                                                       start=(h == 0), stop=(h == 15)),
                              reads=[f"sqb{h % 8}", "indh"], writes=[btok(1)], inc=(h == 15))
                    deferred.append(ind_mm)
                    cx.dma("sp", qs[l][h][0:96, tcs(tc)], st[0:96, :], reads=[stok], writes=["qs"])
                while deferred:
                    deferred.pop(0)()
                cx.op("dve", lambda e: e.tensor_reduce(out=qmx[:, tc:tc + 1], in_=bank(1)[0:16, :], axis=AX.X, op=ALU.max),
                      reads=[btok(1)], writes=["qmx"])
            cx.op("dve", lambda e: e.tensor_copy(out=qmax[:], in_=qmx[:, 0:4]), reads=["qmx"], writes=["qmax"])

        def st_kv():
            nonlocal si
            wk = wbig[:, 0:2 * 1024].rearrange("p (k c) -> p k c", k=2)
            wv = wbig[:, 2048:2048 + 2 * 1024].rearrange("p (k c) -> p k c", k=2)
            cx.dma("pool", wk, m_wk[j].rearrange("(k p) c -> p k c", p=128), writes=["wbig"], reads=["wbig2"])
            cx.dma("pool", wv, m_wv[j].rearrange("(k p) c -> p k c", p=128), writes=["wbig2"], reads=["wbig"])
            kdef = []
            for tc in range(4):
                for hp in range(8):
                    bk = proj(128, [wk[:, k, hp * 128:(hp + 1) * 128] for k in range(2)], [kvn[:, k, tcs(tc)] for k in range(2)],
                              ["wbig", "wbig2", "kvn"])
                    while kdef:
                        kdef.pop(0)()
                    st = stg[si % 4]
                    stok = f"stg{si % 4}"
                    si += 1
                    cx.op("act", lambda e: e.activation(out=st, in_=bank(bk), func=AF.Copy), reads=[btok(bk)], writes=[stok])
                    sq = sqb[:, hp, :]
                    cx.op("act", lambda e: e.activation(out=sq, in_=bank(bk), func=AF.Square), reads=[btok(bk)],
                          writes=[f"sqb{hp}"])
                    def ind_mm(hp=hp, sq=sq):
                        cx.op("pe", lambda e: e.matmul(bank(1)[0:16, :], ind2[:, hp * 16:(hp + 1) * 16], sq, start=(hp == 0), stop=False),
                              reads=[f"sqb{hp}", "ind2"], writes=[btok(1)], inc=False)
                    kdef.append(ind_mm)
                    g = hp // 2
                    cx.dma("sp", kown[l][g][(hp % 2) * 128:(hp % 2) * 128 + 128, tcs(tc)], st, reads=[stok], writes=[f"kown{g}"])
                while kdef:
                    kdef.pop(0)()
                cx.op("pe", lambda e: e.matmul(bank(1)[0:16, :], ones_bf[0:32, 0:16], kropesq[0:32, tcs(tc)], start=False, stop=True),
                      reads=["kropesq", "ones"], writes=[btok(1)], inc=True)
                cx.op("dve", lambda e: e.tensor_reduce(out=kmx[:, tc:tc + 1], in_=bank(1)[0:16, :], axis=AX.X, op=ALU.max),
                      reads=[btok(1)], writes=["kmx"])
            cx.dma("sp", stown[l][:, :], kmx[:], reads=["kmx"], writes=["stown"])
            for tb in range(16):
                vs = vst[tb % 2]
                vtok = f"vst{tb % 2}"
                for half in range(2):
                    bk = proj(128, [kvn[:, k, tb * 128:(tb + 1) * 128] for k in range(2)],
                              [wv[:, k, half * 512:(half + 1) * 512] for k in range(2)], ["wbig", "wbig2", "kvn"])
                    cx.op("act" if half == 0 else "dve",
                          (lambda e: e.activation(out=vs[:, half * 512:(half + 1) * 512], in_=bank(bk), func=AF.Copy)) if half == 0
                          else (lambda e: e.tensor_copy(out=vs[:, half * 512:(half + 1) * 512], in_=bank(bk))),
                          reads=[btok(bk)], writes=[vtok])
                for g in range(4):
                    cx.dma("sp", vown[l][g][tb * 128:(tb + 1) * 128, :], vs[:, g * 256:(g + 1) * 256], reads=[vtok],
                           writes=[f"vown{g}"])

        st_kv()
        st_gate()
        st_qup()


    def p_phase_diff(l):
        j = l // 2
        make_hT(l)
        spill_x()
        cx.op("dve", lambda e: e.memset(qmx[:], 0.0), writes=["qmx"])
        cx.op("dve", lambda e: e.memset(kmx[:], 0.0), writes=["kmx"])
        for i in range(4):
            cx.op("pool", lambda e: e.memset(stg[i], 0.0), writes=[f"stg{i}"])
        si = 0
        order = [2, 3, 4, 5, 0, 1, 6, 7]

        def issue_load(pidx):
            pcc = order[pidx]
            load_w(v3(wb[pidx % 2], 8), d_w_in[j][:, pcc * 512:(pcc + 1) * 512].rearrange("(k p) c -> p k c", p=128),
                   f"wb{pidx % 2}")
            if pidx == 7:
                cx.collective(stown[l].ap().opt(), stall[l].ap().opt(), reads=["stown"], writes=["stall"])
                for g in range(4):
                    cx.collective(kown[l][g].ap().opt(), kall[l][g].ap().opt(), reads=[f"kown{g}"], writes=[f"kall{g}"])
                    cx.collective(vown[l][g].ap().opt(), vall[l][g].ap().opt(), reads=[f"vown{g}"], writes=[f"vall{g}"])

        issue_load(0)
        for pidx, pc in enumerate(order):
            w = v3(wb[pidx % 2], 8)
            wtok = f"wb{pidx % 2}"
            if pidx + 1 < 8:
                issue_load(pidx + 1)
            kind = pc // 2
            ddef = []
            if kind in (0, 1):
                mx = qmx if kind == 0 else kmx
                mtok = "qmx" if kind == 0 else "kmx"
                for tc in range(4):
                    for hh in range(4):
                        h = (pc % 2) * 4 + hh
                        bk = proj(128, [w[:, k, hh * 128:(hh + 1) * 128] for k in range(8)],
                                  [hT[:, k, tcs(tc)] for k in range(8)], [wtok, f"hT{tc}"])
                        while ddef:
                            ddef.pop(0)()
                        sq = sqb[:, hh, :]
                        if kind == 0:
                            s0, s1 = stg[(si % 2) * 2], stg[(si % 2) * 2 + 1]
                            t0, t1k = f"stg{(si % 2) * 2}", f"stg{(si % 2) * 2 + 1}"
                            si += 1
                            cx.op("act", lambda e: e.activation(out=s0[0:64, :], in_=bank(bk)[0:64, :], func=AF.Copy, scale=0.125),
                                  reads=[btok(bk)], writes=[t0])
                            cx.op("act", lambda e: e.activation(out=s1[64:128, :], in_=bank(bk)[64:128, :], func=AF.Copy, scale=0.125),
                                  reads=[btok(bk)], writes=[t1k])
                            cx.op("act", lambda e: e.activation(out=sq, in_=bank(bk), func=AF.Square, scale=0.125),
                                  reads=[btok(bk)], writes=[f"sqb{hh}"])
                            cx.dma("sp", qs[l][2 * h][:, tcs(tc)], s0, reads=[t0], writes=["qs"])
                            cx.dma("sp", qs[l][2 * h + 1][:, tcs(tc)], s1, reads=[t1k], writes=["qs"])
                        else:
                            st = vst[si % 2][:, 0:512]
                            stok = f"vst{si % 2}"
                            si += 1
                            cx.op("act", lambda e: e.activation(out=st, in_=bank(bk), func=AF.Copy), reads=[btok(bk)], writes=[stok])
                            cx.op("act", lambda e: e.activation(out=sq, in_=bank(bk), func=AF.Square),
                                  reads=[btok(bk)], writes=[f"sqb{hh}"])
                            g = h // 2
                            cx.dma("sp", kown[l][g][(h % 2) * 128:(h % 2) * 128 + 128, tcs(tc)], st, reads=[stok],
                                   writes=[f"kown{g}"])
                        def ind_mm(h=h, hh=hh, sq=sq):
                            cx.op("pe", lambda e: e.matmul(bank(1)[0:16, :], ind2[:, h * 16:(h + 1) * 16], sq,
                                                           start=(hh == 0), stop=(hh == 3)),
                                  reads=[f"sqb{hh}", "ind2"], writes=[btok(1)], inc=(hh == 3))
                        ddef.append(ind_mm)
                    while ddef:
                        ddef.pop(0)()
                    col = (pc % 2) * 4 + tc
                    cx.op("dve", lambda e: e.tensor_reduce(out=mx[:, col:col + 1], in_=bank(1)[0:16, :], axis=AX.X, op=ALU.max),
                          reads=[btok(1)], writes=[mtok])
                if pc == 1:
                    cx.op("dve", lambda e: e.tensor_tensor(out=qmax[:], in0=qmx[:, 0:4], in1=qmx[:, 4:8], op=ALU.max),
                          reads=["qmx"], writes=["qmax"])
                if pc == 3:
                    cx.dma("sp", stown[l][:, :], kmx[:], reads=["kmx"], writes=["stown"])
            elif kind == 2:
                for tb in range(16):
                    bk = proj(128, [hT[:, k, tb * 128:(tb + 1) * 128] for k in range(8)], [w[:, k, :] for k in range(8)],
                              [wtok, f"hT{tb // 4}"])
                    vs = vst[tb % 2][:, 0:512]
                    vtok = f"vst{tb % 2}"
                    cx.op("act" if tb % 2 == 0 else "dve",
                          (lambda e: e.activation(out=vs, in_=bank(bk), func=AF.Copy)) if tb % 2 == 0
                          else (lambda e: e.tensor_copy(out=vs, in_=bank(bk))),
                          reads=[btok(bk)], writes=[vtok])
                    for gg in range(2):
                        g = (pc % 2) * 2 + gg
                        cx.dma("sp", vown[l][g][tb * 128:(tb + 1) * 128, :], vs[:, gg * 256:(gg + 1) * 256], reads=[vtok],
                               writes=[f"vown{g}"])
            else:
                for ci in range(4):
                    for tc in range(4):
                        bk = proj(128, [w[:, k, ci * 128:(ci + 1) * 128] for k in range(8)],
                                  [hT[:, k, tcs(tc)] for k in range(8)], [wtok, f"hT{tc}"])
                        c8 = (pc % 2) * 4 + ci
                        cx.op("act", lambda e: e.activation(out=gog[:, c8, tcs(tc)], in_=bank(bk), func=AF.Silu),
                              reads=[btok(bk)], writes=[f"gog{c8}"])

    def a_prologue(l):
        cx.dma("sp", stl[:].rearrange("m (r c) -> m r c", r=4), stall[l].ap().rearrange("(r m) c -> m r c", r=4),
               reads=["stall"], writes=["stl"])
        cx.op("dve", lambda e: e.tensor_reduce(out=kmax[:], in_=stl[:], axis=AX.X, op=ALU.max), reads=["stl"], writes=["kmax"])
        cx.op("dve", lambda e: e.tensor_scalar(out=nb16[:], in0=qmax[:], scalar1=kmax[:, 0:1], scalar2=None, op0=ALU.mult),
              reads=["qmax", "kmax"], writes=["nb16"])
        cx.op("act", lambda e: e.activation(out=nb16[:], in_=nb16[:], func=AF.Sqrt), reads=["nb16"], writes=["nb16"])
        cx.op("dve", lambda e: e.tensor_scalar(out=nb16[:], in0=nb16[:], scalar1=-1.02, scalar2=None, op0=ALU.mult),
              reads=["nb16"], writes=["nb16"])
        cx.dma("sp", nbd[l].rearrange("(m c) -> m c", m=16), nb16[:], reads=["nb16"], writes=["nbd"])
        cx.dma("sp", negb[:], nbd[l:l + 1, :].partition_broadcast(128), reads=["nbd"], writes=["negb"])

    def a_phase_mla(l):
        a_prologue(l)
        for b in range(2):
            cx.op("dve", lambda e: e.memset(Vt[b][:, :, 64:128], 1.0), writes=[f"Vt{b}"])
            for r in range(4):
                cx.dma("sp", Kt[b][64:96, r * T:(r + 1) * T], small[l][r * 32:(r + 1) * 32, :], reads=["small"],
                       writes=[f"Kr{b}"])

        def loads(h):
            b = h % 2
            g, hh = h // 4, h % 4
            cx.dma("sp", qt[b][0:96, :], qs[l][h][0:96, :], reads=["qs"], writes=[f"qt{b}"])
            for r in range(4):
                cx.dma("sp", Kt[b][0:64, r * T:(r + 1) * T], kall[l][g][r * 256 + hh * 64:r * 256 + hh * 64 + 64, :],
                       reads=[f"kall{g}"], writes=[f"Kt{b}"])
            for r in range(4):
                cx.dma("sp", Vt[b][:, r * 16:(r + 1) * 16, 0:64],
                       vall[l][g][r * T:(r + 1) * T, hh * 64:(hh + 1) * 64].rearrange("(k p) c -> p k c", p=128),
                       reads=[f"vall{g}"], writes=[f"Vt{b}"])

        loads(0)
        it = 0
        for h in range(16):
            if h + 1 < 16:
                loads(h + 1)
            b = h % 2
            for tc in range(4):
                ab = 4 + (it % 2)
                it += 1
                NU = 32
                bias_ap = negb[:, h * 4 + tc:h * 4 + tc + 1]

                SB = [PS[0], PS[1], PS[3]]

                def qk(u):
                    for kk in range(2):
                        kb = 2 * u + kk
                        cx.op("pe", lambda e: e.matmul(SB[u % 3][:, kk * 512:(kk + 1) * 512], Kt[b][0:96, kb * 128:(kb + 1) * 128],
                                                       qt[b][0:96, tcs(tc)], start=True, stop=True),
                              reads=[f"Kt{b}", f"Kr{b}", f"qt{b}"], writes=[f"S{u % 3}"], inc=(kk == 1))

                def ex(u):
                    cx.op("act", lambda e: e.activation(out=Pt[u % 3], in_=SB[u % 3][:, :], func=AF.Exp, bias=bias_ap, scale=1.0),
                          reads=[f"S{u % 3}", "negb"], writes=[f"P{u % 3}"])

                def pv(u):
                    for kk in range(2):
                        kb = 2 * u + kk
                        cx.op("pe", lambda e: e.matmul(bank(ab), Vt[b][:, kb, :], Pt[u % 3][:, kk * 512:(kk + 1) * 512],
                                                       start=(kb == 0), stop=(kb == 63)),
                              reads=[f"Vt{b}", f"P{u % 3}"], writes=[btok(ab)], inc=(kk == 1))

                qk(0)
                qk(1)
                for u in range(NU):
                    if u + 2 < NU:
                        qk(u + 2)
                    ex(u)
                    pv(u)
                ri = rinv[it % 2]
                ob = osb[it % 2]
                pb = (h % 2) * 64
                cx.op("dve", lambda e: e.reciprocal(out=ri[0:64, :], in_=bank(ab)[64:128, :]), reads=[btok(ab)],
                      writes=[f"rinv{it % 2}"])
                cx.op("dve", lambda e: e.tensor_tensor(out=ob[pb:pb + 64, :], in0=bank(ab)[0:64, :], in1=ri[0:64, :], op=ALU.mult),
                      reads=[btok(ab), f"rinv{it % 2}"], writes=[f"osb{it % 2}"])
                cx.op("dve", lambda e: e.tensor_tensor(out=gog[pb:pb + 64, h // 2, tcs(tc)], in0=ob[pb:pb + 64, :],
                                                        in1=gog[pb:pb + 64, h // 2, tcs(tc)], op=ALU.mult),
                      reads=[f"osb{it % 2}", f"gog{h // 2}"], writes=[f"gog{h // 2}"])

    def a_phase_diff(l):
        j = l // 2
        a_prologue(l)
        cx.dma("sp", posbc, posd[:, :], writes=["posbc"])
        nv = negb[:].rearrange("p (h c t) -> p h c t", h=8, c=2)
        cx.op("dve", lambda e: e.tensor_tensor(out=negbh[:].rearrange("p (h t) -> p h t", h=8), in0=nv[:, :, 0, :],
                                               in1=nv[:, :, 1, :], op=ALU.min), reads=["negb"], writes=["negbh"])

        def loads(h):
            b = h % 2
            g = h // 2
            for c in range(2):
                cx.dma("sp", qt[2 * b + c], qs[l][2 * h + c][:, :], reads=["qs"], writes=[f"qt{2 * b + c}"])
            for r in range(4):
                cx.dma("sp", Kt[b][:, r * T:(r + 1) * T], kall[l][g][r * 256 + (h % 2) * 128:r * 256 + (h % 2) * 128 + 128, :],
                       reads=[f"kall{g}"], writes=[f"Kt{b}"])
            for r in range(4):
                cx.dma("sp", Vt[b][:, r * 16:(r + 1) * 16, :],
                       vall[l][g][r * T:(r + 1) * T, (h % 2) * 128:(h % 2) * 128 + 128].rearrange("(k p) c -> p k c", p=128),
                       reads=[f"vall{g}"], writes=[f"Vt{b}"])

        loads(0)
        it = 0
        for h in range(8):
            if h + 1 < 8:
                loads(h + 1)
            b = h % 2
            slope = 2.0 ** (-(h + 1))
            for tc in range(4):
                it += 1
                NU = 64

                def qk(u):
                    for c in range(2):
                        cx.op("pe", lambda e: e.matmul(bank((u % 2) * 2 + c), Kt[b][c * 64:(c + 1) * 64, u * 128:(u + 1) * 128],
                                                       qt[2 * b + c][c * 64:(c + 1) * 64, tcs(tc)], start=True, stop=True),
                              reads=[f"Kt{b}", f"qt{2 * b + c}"], writes=[f"S{u % 2}"], inc=(c == 1))

                def bias(u):
                    cx.op("act", lambda e: e.activation(out=tt[u % 4], in_=posbc[:, tcs(tc)], func=AF.Abs,
                                                        bias=negpk[:, u:u + 1], scale=1.0),
                          reads=["posbc", "negpk"], writes=[f"tt{u % 4}"])

                def stt(u):
                    tm = tmpf[u % 3]
                    for c in range(2):
                        cx.op("dve", lambda e: e.scalar_tensor_tensor(out=tm[:, c * 512:(c + 1) * 512], in0=tt[u % 4], scalar=-slope,
                                                                     in1=bank((u % 2) * 2 + c), op0=ALU.mult, op1=ALU.add),
                              reads=[f"tt{u % 4}", f"S{u % 2}"], writes=[f"tm{u % 3}"])

                def exa(u):
                    cx.op("act", lambda e: e.activation(out=Pt[u % 3], in_=tmpf[u % 3], func=AF.Exp,
                                                        bias=negbh[:, h * 4 + tc:h * 4 + tc + 1], scale=1.0),
                          reads=[f"tm{u % 3}", "negbh"], writes=[f"P{u % 3}"])

                def pv(u):
                    for c in range(2):
                        cx.op("pe", lambda e: e.matmul(bank(4 + c), Vt[b][:, u, :], Pt[u % 3][:, c * 512:(c + 1) * 512],
                                                       start=(u == 0), stop=(u == 63)),
                              reads=[f"Vt{b}", f"P{u % 3}"], writes=[btok(4 + c)], inc=False)
                    for c in range(2):
                        cx.op("pe", lambda e: e.matmul(bank(6 + c), ones_bf[:, :], Pt[u % 3][:, c * 512:(c + 1) * 512],
                                                       start=(u == 0), stop=(u == 63)),
                              reads=["ones", f"P{u % 3}"], writes=[btok(6 + c)], inc=(c == 1))

                qk(0)
                bias(0)
                qk(1)
                bias(1)
                bias(2)
                for u in range(NU):
                    stt(u)
                    if u + 2 < NU:
                        qk(u + 2)
                    if u + 3 < NU:
                        bias(u + 3)
                    exa(u)
                    pv(u)
                for c in range(2):
                    cx.op("act", lambda e: e.activation(out=rinv[c], in_=bank(6 + c), func=AF.Ln), reads=[btok(6 + c)],
                          writes=[f"rinv{c}"])
                    cx.op("act", lambda e: e.activation(out=rinv[c], in_=rinv[c], func=AF.Exp, scale=-1.0), reads=[f"rinv{c}"],
                          writes=[f"rinv{c}"])
                    cx.op("dve", lambda e: e.tensor_tensor(out=osb[c], in0=bank(4 + c), in1=rinv[c], op=ALU.mult),
                          reads=[btok(4 + c), f"rinv{c}"], writes=[f"osb{c}"])
                cx.op("dve", lambda e: e.scalar_tensor_tensor(out=osb[0], in0=osb[1], scalar=neglam[:, j:j + 1], in1=osb[0],
                                                             op0=ALU.mult, op1=ALU.add),
                      reads=["osb0", "osb1", "neglam"], writes=["osb0"])
                cx.op("act", lambda e: e.activation(out=sqo, in_=osb[0], func=AF.Square), reads=["osb0"], writes=["sqo"])
                cx.op("pe", lambda e: e.matmul(bank(6), ones_bf[:, :], sqo, start=True, stop=True), reads=["sqo", "ones"],
                      writes=[btok(6)])
                cx.op("act", lambda e: e.activation(out=rr, in_=bank(6), func=AF.Ln, bias=epst[:, 0:1], scale=1.0 / 128),
                      reads=[btok(6), "eps"], writes=["rr"])
                cx.op("act", lambda e: e.activation(out=rr, in_=rr, func=AF.Exp, scale=-0.5), reads=["rr"], writes=["rr"])
                cx.op("dve", lambda e: e.scalar_tensor_tensor(out=osb[1], in0=osb[0], scalar=hgs[:, j:j + 1], in1=rr,
                                                             op0=ALU.mult, op1=ALU.mult),
                      reads=["osb0", "rr", "hgs"], writes=["osb1"])
                cx.op("dve", lambda e: e.tensor_tensor(out=gog[:, h, tcs(tc)], in0=osb[1], in1=gog[:, h, tcs(tc)], op=ALU.mult),
                      reads=["osb1", f"gog{h}"], writes=[f"gog{h}"])

    def out_proj(l, w_o):
        j = l // 2
        for k in range(8):
            cx.dma("sp", xT[:, k, :], xs[k * 128:(k + 1) * 128, :], reads=["xs"], writes=[f"xT{k}"])
        for pc in range(2):
            load_w(v3(wb[pc], 8), w_o[j][:, pc * 512:(pc + 1) * 512].rearrange("(k p) c -> p k c", p=128), f"wb{pc}")
        for pc in range(2):
            w = v3(wb[pc], 8)
            wtok = f"wb{pc}"
            for jj in range(4):
                jc = pc * 4 + jj
                for tc in range(4):
                    bk = proj(128, [w[:, k, jj * 128:(jj + 1) * 128] for k in range(8)], [gog[:, k, tcs(tc)] for k in range(8)],
                              [wtok] + [f"gog{k}" for k in range(8)])
                    cx.op("dve", lambda e: e.scalar_tensor_tensor(out=xT[:, jc, tcs(tc)], in0=bank(bk),
                                                                 scalar=mods[:, l * 24 + 16 + jc:l * 24 + 17 + jc],
                                                                 in1=xT[:, jc, tcs(tc)], op0=ALU.mult, op1=ALU.add),
                          reads=[btok(bk), "mods", f"xT{jc}"], writes=[f"xT{jc}"])

    def final_norm():
        for tc in range(4):
            cx.op("act", lambda e: e.activation(out=sqb[:, :, :], in_=xT[:, :, tcs(tc)], func=AF.Square),
                  reads=[f"xT{k}" for k in range(8)], writes=["sqb"])
            norm_stats(tc, 1024, [sqb[:, k, :] for k in range(8)], ["sqb"] * 8, rstd, 0)
            for k in range(8):
                tb = t1[k % 2]
                cx.op("dve", lambda e: e.scalar_tensor_tensor(out=tb, in0=xT[:, k, tcs(tc)], scalar=fgl[:, k:k + 1], in1=rstd,
                                                             op0=ALU.mult, op1=ALU.mult),
                      reads=[f"xT{k}", "rstd", "fgl"], writes=[f"t1{k % 2}"])
                cx.dma("sp", y_out[k * 128:(k + 1) * 128, tcs(tc)], tb, reads=[f"t1{k % 2}"], writes=["yout"])

    def main_body():
        ckpt(1)
        for li, l in enumerate(layers):
            if l % 2 == 0:
                p_phase_mla(l)
                cx.barrier()
                ckpt(2)
                a_prologue_only = int(os.environ.get("KSTOP", "99")) == 3
                if a_prologue_only:
                    a_prologue(l)
                    ckpt(3)
                a_phase_mla(l)
                cx.barrier()
                ckpt(4)
                out_proj(l, m_wo)
            else:
                p_phase_diff(l)
                cx.barrier()
                ckpt(2)
                a_phase_diff(l)
                cx.barrier()
                ckpt(4)
                out_proj(l, d_wo)
            if li + 1 < len(layers):
                cx.new_epoch()
        final_norm()

    try:
        main_body()
    except StopBuild:
        pass
    cx.barrier()
    blk.__exit__(None, None, None)
    return nc, cx


def host_inputs(inp):
    f32 = np.float32
    x = np.asarray(inp["x"], f32)
    c = np.asarray(inp["c"], f32)
    pos = np.asarray(inp["positions"], np.int32)

    def lay(v, n):
        return np.ascontiguousarray(np.asarray(v, f32).reshape(n, 128).T)

    shared = {}
    shared["ada_w"] = np.ascontiguousarray(np.asarray(inp["ada_w"], f32))
    shared["ada_b_lay"] = np.stack([lay(inp["ada_b"][l], 24) for l in range(4)])
    shared["norm_g_lay"] = np.stack([lay(inp["norm_g"][l], 8) for l in range(4)])
    shared["final_g_lay"] = lay(inp["final_g"], 8)
    w_in = np.asarray(inp["mla_w_in"], f32)
    shared["m_w_in"] = np.ascontiguousarray(w_in)
    rot_cols = list(range(656, 672)) + list(range(640, 656))
    shared["m_w_rot"] = np.ascontiguousarray(w_in[:, :, rot_cols])
    shared["m_gq"] = np.stack([lay(inp["mla_q_norm_g"][j], 3) for j in range(2)])
    shared["m_gkv"] = np.stack([lay(inp["mla_kv_norm_g"][j], 2) for j in range(2)])
    wq = np.asarray(inp["mla_w_q_up"], f32).reshape(2, 384, 16, 96)
    shared["m_wq"] = np.ascontiguousarray(wq.reshape(2, 384, 16 * 96))
    wq_rot = np.concatenate([wq[..., 80:96], wq[..., 64:80]], axis=-1)
    shared["m_wqr"] = np.ascontiguousarray(wq_rot.reshape(2, 384, 16 * 32))
    wkv = np.asarray(inp["mla_w_kv_up"], f32).reshape(2, 256, 16, 128)
    shared["m_wk"] = np.ascontiguousarray(wkv[..., 0:64].reshape(2, 256, 1024))
    shared["m_wv"] = np.ascontiguousarray(wkv[..., 64:128].reshape(2, 256, 1024))
    shared["m_wo"] = np.ascontiguousarray(np.asarray(inp["mla_w_o"], f32))
    shared["d_w_in"] = np.ascontiguousarray(np.asarray(inp["diff_w_in"], f32))
    shared["d_wo"] = np.ascontiguousarray(np.asarray(inp["diff_w_o"], f32))
    shared["d_hg"] = np.ascontiguousarray(np.asarray(inp["diff_head_g"], f32).reshape(2, 128).T)
    lq = np.stack([np.stack([np.broadcast_to(np.asarray(inp[n], f32)[j][None, :], (128, 64))
                             for n in ("diff_lq1", "diff_lk1", "diff_lq2", "diff_lk2")]) for j in range(2)])
    shared["d_lq"] = np.ascontiguousarray(lq)
    ind2 = np.zeros((128, 8, 16), f32)
    for hp in range(8):
        ind2[0:64, hp, 2 * hp] = 1.0
        ind2[64:128, hp, 2 * hp + 1] = 1.0
    shared["k_ind2"] = ind2.reshape(128, 128)
    indh = np.zeros((128, 16, 16), f32)
    for h in range(16):
        indh[:, h, h] = 1.0
    shared["k_indh"] = indh.reshape(128, 256)
    inv = (10000.0 ** (-np.arange(0, 32, 2, dtype=np.float32) / 32)).astype(f32)
    p = np.arange(128)
    shared["k_invf"] = inv[p % 16].reshape(128, 1).astype(f32)
    shared["k_sgn"] = np.where((p % 32) < 16, -1.0, 1.0).reshape(128, 1).astype(f32)

    maps = []
    for core in range(NCORE):
        b, r = core // 4, core % 4
        m = dict(shared)
        m["xT_in"] = np.ascontiguousarray(x[b, r * T:(r + 1) * T, :].T)
        m["c_lay"] = lay(c[b], 8)
        m["pos_own"] = np.ascontiguousarray(np.broadcast_to(pos[b, r * T:(r + 1) * T][None, :], (128, T))).astype(np.int32)
        m["pos_all"] = np.ascontiguousarray(pos[b].reshape(64, 128).T).astype(np.int32)
        m["pos_c0"] = np.full((128, 1), pos[b, 0], np.int32)
        maps.append(m)
    return maps


_CACHE = {}


def kernel(**inputs):
    key = tuple(LAYERS)
    if key not in _CACHE:
        _CACHE[key] = build_program(LAYERS)
    nc, cx = _CACHE[key]
    maps = host_inputs(inputs)
    res = run_bass_kernel_spmd(nc, maps, core_ids=list(range(NCORE)))
    out = np.empty((2, S, D), np.float32)
    for core in range(NCORE):
        b, r = core // 4, core % 4
        out[b, r * T:(r + 1) * T, :] = res.results[core]["yT_out"].T
    return out
```

```python
import math
import numpy as np
import concourse.bass as bass
import concourse.mybir as mybir
from concourse.bass_utils import run_bass_kernel_spmd

F32 = mybir.dt.float32
BF16 = mybir.dt.bfloat16
I32 = mybir.dt.int32
AF = mybir.ActivationFunctionType
ALU = mybir.AluOpType
AX = mybir.AxisListType

D = 1024
S = 8192
T = 2048
NCORE = 8
EPS = 1e-6
MLA_IN = 1696
LAYERS = [0, 1, 2, 3]
GROUPS = [[0, 1, 2, 3], [4, 5, 6, 7]]


import os


class StopBuild(Exception):
    pass


def ckpt(n):
    if int(os.environ.get("KSTOP", "99")) == n:
        raise StopBuild()


class Ctx:
    def __init__(self, nc):
        self.nc = nc
        self.E = {"pe": nc.tensor, "act": nc.scalar, "dve": nc.vector, "pool": nc.gpsimd, "sp": nc.sync}
        self.sems = {}
        self.cnt = {}
        self.cur = {}
        self.epoch = 0
        for e in ("pe", "act", "dve", "pool"):
            self._new_csem(e)
        self.NR = 8
        self.ring = {}
        for q in ("sp", "pool", "act"):
            self.ring[q] = 0
            for i in range(self.NR):
                k = f"d_{q}_{i}"
                self.sems[k] = nc.alloc_semaphore(k)
                self.cnt[k] = 0
        self.ccn = 0
        for i in range(12):
            k = f"cc_{i}"
            self.sems[k] = nc.alloc_semaphore(k)
            self.cnt[k] = 0
        self.seen = {e: {} for e in self.E}
        self.lastw = {}
        self.readers = {}
        self.ninst = 0

    def _new_csem(self, e):
        k = f"c_{e}_{self.epoch}"
        self.sems[k] = self.nc.alloc_semaphore(k)
        self.cnt[k] = 0
        self.cur[e] = k

    def new_epoch(self):
        self.barrier()
        self.epoch += 1
        for e in ("pe", "act", "dve", "pool"):
            self._new_csem(e)

    def _need(self, eng, ev):
        if ev is None:
            return
        key, val = ev
        if val <= 0 or self.seen[eng].get(key, 0) >= val:
            return
        self.E[eng].wait_ge(self.sems[key], val)
        self.seen[eng][key] = val
        self.ninst += 1

    def _deps(self, eng, reads, writes):
        own = self.cur.get(eng)
        for t in reads:
            ev = self.lastw.get(t)
            if ev is not None and not (eng == "pe" and ev[0] == own):
                self._need(eng, ev)
        for t in writes:
            ev = self.lastw.get(t)
            if ev is not None and not (eng == "pe" and ev[0] == own):
                self._need(eng, ev)
            for k, v in self.readers.get(t, {}).items():
                if not (eng == "pe" and k == own):
                    self._need(eng, (k, v))

    def _record(self, ev, reads, writes):
        for t in writes:
            self.lastw[t] = ev
            self.readers[t] = {}
        for t in reads:
            d = self.readers.setdefault(t, {})
            if d.get(ev[0], 0) < ev[1]:
                d[ev[0]] = ev[1]

    def op(self, eng, fn, reads=(), writes=(), inc=True):
        excl = [t for t in reads if t.startswith("bank") or t in ("S0", "S1", "S2")]
        if excl:
            writes = list(writes) + excl
        self._deps(eng, reads, writes)
        ins = fn(self.E[eng])
        self.ninst += 1
        key = self.cur[eng]
        if inc:
            self.cnt[key] += 1
            ins.then_inc(self.sems[key], 1)
            ev = (key, self.cnt[key])
        else:
            ev = (key, self.cnt[key] + 1)
        self._record(ev, reads, writes)
        return ev

    def dma(self, q, out, in_, reads=(), writes=()):
        self._deps(q, reads, writes)
        i = self.ring[q]
        self.ring[q] = (i + 1) % self.NR
        key = f"d_{q}_{i}"
        self._need(q, (key, self.cnt[key]))
        ins = self.E[q].dma_start(out=out, in_=in_)
        self.ninst += 1
        self.cnt[key] += 16
        ins.then_inc(self.sems[key], 16)
        ev = (key, self.cnt[key])
        self._record(ev, reads, writes)
        return ev

    def collective(self, in_ap, out_ap, reads=(), writes=()):
        q = "pool"
        if os.environ.get("KNOCC"):
            return None
        self._deps(q, reads, writes)
        key = f"cc_{self.ccn % 12}"
        self.ccn += 1
        self._need(q, (key, self.cnt[key]))
        ins = self.E[q].collective_compute("AllGather", ALU.bypass, replica_groups=GROUPS,
                                          ins=[in_ap], outs=[out_ap])
        self.ninst += 1
        self.cnt[key] += 1
        ins.then_inc(self.sems[key], 1)
        ev = (key, self.cnt[key])
        self._record(ev, reads, writes)
        return ev

    def barrier(self):
        for eng in self.E:
            for key, c in self.cnt.items():
                if c > 0 and not key.startswith("cc_"):
                    self._need(eng, (key, c))
        keep = {t: ev for t, ev in self.lastw.items() if ev[0].startswith("cc_")}
        self.lastw.clear()
        self.readers.clear()
        self.lastw.update(keep)


def build_program(layers):
    nc = bass.Bass("TRN2", target_bir_lowering=False)
    dt = nc.dram_tensor

    x_in = dt("xT_in", [D, T], F32, kind="ExternalInput")
    y_out = dt("yT_out", [D, T], F32, kind="ExternalOutput")
    c_lay = dt("c_lay", [128, 8], F32, kind="ExternalInput")
    pos_own = dt("pos_own", [128, T], I32, kind="ExternalInput")
    pos_all = dt("pos_all", [128, 64], I32, kind="ExternalInput")
    pos_c0 = dt("pos_c0", [128, 1], I32, kind="ExternalInput")
    ada_w = dt("ada_w", [4, D, 3 * D], F32, kind="ExternalInput")
    ada_b = dt("ada_b_lay", [4, 128, 24], F32, kind="ExternalInput")
    norm_g = dt("norm_g_lay", [4, 128, 8], F32, kind="ExternalInput")
    final_g = dt("final_g_lay", [128, 8], F32, kind="ExternalInput")
    m_w_in = dt("m_w_in", [2, D, MLA_IN], F32, kind="ExternalInput")
    m_w_rot = dt("m_w_rot", [2, D, 32], F32, kind="ExternalInput")
    m_gq = dt("m_gq", [2, 128, 3], F32, kind="ExternalInput")
    m_gkv = dt("m_gkv", [2, 128, 2], F32, kind="ExternalInput")
    m_wq = dt("m_wq", [2, 384, 16 * 96], F32, kind="ExternalInput")
    m_wqr = dt("m_wqr", [2, 384, 16 * 32], F32, kind="ExternalInput")
    m_wk = dt("m_wk", [2, 256, 1024], F32, kind="ExternalInput")
    m_wv = dt("m_wv", [2, 256, 1024], F32, kind="ExternalInput")
    m_wo = dt("m_wo", [2, D, D], F32, kind="ExternalInput")
    d_w_in = dt("d_w_in", [2, D, 4096], F32, kind="ExternalInput")
    d_wo = dt("d_wo", [2, D, D], F32, kind="ExternalInput")
    d_hg = dt("d_hg", [128, 2], F32, kind="ExternalInput")
    d_lq = dt("d_lq", [2, 4, 128, 64], F32, kind="ExternalInput")
    k_ind2 = dt("k_ind2", [128, 8 * 16], F32, kind="ExternalInput")
    k_indh = dt("k_indh", [128, 16 * 16], F32, kind="ExternalInput")
    k_invf = dt("k_invf", [128, 1], F32, kind="ExternalInput")
    k_sgn = dt("k_sgn", [128, 1], F32, kind="ExternalInput")

    xs = dt("xs", [D, T], F32)
    cosd = dt("cosd", [128, T], F32)
    sind = dt("sind", [128, T], F32)
    posd = dt("posd", [128, T], F32)
    nbd = dt("nbd", [4, 64], F32)
    qs = [dt(f"qs{l}", [16, 128, T], BF16) for l in range(4)]
    kown = [[dt(f"kown{l}_{g}", [256, T], BF16) for g in range(4)] for l in range(4)]
    kall = [[dt(f"kall{l}_{g}", [1024, T], BF16) for g in range(4)] for l in range(4)]
    vown = [[dt(f"vown{l}_{g}", [T, 256], BF16) for g in range(4)] for l in range(4)]
    vall = [[dt(f"vall{l}_{g}", [4 * T, 256], BF16) for g in range(4)] for l in range(4)]
    smown = [dt(f"smown{l}", [32, T], BF16) for l in range(4)]
    small = [dt(f"small{l}", [128, T], BF16) for l in range(4)]
    stown = [dt(f"stown{l}", [16, 8], F32) for l in range(4)]
    stall = [dt(f"stall{l}", [64, 8], F32) for l in range(4)]

    sb = nc.alloc_sbuf_tensor if hasattr(nc, "alloc_sbuf_tensor") else None

    def sbt(name, shape, dtype):
        cm = nc.sbuf_tensor(name, shape, dtype)
        return cm.__enter__()

    def pst(name, shape, dtype):
        cm = nc.psum_tensor(name, shape, dtype)
        return cm.__enter__()

    ARENA_N = 200 * 1024 // 2
    arena = sbt("arena", [128, ARENA_N], BF16)

    def carve(off_bytes, nelem, dtype):
        a = arena[:, off_bytes // 2: off_bytes // 2 + (nelem * (2 if dtype == BF16 else 4)) // 2]
        if dtype != BF16:
            a = a.bitcast(dtype)
        return a

    ones_bf = sbt("ones_bf", [128, 128], BF16)
    ident_unused = None
    cact = sbt("cact", [128, 8], BF16)
    ctmp = sbt("ctmp", [128, 8], F32)
    mods = sbt("mods", [128, 4 * 24], F32)
    adab = sbt("adab", [128, 4 * 24], F32)
    Acoef = sbt("Acoef", [128, 4 * 8], F32)
    gnorm = sbt("gnorm", [128, 4 * 8], F32)
    fgl = sbt("fgl", [128, 8], F32)
    gq = sbt("gq", [128, 2 * 3], F32)
    gkv = sbt("gkv", [128, 2 * 2], F32)
    hgs = sbt("hgs", [128, 2], F32)
    lamt = sbt("lamt", [128, 8], F32)
    lqt = sbt("lqt", [128, 4 * 64], F32)
    lqp = sbt("lqp", [128, 64], F32)
    neglam = sbt("neglam", [128, 2], F32)
    ind2f = sbt("ind2f", [128, 128], F32)
    indhf = sbt("indhf", [128, 256], F32)
    ind2 = sbt("ind2", [128, 128], BF16)
    indh = sbt("indh", [128, 256], BF16)
    invf = sbt("invf", [128, 1], F32)
    sgn = sbt("sgn", [128, 1], F32)
    epst = sbt("epst", [128, 1], F32)
    zerot = sbt("zerot", [128, 1], F32)
    pkrel = sbt("pkrel", [128, 64], F32)
    pki = sbt("pki", [128, 64], I32)
    negpk = sbt("negpk", [128, 64], F32)
    c0i = sbt("c0i", [128, 1], I32)
    c0f = sbt("c0f", [128, 1], F32)
    qmx = sbt("qmx", [16, 8], F32)
    kmx = sbt("kmx", [16, 8], F32)
    qmax = sbt("qmax", [16, 4], F32)
    stl = sbt("stl", [16, 32], F32)
    kmax = sbt("kmax", [16, 1], F32)
    nb16 = sbt("nb16", [16, 4], F32)
    negb = sbt("negb", [128, 64], F32)
    negbh = sbt("negbh", [128, 32], F32)

    PS = [pst(f"ps{i}", [128, 1024], F32) for i in range(4)]

    def bank(i):
        return PS[i // 2][:, (i % 2) * 512:(i % 2) * 512 + 512]

    def btok(i):
        return f"bank{i}"

    blk = nc.Block()
    blk.__enter__()
    cx = Ctx(nc)

    def v3(ap, k):
        return ap.rearrange("p (k t) -> p k t", k=k)

    xT = v3(carve(0, 8 * T, F32), 8)
    gog = v3(carve(65536, 8 * T, BF16), 8)
    hT = v3(carve(98304, 8 * T, BF16), 8)
    wbig = carve(131072, 8192, BF16)
    wb = [carve(131072, 4096, BF16), carve(139264, 4096, BF16)]
    sqb = v3(carve(147456, 8 * 512, BF16), 8)
    t1 = [carve(155648, 512, F32), carve(157696, 512, F32)]
    rstd = carve(159744, 512, F32)
    qn = v3(carve(161792, 3 * T, BF16), 3)
    kvn = v3(carve(174080, 2 * T, BF16), 2)
    lat = v3(carve(182272, 5 * 512, F32), 5)
    krope = carve(182272, T, BF16)
    kropesq = carve(182272 + 4096, T, BF16)
    cosb = carve(192512, 512, F32)
    sinb = carve(194560, 512, F32)
    stg = [carve(196608 + i * 1024, 512, BF16) for i in range(4)]
    vst = [carve(200704, 1024, BF16), carve(202752, 1024, BF16)]
    Kt = [carve(0, 8192, BF16), carve(16384, 8192, BF16)]
    Vt = [carve(32768, 8192, BF16).rearrange("p (k c) -> p k c", k=64),
          carve(49152, 8192, BF16).rearrange("p (k c) -> p k c", k=64)]
    qt = [carve(98304 + i * 4096, T, BF16) for i in range(4)]
    Pt = [carve(114688 + i * 2048, 1024, BF16) for i in range(3)]
    tmpf = [carve(120832 + i * 4096, 1024, F32) for i in range(2)] + [carve(154624, 1024, F32)]
    tt = [carve(129024, 512, F32), carve(131072, 512, F32), carve(158720, 512, F32), carve(160768, 512, F32)]
    posbc = carve(133120, T, F32)
    rinv = [carve(141312 + i * 2048, 512, F32) for i in range(2)]
    osb = [carve(145408 + i * 2048, 512, F32) for i in range(2)]
    sqo = carve(149504, 512, BF16)
    rr = carve(150528, 512, F32)
    zer512 = carve(152576, 512, F32)
    dd = [carve(154624 + i * 2048, 512, F32) for i in range(2)]

    def tcs(tc):
        return slice(tc * 512, (tc + 1) * 512)

    cx.op("dve", lambda e: e.memset(ones_bf[:], 1.0), writes=["ones"])
    cx.op("dve", lambda e: e.memset(epst[:], EPS), writes=["eps"])
    cx.op("dve", lambda e: e.memset(zerot[:], 0.0), writes=["zero"])
    cx.dma("sp", ctmp[:], c_lay[:, :], writes=["ctmp"])
    cx.op("act", lambda e: e.activation(out=cact[:], in_=ctmp[:], func=AF.Silu), reads=["ctmp"], writes=["cact"])
    cx.dma("sp", adab[:].rearrange("p (l j) -> p l j", l=4), ada_b.ap().rearrange("l p j -> p l j"), writes=["adab"])
    cx.dma("sp", gnorm[:].rearrange("p (l j) -> p l j", l=4), norm_g.ap().rearrange("l p j -> p l j"), writes=["gnorm"])
    cx.dma("sp", fgl[:], final_g[:, :], writes=["fgl"])
    cx.dma("sp", gq[:].rearrange("p (l j) -> p l j", l=2), m_gq.ap().rearrange("l p j -> p l j"), writes=["gq"])
    cx.dma("sp", gkv[:].rearrange("p (l j) -> p l j", l=2), m_gkv.ap().rearrange("l p j -> p l j"), writes=["gkv"])
    cx.dma("sp", hgs[:], d_hg[:, :], writes=["hgs"])
    cx.dma("sp", ind2f[:], k_ind2[:, :], writes=["ind2f"])
    cx.dma("sp", indhf[:], k_indh[:, :], writes=["indhf"])
    cx.op("dve", lambda e: e.tensor_copy(out=ind2[:], in_=ind2f[:]), reads=["ind2f"], writes=["ind2"])
    cx.op("dve", lambda e: e.tensor_copy(out=indh[:], in_=indhf[:]), reads=["indhf"], writes=["indh"])
    cx.dma("sp", invf[:], k_invf[:, :], writes=["invf"])
    cx.dma("sp", sgn[:], k_sgn[:, :], writes=["sgn"])
    cx.dma("sp", pki[:], pos_all[:, :], writes=["pki"])
    cx.dma("sp", c0i[:], pos_c0[:, :], writes=["c0i"])
    cx.op("dve", lambda e: e.tensor_copy(out=c0f[:], in_=c0i[:]), reads=["c0i"], writes=["c0f"])
    cx.op("dve", lambda e: e.tensor_copy(out=pkrel[:], in_=pki[:]), reads=["pki"], writes=["pkrel"])
    cx.op("dve", lambda e: e.tensor_scalar(out=pkrel[:], in0=pkrel[:], scalar1=c0f[:, 0:1], scalar2=None,
                                           op0=ALU.subtract), reads=["pkrel", "c0f"], writes=["pkrel"])
    cx.op("dve", lambda e: e.tensor_scalar(out=negpk[:], in0=pkrel[:], scalar1=-1.0, scalar2=None, op0=ALU.mult),
          reads=["pkrel"], writes=["negpk"])
    for j in range(2):
        l = 2 * j + 1
        lam_init = 0.8 - 0.6 * math.exp(-0.3 * l)
        cx.dma("sp", lqt[:].rearrange("p (a d) -> p a d", a=4), d_lq[j].rearrange("a p d -> p a d"), writes=["lqt"])
        for a in range(2):
            cx.op("dve", lambda e: e.tensor_tensor(out=lqp[:], in0=lqt[:, (2 * a) * 64:(2 * a + 1) * 64],
                                                   in1=lqt[:, (2 * a + 1) * 64:(2 * a + 2) * 64], op=ALU.mult),
                  reads=["lqt"], writes=["lqp"])
            cx.op("dve", lambda e: e.tensor_reduce(out=lamt[:, a:a + 1], in_=lqp[:], axis=AX.X, op=ALU.add),
                  reads=["lqp"], writes=["lamt"])
        cx.op("act", lambda e: e.activation(out=lamt[:, 2:4], in_=lamt[:, 0:2], func=AF.Exp), reads=["lamt"], writes=["lamt"])
        cx.op("dve", lambda e: e.tensor_tensor(out=lamt[:, 4:5], in0=lamt[:, 3:4], in1=lamt[:, 2:3], op=ALU.subtract),
              reads=["lamt"], writes=["lamt"])
        cx.op("dve", lambda e: e.tensor_scalar(out=neglam[:, j:j + 1], in0=lamt[:, 4:5], scalar1=-lam_init, scalar2=None,
                                               op0=ALU.add), reads=["lamt"], writes=["neglam"])
        cx.op("dve", lambda e: e.tensor_scalar(out=hgs[:, j:j + 1], in0=hgs[:, j:j + 1], scalar1=(1.0 - lam_init),
                                               scalar2=None, op0=ALU.mult), reads=["hgs"], writes=["hgs"])

    pi = 0
    for l in layers:
        for p6 in range(6):
            w = v3(wb[pi % 2], 8)
            tok = f"wb{pi % 2}"
            cx.dma("pool", w, ada_w[l][:, p6 * 512:(p6 + 1) * 512].rearrange("(k p) c -> p k c", p=128), writes=[tok])
            for jj in range(4):
                j = p6 * 4 + jj
                for k in range(8):
                    cx.op("pe", lambda e: e.matmul(bank(0)[:, j:j + 1], w[:, k, jj * 128:(jj + 1) * 128], cact[:, k:k + 1],
                                                   start=(k == 0), stop=(k == 7)),
                          reads=[tok, "cact"], writes=[btok(0)], inc=(k == 7))
            pi += 1
        cx.op("dve", lambda e: e.tensor_tensor(out=mods[:, l * 24:(l + 1) * 24], in0=bank(0)[:, 0:24],
                                               in1=adab[:, l * 24:(l + 1) * 24], op=ALU.add),
              reads=[btok(0), "adab"], writes=["mods"])
        cx.op("dve", lambda e: e.scalar_tensor_tensor(out=Acoef[:, l * 8:(l + 1) * 8], in0=mods[:, l * 24 + 8:l * 24 + 16],
                                                     scalar=1.0, in1=gnorm[:, l * 8:(l + 1) * 8], op0=ALU.add, op1=ALU.mult),
              reads=["mods", "gnorm"], writes=["Acoef"])

    for k in range(8):
        cx.dma("sp", xT[:, k, :], x_in[k * 128:(k + 1) * 128, :], writes=[f"xT{k}"])

    TWO_PI = 2.0 * math.pi
    MAGIC = 12582912.0
    pint = carve(98304, 512, I32)
    pf = carve(98304 + 2048, 512, F32)
    ya = carve(98304 + 4096, 512, F32)
    yb = carve(98304 + 6144, 512, F32)
    yc = carve(98304 + 8192, 512, F32)
    for tc in range(4):
        cx.dma("sp", pint, pos_own[:, tcs(tc)], writes=["pint"])
        cx.op("dve", lambda e: e.tensor_copy(out=pf, in_=pint), reads=["pint"], writes=["pf"])
        cx.op("dve", lambda e: e.tensor_scalar(out=ya, in0=pf, scalar1=c0f[:, 0:1], scalar2=None, op0=ALU.subtract),
              reads=["pf", "c0f"], writes=["ya"])
        cx.dma("sp", posd[:, tcs(tc)], ya, reads=["ya"], writes=["posd"])
        for which in range(2):
            off = 0.0 if which == 0 else 0.25
            cx.op("dve", lambda e: e.tensor_scalar(out=ya, in0=pf, scalar1=invf[:, 0:1], scalar2=1.0 / TWO_PI,
                                                   op0=ALU.mult, op1=ALU.mult), reads=["pf", "invf"], writes=["ya"])
            if which == 1:
                cx.op("dve", lambda e: e.tensor_scalar(out=ya, in0=ya, scalar1=off, scalar2=None, op0=ALU.add),
                      reads=["ya"], writes=["ya"])
            cx.op("dve", lambda e: e.tensor_scalar(out=yb, in0=ya, scalar1=MAGIC, scalar2=None, op0=ALU.add),
                  reads=["ya"], writes=["yb"])
            cx.op("dve", lambda e: e.tensor_scalar(out=yb, in0=yb, scalar1=MAGIC, scalar2=None, op0=ALU.subtract),
                  reads=["yb"], writes=["yb"])
            cx.op("dve", lambda e: e.tensor_tensor(out=yc, in0=ya, in1=yb, op=ALU.subtract), reads=["ya", "yb"], writes=["yc"])
            cx.op("dve", lambda e: e.tensor_scalar(out=yc, in0=yc, scalar1=0.49999, scalar2=-0.49999, op0=ALU.min, op1=ALU.max),
                  reads=["yc"], writes=["yc"])
            cx.op("act", lambda e: e.activation(out=yb, in_=yc, func=AF.Sin, scale=TWO_PI), reads=["yc"], writes=["yb"])
            if which == 0:
                cx.op("dve", lambda e: e.tensor_scalar(out=yb, in0=yb, scalar1=sgn[:, 0:1], scalar2=None, op0=ALU.mult),
                      reads=["yb", "sgn"], writes=["yb"])
                cx.dma("sp", sind[:, tcs(tc)], yb, reads=["yb"], writes=["sind"])
            else:
                cx.dma("sp", cosd[:, tcs(tc)], yb, reads=["yb"], writes=["cosd"])
    cx.barrier()

    def norm_stats(tc, nfeat, src_chunks, src_tokens, dst_rstd, bk):
        n = len(src_chunks)
        for i, (ap, tok) in enumerate(zip(src_chunks, src_tokens)):
            cx.op("pe", lambda e: e.matmul(bank(bk), ones_bf[:, :], ap, start=(i == 0), stop=(i == n - 1)),
                  reads=[tok, "ones"], writes=[btok(bk)], inc=(i == n - 1))
        cx.op("act", lambda e: e.activation(out=dst_rstd, in_=bank(bk), func=AF.Ln, bias=epst[:, 0:1], scale=1.0 / nfeat),
              reads=[btok(bk), "eps"], writes=["rstd"])
        cx.op("act", lambda e: e.activation(out=dst_rstd, in_=dst_rstd, func=AF.Exp, scale=-0.5), reads=["rstd"], writes=["rstd"])

    def make_hT(l):
        for tc in range(4):
            cx.op("act", lambda e: e.activation(out=sqb[:, :, :], in_=xT[:, :, tcs(tc)], func=AF.Square),
                  reads=[f"xT{k}" for k in range(8)], writes=["sqb"])
            norm_stats(tc, 1024, [sqb[:, k, :] for k in range(8)], ["sqb"] * 8, rstd, 0)
            for k in range(8):
                tb = t1[k % 2]
                cx.op("dve", lambda e: e.tensor_tensor(out=tb, in0=xT[:, k, tcs(tc)], in1=rstd, op=ALU.mult),
                      reads=[f"xT{k}", "rstd"], writes=[f"t1{k % 2}"])
                cx.op("act", lambda e: e.activation(out=hT[:, k, tcs(tc)], in_=tb, func=AF.Identity,
                                                    bias=mods[:, l * 24 + k:l * 24 + k + 1],
                                                    scale=Acoef[:, l * 8 + k:l * 8 + k + 1]),
                      reads=[f"t1{k % 2}", "mods", "Acoef"], writes=[f"hT{tc}"])

    def spill_x():
        for k in range(8):
            cx.dma("sp", xs[k * 128:(k + 1) * 128, :], xT[:, k, :], reads=[f"xT{k}"], writes=["xs"])

    def load_w(dst, src_ap, tok):
        cx.dma("pool", dst, src_ap, writes=[tok])

    bkc = [0]

    def nextbank():
        bkc[0] = (bkc[0] + 1) % 6
        return bkc[0] + 2

    def proj(M, lhs_list, rhs_list, reads):
        bk = nextbank()
        n = len(lhs_list)
        for i in range(n):
            cx.op("pe", lambda e: e.matmul(bank(bk)[0:M, :], lhs_list[i], rhs_list[i], start=(i == 0), stop=(i == n - 1)),
                  reads=reads, writes=[btok(bk)], inc=(i == n - 1))
        return bk

    def p_phase_mla(l):
        j = l // 2
        make_hT(l)
        spill_x()
        scale = 96.0 ** -0.5
        ckpt(21)
        wl = wbig[:, 0:8 * 640].rearrange("p (k c) -> p k c", k=8)
        load_w(wl, m_w_in[j][:, 0:640].rearrange("(k p) c -> p k c", p=128), "wbig")
        for tc in range(4):
            for ci in range(5):
                bk = proj(128, [wl[:, k, ci * 128:(ci + 1) * 128] for k in range(8)], [hT[:, k, tcs(tc)] for k in range(8)],
                          ["wbig", f"hT{tc}"])
                cx.op("dve", lambda e: e.tensor_copy(out=lat[:, ci, :], in_=bank(bk)), reads=[btok(bk)], writes=[f"lat{ci}"])
                cx.op("act", lambda e: e.activation(out=sqb[:, ci, :], in_=bank(bk), func=AF.Square),
                      reads=[btok(bk)], writes=[f"sqb{ci}"])
            norm_stats(tc, 384, [sqb[:, ci, :] for ci in range(3)], [f"sqb{ci}" for ci in range(3)], rstd, 0)
            for ci in range(3):
                cx.op("dve", lambda e: e.scalar_tensor_tensor(out=qn[:, ci, tcs(tc)], in0=lat[:, ci, :],
                                                             scalar=gq[:, j * 3 + ci:j * 3 + ci + 1], in1=rstd,
                                                             op0=ALU.mult, op1=ALU.mult),
                      reads=[f"lat{ci}", "rstd", "gq"], writes=["qn"])
            norm_stats(tc, 256, [sqb[:, 3 + ci, :] for ci in range(2)], [f"sqb{3 + ci}" for ci in range(2)], rstd, 1)
            for ci in range(2):
                cx.op("dve", lambda e: e.scalar_tensor_tensor(out=kvn[:, ci, tcs(tc)], in0=lat[:, 3 + ci, :],
                                                             scalar=gkv[:, j * 2 + ci:j * 2 + ci + 1], in1=rstd,
                                                             op0=ALU.mult, op1=ALU.mult),
                      reads=[f"lat{3 + ci}", "rstd", "gkv"], writes=["kvn"])
        ckpt(22)
        wr = wbig[:, 0:8 * 64].rearrange("p (k c) -> p k c", k=8)
        cx.dma("pool", wr[:, :, 0:32], m_w_in[j][:, 640:672].rearrange("(k p) c -> p k c", p=128), writes=["wbig"])
        cx.dma("pool", wr[:, :, 32:64], m_w_rot[j].rearrange("(k p) c -> p k c", p=128), writes=["wbig2"],
               reads=["wbig"])
        for tc in range(4):
            cx.dma("sp", cosb, cosd[:, tcs(tc)], writes=["cosb"])
            cx.dma("sp", sinb, sind[:, tcs(tc)], writes=["sinb"])
            b1 = proj(32, [wr[:, k, 0:32] for k in range(8)], [hT[:, k, tcs(tc)] for k in range(8)], ["wbig", "wbig2", f"hT{tc}"])
            b2 = proj(32, [wr[:, k, 32:64] for k in range(8)], [hT[:, k, tcs(tc)] for k in range(8)], ["wbig", "wbig2", f"hT{tc}"])
            cx.op("dve", lambda e: e.tensor_tensor(out=t1[0][0:32, :], in0=bank(b1)[0:32, :], in1=cosb[0:32, :], op=ALU.mult),
                  reads=[btok(b1), "cosb"], writes=["t10"])
            cx.op("dve", lambda e: e.tensor_tensor(out=t1[1][0:32, :], in0=bank(b2)[0:32, :], in1=sinb[0:32, :], op=ALU.mult),
                  reads=[btok(b2), "sinb"], writes=["t11"])
            cx.op("pool", lambda e: e.tensor_tensor(out=krope[0:32, tcs(tc)], in0=t1[0][0:32, :], in1=t1[1][0:32, :], op=ALU.add),
                  reads=["t10", "t11"], writes=["krope"])
            cx.op("act", lambda e: e.activation(out=kropesq[0:32, tcs(tc)], in_=krope[0:32, tcs(tc)], func=AF.Square),
                  reads=["krope"], writes=["kropesq"])
        cx.dma("sp", smown[l][:, :], krope[0:32, :], reads=["krope"], writes=["smown"])
        si = 0
        cx.op("dve", lambda e: e.memset(qmx[:], 0.0), writes=["qmx"])
        cx.op("dve", lambda e: e.memset(kmx[:], 0.0), writes=["kmx"])
        def st_gate():
            nonlocal si
            for pc in range(2):
                load_w(v3(wb[pc], 8), m_w_in[j][:, 672 + pc * 512:672 + (pc + 1) * 512].rearrange("(k p) c -> p k c", p=128),
                       "wbig" if pc == 0 else "wbig2")
            for pc in range(2):
                w = v3(wb[pc], 8)
                for ci in range(4):
                    for tc in range(4):
                        bk = proj(128, [w[:, k, ci * 128:(ci + 1) * 128] for k in range(8)],
                                  [hT[:, k, tcs(tc)] for k in range(8)], ["wbig", "wbig2", f"hT{tc}"])
                        cx.op("act", lambda e: e.activation(out=gog[:, pc * 4 + ci, tcs(tc)], in_=bank(bk), func=AF.Silu),
                              reads=[btok(bk)], writes=[f"gog{pc * 4 + ci}"])

        def st_qup():
            nonlocal si
            wq = wbig[:, 0:3 * 1536].rearrange("p (k c) -> p k c", k=3)
            wqr = wbig[:, 3 * 1536:3 * 1536 + 3 * 512].rearrange("p (k c) -> p k c", k=3)
            cx.dma("pool", wq, m_wq[j].rearrange("(k p) c -> p k c", p=128), writes=["wbig"], reads=["wbig2"])
            cx.dma("pool", wqr, m_wqr[j].rearrange("(k p) c -> p k c", p=128), writes=["wbig2"], reads=["wbig"])
            cx.collective(stown[l].ap().opt(), stall[l].ap().opt(), reads=["stown"], writes=["stall"])
            cx.collective(smown[l].ap().opt(), small[l].ap().opt(), reads=["smown"], writes=["small"])
            for g in range(4):
                cx.collective(kown[l][g].ap().opt(), kall[l][g].ap().opt(), reads=[f"kown{g}"], writes=[f"kall{g}"])
                cx.collective(vown[l][g].ap().opt(), vall[l][g].ap().opt(), reads=[f"vown{g}"], writes=[f"vall{g}"])
            deferred = []
            stgx = [(stg[i], f"stg{i}") for i in range(4)] + [(vst[0][:, 0:512], "vst0"), (vst[0][:, 512:1024], "vst0"),
                                                               (vst[1][:, 0:512], "vst1"), (vst[1][:, 512:1024], "vst1")]
            for tc in range(4):
                cx.dma("sp", cosb, cosd[:, tcs(tc)], writes=["cosb"])
                cx.dma("sp", sinb, sind[:, tcs(tc)], writes=["sinb"])
                for h in range(16):
                    b1 = proj(96, [wq[:, k, h * 96:(h + 1) * 96] for k in range(3)], [qn[:, k, tcs(tc)] for k in range(3)],
                              ["wbig", "wbig2", "qn"])
                    b2 = proj(32, [wqr[:, k, h * 32:(h + 1) * 32] for k in range(3)], [qn[:, k, tcs(tc)] for k in range(3)],
                              ["wbig", "wbig2", "qn"])
                    while deferred:
                        deferred.pop(0)()
                    st, stok = stgx[si % 8]
                    si += 1
                    cx.op("act", lambda e: e.activation(out=st[0:64, :], in_=bank(b1)[0:64, :], func=AF.Copy, scale=scale),
                          reads=[btok(b1)], writes=[stok])
                    cx.op("dve", lambda e: e.scalar_tensor_tensor(out=t1[0][64:96, :], in0=bank(b1)[64:96, :], scalar=scale,
                                                                 in1=cosb[64:96, :], op0=ALU.mult, op1=ALU.mult),
                          reads=[btok(b1), "cosb"], writes=["t10"])
                    cx.op("dve", lambda e: e.scalar_tensor_tensor(out=t1[1][64:96, :], in0=bank(b2)[0:32, :], scalar=scale,
                                                                 in1=sinb[0:32, :], op0=ALU.mult, op1=ALU.mult),
                          reads=[btok(b2), "sinb"], writes=["t11"])
                    cx.op("dve", lambda e: e.tensor_tensor(out=st[64:96, :], in0=t1[0][64:96, :], in1=t1[1][64:96, :], op=ALU.add),
                          reads=["t10", "t11"], writes=[stok])
                    sq = sqb[:, h % 8, :]
                    cx.op("act", lambda e: e.activation(out=sq[0:96, :], in_=st[0:96, :], func=AF.Square),
                          reads=[stok], writes=[f"sqb{h % 8}"])

                    def ind_mm(h=h, sq=sq):
                        cx.op("pe", lambda e: e.matmul(bank(1)[0:16, :], indh[0:96, h * 16:(h + 1) * 16], sq[0:96, :],
                                                       start=(h == 0), stop=(h == 15)),
                              reads=[f"sqb{h % 8}", "indh"], writes=[btok(1)], inc=(h == 15))
                    deferred.append(ind_mm)
                    cx.dma("sp", qs[l][h][0:96, tcs(tc)], st[0:96, :], reads=[stok], writes=["qs"])
                while deferred:
                    deferred.pop(0)()
                cx.op("dve", lambda e: e.tensor_reduce(out=qmx[:, tc:tc + 1], in_=bank(1)[0:16, :], axis=AX.X, op=ALU.max),
                      reads=[btok(1)], writes=["qmx"])
            cx.op("dve", lambda e: e.tensor_copy(out=qmax[:], in_=qmx[:, 0:4]), reads=["qmx"], writes=["qmax"])

        def st_kv():
            nonlocal si
            wk = wbig[:, 0:2 * 1024].rearrange("p (k c) -> p k c", k=2)
            wv = wbig[:, 2048:2048 + 2 * 1024].rearrange("p (k c) -> p k c", k=2)
            cx.dma("pool", wk, m_wk[j].rearrange("(k p) c -> p k c", p=128), writes=["wbig"], reads=["wbig2"])
            cx.dma("pool", wv, m_wv[j].rearrange("(k p) c -> p k c", p=128), writes=["wbig2"], reads=["wbig"])
            kdef = []
            for tc in range(4):
                for hp in range(8):
                    bk = proj(128, [wk[:, k, hp * 128:(hp + 1) * 128] for k in range(2)], [kvn[:, k, tcs(tc)] for k in range(2)],
                              ["wbig", "wbig2", "kvn"])
                    while kdef:
                        kdef.pop(0)()
                    st = stg[si % 4]
                    stok = f"stg{si % 4}"
                    si += 1
                    cx.op("act", lambda e: e.activation(out=st, in_=bank(bk), func=AF.Copy), reads=[btok(bk)], writes=[stok])
                    sq = sqb[:, hp, :]
                    cx.op("act", lambda e: e.activation(out=sq, in_=bank(bk), func=AF.Square), reads=[btok(bk)],
                          writes=[f"sqb{hp}"])
                    def ind_mm(hp=hp, sq=sq):
                        cx.op("pe", lambda e: e.matmul(bank(1)[0:16, :], ind2[:, hp * 16:(hp + 1) * 16], sq, start=(hp == 0), stop=False),
                              reads=[f"sqb{hp}", "ind2"], writes=[btok(1)], inc=False)
                    kdef.append(ind_mm)
                    g = hp // 2
                    cx.dma("sp", kown[l][g][(hp % 2) * 128:(hp % 2) * 128 + 128, tcs(tc)], st, reads=[stok], writes=[f"kown{g}"])
                while kdef:
                    kdef.pop(0)()
                cx.op("pe", lambda e: e.matmul(bank(1)[0:16, :], ones_bf[0:32, 0:16], kropesq[0:32, tcs(tc)], start=False, stop=True),
                      reads=["kropesq", "ones"], writes=[btok(1)], inc=True)
                cx.op("dve", lambda e: e.tensor_reduce(out=kmx[:, tc:tc + 1], in_=bank(1)[0:16, :], axis=AX.X, op=ALU.max),
                      reads=[btok(1)], writes=["kmx"])
            cx.dma("sp", stown[l][:, :], kmx[:], reads=["kmx"], writes=["stown"])
            for tb in range(16):
                vs = vst[tb % 2]
                vtok = f"vst{tb % 2}"
                for half in range(2):
                    bk = proj(128, [kvn[:, k, tb * 128:(tb + 1) * 128] for k in range(2)],
                              [wv[:, k, half * 512:(half + 1) * 512] for k in range(2)], ["wbig", "wbig2", "kvn"])
                    cx.op("act" if half == 0 else "dve",
                          (lambda e: e.activation(out=vs[:, half * 512:(half + 1) * 512], in_=bank(bk), func=AF.Copy)) if half == 0
                          else (lambda e: e.tensor_copy(out=vs[:, half * 512:(half + 1) * 512], in_=bank(bk))),
                          reads=[btok(bk)], writes=[vtok])
                for g in range(4):
                    cx.dma("sp", vown[l][g][tb * 128:(tb + 1) * 128, :], vs[:, g * 256:(g + 1) * 256], reads=[vtok],
                           writes=[f"vown{g}"])

        st_kv()
        st_gate()
        st_qup()


    def p_phase_diff(l):
        j = l // 2
        make_hT(l)
        spill_x()
        cx.op("dve", lambda e: e.memset(qmx[:], 0.0), writes=["qmx"])
        cx.op("dve", lambda e: e.memset(kmx[:], 0.0), writes=["kmx"])
        for i in range(4):
            cx.op("pool", lambda e: e.memset(stg[i], 0.0), writes=[f"stg{i}"])
        si = 0
        order = [2, 3, 4, 5, 0, 1, 6, 7]

        def issue_load(pidx):
            pcc = order[pidx]
            load_w(v3(wb[pidx % 2], 8), d_w_in[j][:, pcc * 512:(pcc + 1) * 512].rearrange("(k p) c -> p k c", p=128),
                   f"wb{pidx % 2}")
            if pidx == 7:
                cx.collective(stown[l].ap().opt(), stall[l].ap().opt(), reads=["stown"], writes=["stall"])
                for g in range(4):
                    cx.collective(kown[l][g].ap().opt(), kall[l][g].ap().opt(), reads=[f"kown{g}"], writes=[f"kall{g}"])
                    cx.collective(vown[l][g].ap().opt(), vall[l][g].ap().opt(), reads=[f"vown{g}"], writes=[f"vall{g}"])

        issue_load(0)
        for pidx, pc in enumerate(order):
            w = v3(wb[pidx % 2], 8)
            wtok = f"wb{pidx % 2}"
            if pidx + 1 < 8:
                issue_load(pidx + 1)
            kind = pc // 2
            ddef = []
            if kind in (0, 1):
                mx = qmx if kind == 0 else kmx
                mtok = "qmx" if kind == 0 else "kmx"
                for tc in range(4):
                    for hh in range(4):
                        h = (pc % 2) * 4 + hh
                        bk = proj(128, [w[:, k, hh * 128:(hh + 1) * 128] for k in range(8)],
                                  [hT[:, k, tcs(tc)] for k in range(8)], [wtok, f"hT{tc}"])
                        while ddef:
                            ddef.pop(0)()
                        sq = sqb[:, hh, :]
                        if kind == 0:
                            s0, s1 = stg[(si % 2) * 2], stg[(si % 2) * 2 + 1]
                            t0, t1k = f"stg{(si % 2) * 2}", f"stg{(si % 2) * 2 + 1}"
                            si += 1
                            cx.op("act", lambda e: e.activation(out=s0[0:64, :], in_=bank(bk)[0:64, :], func=AF.Copy, scale=0.125),
                                  reads=[btok(bk)], writes=[t0])
                            cx.op("act", lambda e: e.activation(out=s1[64:128, :], in_=bank(bk)[64:128, :], func=AF.Copy, scale=0.125),
                                  reads=[btok(bk)], writes=[t1k])
                            cx.op("act", lambda e: e.activation(out=sq, in_=bank(bk), func=AF.Square, scale=0.125),
                                  reads=[btok(bk)], writes=[f"sqb{hh}"])
                            cx.dma("sp", qs[l][2 * h][:, tcs(tc)], s0, reads=[t0], writes=["qs"])
                            cx.dma("sp", qs[l][2 * h + 1][:, tcs(tc)], s1, reads=[t1k], writes=["qs"])
                        else:
                            st = vst[si % 2][:, 0:512]
                            stok = f"vst{si % 2}"
                            si += 1
                            cx.op("act", lambda e: e.activation(out=st, in_=bank(bk), func=AF.Copy), reads=[btok(bk)], writes=[stok])
                            cx.op("act", lambda e: e.activation(out=sq, in_=bank(bk), func=AF.Square),
                                  reads=[btok(bk)], writes=[f"sqb{hh}"])
                            g = h // 2
                            cx.dma("sp", kown[l][g][(h % 2) * 128:(h % 2) * 128 + 128, tcs(tc)], st, reads=[stok],
                                   writes=[f"kown{g}"])
                        def ind_mm(h=h, hh=hh, sq=sq):
                            cx.op("pe", lambda e: e.matmul(bank(1)[0:16, :], ind2[:, h * 16:(h + 1) * 16], sq,
                                                           start=(hh == 0), stop=(hh == 3)),
                                  reads=[f"sqb{hh}", "ind2"], writes=[btok(1)], inc=(hh == 3))
                        ddef.append(ind_mm)
                    while ddef:
                        ddef.pop(0)()
                    col = (pc % 2) * 4 + tc
                    cx.op("dve", lambda e: e.tensor_reduce(out=mx[:, col:col + 1], in_=bank(1)[0:16, :], axis=AX.X, op=ALU.max),
                          reads=[btok(1)], writes=[mtok])
                if pc == 1:
                    cx.op("dve", lambda e: e.tensor_tensor(out=qmax[:], in0=qmx[:, 0:4], in1=qmx[:, 4:8], op=ALU.max),
                          reads=["qmx"], writes=["qmax"])
                if pc == 3:
                    cx.dma("sp", stown[l][:, :], kmx[:], reads=["kmx"], writes=["stown"])
            elif kind == 2:
                for tb in range(16):
                    bk = proj(128, [hT[:, k, tb * 128:(tb + 1) * 128] for k in range(8)], [w[:, k, :] for k in range(8)],
                              [wtok, f"hT{tb // 4}"])
                    vs = vst[tb % 2][:, 0:512]
                    vtok = f"vst{tb % 2}"
                    cx.op("act" if tb % 2 == 0 else "dve",
                          (lambda e: e.activation(out=vs, in_=bank(bk), func=AF.Copy)) if tb % 2 == 0
                          else (lambda e: e.tensor_copy(out=vs, in_=bank(bk))),
                          reads=[btok(bk)], writes=[vtok])
                    for gg in range(2):
                        g = (pc % 2) * 2 + gg
                        cx.dma("sp", vown[l][g][tb * 128:(tb + 1) * 128, :], vs[:, gg * 256:(gg + 1) * 256], reads=[vtok],
                               writes=[f"vown{g}"])
            else:
                for ci in range(4):
                    for tc in range(4):
                        bk = proj(128, [w[:, k, ci * 128:(ci + 1) * 128] for k in range(8)],
                                  [hT[:, k, tcs(tc)] for k in range(8)], [wtok, f"hT{tc}"])
                        c8 = (pc % 2) * 4 + ci
                        cx.op("act", lambda e: e.activation(out=gog[:, c8, tcs(tc)], in_=bank(bk), func=AF.Silu),
                              reads=[btok(bk)], writes=[f"gog{c8}"])

    def a_prologue(l):
        cx.dma("sp", stl[:].rearrange("m (r c) -> m r c", r=4), stall[l].ap().rearrange("(r m) c -> m r c", r=4),
               reads=["stall"], writes=["stl"])
        cx.op("dve", lambda e: e.tensor_reduce(out=kmax[:], in_=stl[:], axis=AX.X, op=ALU.max), reads=["stl"], writes=["kmax"])
        cx.op("dve", lambda e: e.tensor_scalar(out=nb16[:], in0=qmax[:], scalar1=kmax[:, 0:1], scalar2=None, op0=ALU.mult),
              reads=["qmax", "kmax"], writes=["nb16"])
        cx.op("act", lambda e: e.activation(out=nb16[:], in_=nb16[:], func=AF.Sqrt), reads=["nb16"], writes=["nb16"])
        cx.op("dve", lambda e: e.tensor_scalar(out=nb16[:], in0=nb16[:], scalar1=-1.02, scalar2=None, op0=ALU.mult),
              reads=["nb16"], writes=["nb16"])
        cx.dma("sp", nbd[l].rearrange("(m c) -> m c", m=16), nb16[:], reads=["nb16"], writes=["nbd"])
        cx.dma("sp", negb[:], nbd[l:l + 1, :].partition_broadcast(128), reads=["nbd"], writes=["negb"])

    def a_phase_mla(l):
        a_prologue(l)
        for b in range(2):
            cx.op("dve", lambda e: e.memset(Vt[b][:, :, 64:128], 1.0), writes=[f"Vt{b}"])
            for r in range(4):
                cx.dma("sp", Kt[b][64:96, r * T:(r + 1) * T], small[l][r * 32:(r + 1) * 32, :], reads=["small"],
                       writes=[f"Kr{b}"])

        def loads(h):
            b = h % 2
            g, hh = h // 4, h % 4
            cx.dma("sp", qt[b][0:96, :], qs[l][h][0:96, :], reads=["qs"], writes=[f"qt{b}"])
            for r in range(4):
                cx.dma("sp", Kt[b][0:64, r * T:(r + 1) * T], kall[l][g][r * 256 + hh * 64:r * 256 + hh * 64 + 64, :],
                       reads=[f"kall{g}"], writes=[f"Kt{b}"])
            for r in range(4):
                cx.dma("sp", Vt[b][:, r * 16:(r + 1) * 16, 0:64],
                       vall[l][g][r * T:(r + 1) * T, hh * 64:(hh + 1) * 64].rearrange("(k p) c -> p k c", p=128),
                       reads=[f"vall{g}"], writes=[f"Vt{b}"])

        loads(0)
        it = 0
        for h in range(16):
            if h + 1 < 16:
                loads(h + 1)
            b = h % 2
            for tc in range(4):
                ab = 4 + (it % 2)
                it += 1
                NU = 32
                bias_ap = negb[:, h * 4 + tc:h * 4 + tc + 1]

                SB = [PS[0], PS[1], PS[3]]

                def qk(u):
                    for kk in range(2):
                        kb = 2 * u + kk
                        cx.op("pe", lambda e: e.matmul(SB[u % 3][:, kk * 512:(kk + 1) * 512], Kt[b][0:96, kb * 128:(kb + 1) * 128],
                                                       qt[b][0:96, tcs(tc)], start=True, stop=True),
                              reads=[f"Kt{b}", f"Kr{b}", f"qt{b}"], writes=[f"S{u % 3}"], inc=(kk == 1))

                def ex(u):
                    cx.op("act", lambda e: e.activation(out=Pt[u % 3], in_=SB[u % 3][:, :], func=AF.Exp, bias=bias_ap, scale=1.0),
                          reads=[f"S{u % 3}", "negb"], writes=[f"P{u % 3}"])

                def pv(u):
                    for kk in range(2):
                        kb = 2 * u + kk
                        cx.op("pe", lambda e: e.matmul(bank(ab), Vt[b][:, kb, :], Pt[u % 3][:, kk * 512:(kk + 1) * 512],
                                                       start=(kb == 0), stop=(kb == 63)),
                              reads=[f"Vt{b}", f"P{u % 3}"], writes=[btok(ab)], inc=(kk == 1))

                qk(0)
                qk(1)
                for u in range(NU):
                    if u + 2 < NU:
                        qk(u + 2)
                    ex(u)
                    pv(u)
                ri = rinv[it % 2]
                ob = osb[it % 2]
                pb = (h % 2) * 64
                cx.op("dve", lambda e: e.reciprocal(out=ri[0:64, :], in_=bank(ab)[64:128, :]), reads=[btok(ab)],
                      writes=[f"rinv{it % 2}"])
                cx.op("dve", lambda e: e.tensor_tensor(out=ob[pb:pb + 64, :], in0=bank(ab)[0:64, :], in1=ri[0:64, :], op=ALU.mult),
                      reads=[btok(ab), f"rinv{it % 2}"], writes=[f"osb{it % 2}"])
                cx.op("dve", lambda e: e.tensor_tensor(out=gog[pb:pb + 64, h // 2, tcs(tc)], in0=ob[pb:pb + 64, :],
                                                        in1=gog[pb:pb + 64, h // 2, tcs(tc)], op=ALU.mult),
                      reads=[f"osb{it % 2}", f"gog{h // 2}"], writes=[f"gog{h // 2}"])

    def a_phase_diff(l):
        j = l // 2
        a_prologue(l)
        cx.dma("sp", posbc, posd[:, :], writes=["posbc"])
        nv = negb[:].rearrange("p (h c t) -> p h c t", h=8, c=2)
        cx.op("dve", lambda e: e.tensor_tensor(out=negbh[:].rearrange("p (h t) -> p h t", h=8), in0=nv[:, :, 0, :],
                                               in1=nv[:, :, 1, :], op=ALU.min), reads=["negb"], writes=["negbh"])

        def loads(h):
            b = h % 2
            g = h // 2
            for c in range(2):
                cx.dma("sp", qt[2 * b + c], qs[l][2 * h + c][:, :], reads=["qs"], writes=[f"qt{2 * b + c}"])
            for r in range(4):
                cx.dma("sp", Kt[b][:, r * T:(r + 1) * T], kall[l][g][r * 256 + (h % 2) * 128:r * 256 + (h % 2) * 128 + 128, :],
                       reads=[f"kall{g}"], writes=[f"Kt{b}"])
            for r in range(4):
                cx.dma("sp", Vt[b][:, r * 16:(r + 1) * 16, :],
                       vall[l][g][r * T:(r + 1) * T, (h % 2) * 128:(h % 2) * 128 + 128].rearrange("(k p) c -> p k c", p=128),
                       reads=[f"vall{g}"], writes=[f"Vt{b}"])

        loads(0)
        it = 0
        for h in range(8):
            if h + 1 < 8:
                loads(h + 1)
            b = h % 2
            slope = 2.0 ** (-(h + 1))
            for tc in range(4):
                it += 1
                NU = 64

                def qk(u):
                    for c in range(2):
                        cx.op("pe", lambda e: e.matmul(bank((u % 2) * 2 + c), Kt[b][c * 64:(c + 1) * 64, u * 128:(u + 1) * 128],
                                                       qt[2 * b + c][c * 64:(c + 1) * 64, tcs(tc)], start=True, stop=True),
                              reads=[f"Kt{b}", f"qt{2 * b + c}"], writes=[f"S{u % 2}"], inc=(c == 1))

                def bias(u):
                    cx.op("act", lambda e: e.activation(out=tt[u % 4], in_=posbc[:, tcs(tc)], func=AF.Abs,
                                                        bias=negpk[:, u:u + 1], scale=1.0),
                          reads=["posbc", "negpk"], writes=[f"tt{u % 4}"])

                def stt(u):
                    tm = tmpf[u % 3]
                    for c in range(2):
                        cx.op("dve", lambda e: e.scalar_tensor_tensor(out=tm[:, c * 512:(c + 1) * 512], in0=tt[u % 4], scalar=-slope,
                                                                     in1=bank((u % 2) * 2 + c), op0=ALU.mult, op1=ALU.add),
                              reads=[f"tt{u % 4}", f"S{u % 2}"], writes=[f"tm{u % 3}"])

                def exa(u):
                    cx.op("act", lambda e: e.activation(out=Pt[u % 3], in_=tmpf[u % 3], func=AF.Exp,
                                                        bias=negbh[:, h * 4 + tc:h * 4 + tc + 1], scale=1.0),
                          reads=[f"tm{u % 3}", "negbh"], writes=[f"P{u % 3}"])

                def pv(u):
                    for c in range(2):
                        cx.op("pe", lambda e: e.matmul(bank(4 + c), Vt[b][:, u, :], Pt[u % 3][:, c * 512:(c + 1) * 512],
                                                       start=(u == 0), stop=(u == 63)),
                              reads=[f"Vt{b}", f"P{u % 3}"], writes=[btok(4 + c)], inc=False)
                    for c in range(2):
                        cx.op("pe", lambda e: e.matmul(bank(6 + c), ones_bf[:, :], Pt[u % 3][:, c * 512:(c + 1) * 512],
                                                       start=(u == 0), stop=(u == 63)),
                              reads=["ones", f"P{u % 3}"], writes=[btok(6 + c)], inc=(c == 1))

                qk(0)
                bias(0)
                qk(1)
                bias(1)
                bias(2)
                for u in range(NU):
                    stt(u)
                    if u + 2 < NU:
                        qk(u + 2)
                    if u + 3 < NU:
                        bias(u + 3)
                    exa(u)
                    pv(u)
                for c in range(2):
                    cx.op("act", lambda e: e.activation(out=rinv[c], in_=bank(6 + c), func=AF.Ln), reads=[btok(6 + c)],
                          writes=[f"rinv{c}"])
                    cx.op("act", lambda e: e.activation(out=rinv[c], in_=rinv[c], func=AF.Exp, scale=-1.0), reads=[f"rinv{c}"],
                          writes=[f"rinv{c}"])
                    cx.op("dve", lambda e: e.tensor_tensor(out=osb[c], in0=bank(4 + c), in1=rinv[c], op=ALU.mult),
                          reads=[btok(4 + c), f"rinv{c}"], writes=[f"osb{c}"])
                cx.op("dve", lambda e: e.scalar_tensor_tensor(out=osb[0], in0=osb[1], scalar=neglam[:, j:j + 1], in1=osb[0],
                                                             op0=ALU.mult, op1=ALU.add),
                      reads=["osb0", "osb1", "neglam"], writes=["osb0"])
                cx.op("act", lambda e: e.activation(out=sqo, in_=osb[0], func=AF.Square), reads=["osb0"], writes=["sqo"])
                cx.op("pe", lambda e: e.matmul(bank(6), ones_bf[:, :], sqo, start=True, stop=True), reads=["sqo", "ones"],
                      writes=[btok(6)])
                cx.op("act", lambda e: e.activation(out=rr, in_=bank(6), func=AF.Ln, bias=epst[:, 0:1], scale=1.0 / 128),
                      reads=[btok(6), "eps"], writes=["rr"])
                cx.op("act", lambda e: e.activation(out=rr, in_=rr, func=AF.Exp, scale=-0.5), reads=["rr"], writes=["rr"])
                cx.op("dve", lambda e: e.scalar_tensor_tensor(out=osb[1], in0=osb[0], scalar=hgs[:, j:j + 1], in1=rr,
                                                             op0=ALU.mult, op1=ALU.mult),
                      reads=["osb0", "rr", "hgs"], writes=["osb1"])
                cx.op("dve", lambda e: e.tensor_tensor(out=gog[:, h, tcs(tc)], in0=osb[1], in1=gog[:, h, tcs(tc)], op=ALU.mult),
                      reads=["osb1", f"gog{h}"], writes=[f"gog{h}"])

    def out_proj(l, w_o):
        j = l // 2
        for k in range(8):
            cx.dma("sp", xT[:, k, :], xs[k * 128:(k + 1) * 128, :], reads=["xs"], writes=[f"xT{k}"])
        for pc in range(2):
            load_w(v3(wb[pc], 8), w_o[j][:, pc * 512:(pc + 1) * 512].rearrange("(k p) c -> p k c", p=128), f"wb{pc}")
        for pc in range(2):
            w = v3(wb[pc], 8)
            wtok = f"wb{pc}"
            for jj in range(4):
                jc = pc * 4 + jj
                for tc in range(4):
                    bk = proj(128, [w[:, k, jj * 128:(jj + 1) * 128] for k in range(8)], [gog[:, k, tcs(tc)] for k in range(8)],
                              [wtok] + [f"gog{k}" for k in range(8)])
                    cx.op("dve", lambda e: e.scalar_tensor_tensor(out=xT[:, jc, tcs(tc)], in0=bank(bk),
                                                                 scalar=mods[:, l * 24 + 16 + jc:l * 24 + 17 + jc],
                                                                 in1=xT[:, jc, tcs(tc)], op0=ALU.mult, op1=ALU.add),
                          reads=[btok(bk), "mods", f"xT{jc}"], writes=[f"xT{jc}"])

    def final_norm():
        for tc in range(4):
            cx.op("act", lambda e: e.activation(out=sqb[:, :, :], in_=xT[:, :, tcs(tc)], func=AF.Square),
                  reads=[f"xT{k}" for k in range(8)], writes=["sqb"])
            norm_stats(tc, 1024, [sqb[:, k, :] for k in range(8)], ["sqb"] * 8, rstd, 0)
            for k in range(8):
                tb = t1[k % 2]
                cx.op("dve", lambda e: e.scalar_tensor_tensor(out=tb, in0=xT[:, k, tcs(tc)], scalar=fgl[:, k:k + 1], in1=rstd,
                                                             op0=ALU.mult, op1=ALU.mult),
                      reads=[f"xT{k}", "rstd", "fgl"], writes=[f"t1{k % 2}"])
                cx.dma("sp", y_out[k * 128:(k + 1) * 128, tcs(tc)], tb, reads=[f"t1{k % 2}"], writes=["yout"])

    def main_body():
        ckpt(1)
        for li, l in enumerate(layers):
            if l % 2 == 0:
                p_phase_mla(l)
                cx.barrier()
                ckpt(2)
                a_prologue_only = int(os.environ.get("KSTOP", "99")) == 3
                if a_prologue_only:
                    a_prologue(l)
                    ckpt(3)
                a_phase_mla(l)
                cx.barrier()
                ckpt(4)
                out_proj(l, m_wo)
            else:
                p_phase_diff(l)
                cx.barrier()
                ckpt(2)
                a_phase_diff(l)
                cx.barrier()
                ckpt(4)
                out_proj(l, d_wo)
            if li + 1 < len(layers):
                cx.new_epoch()
        final_norm()

    try:
        main_body()
    except StopBuild:
        pass
    cx.barrier()
    blk.__exit__(None, None, None)
    return nc, cx


def host_inputs(inp):
    f32 = np.float32
    x = np.asarray(inp["x"], f32)
    c = np.asarray(inp["c"], f32)
    pos = np.asarray(inp["positions"], np.int32)

    def lay(v, n):
        return np.ascontiguousarray(np.asarray(v, f32).reshape(n, 128).T)

    shared = {}
    shared["ada_w"] = np.ascontiguousarray(np.asarray(inp["ada_w"], f32))
    shared["ada_b_lay"] = np.stack([lay(inp["ada_b"][l], 24) for l in range(4)])
    shared["norm_g_lay"] = np.stack([lay(inp["norm_g"][l], 8) for l in range(4)])
    shared["final_g_lay"] = lay(inp["final_g"], 8)
    w_in = np.asarray(inp["mla_w_in"], f32)
    shared["m_w_in"] = np.ascontiguousarray(w_in)
    rot_cols = list(range(656, 672)) + list(range(640, 656))
    shared["m_w_rot"] = np.ascontiguousarray(w_in[:, :, rot_cols])
    shared["m_gq"] = np.stack([lay(inp["mla_q_norm_g"][j], 3) for j in range(2)])
    shared["m_gkv"] = np.stack([lay(inp["mla_kv_norm_g"][j], 2) for j in range(2)])
    wq = np.asarray(inp["mla_w_q_up"], f32).reshape(2, 384, 16, 96)
    shared["m_wq"] = np.ascontiguousarray(wq.reshape(2, 384, 16 * 96))
    wq_rot = np.concatenate([wq[..., 80:96], wq[..., 64:80]], axis=-1)
    shared["m_wqr"] = np.ascontiguousarray(wq_rot.reshape(2, 384, 16 * 32))
    wkv = np.asarray(inp["mla_w_kv_up"], f32).reshape(2, 256, 16, 128)
    shared["m_wk"] = np.ascontiguousarray(wkv[..., 0:64].reshape(2, 256, 1024))
    shared["m_wv"] = np.ascontiguousarray(wkv[..., 64:128].reshape(2, 256, 1024))
    shared["m_wo"] = np.ascontiguousarray(np.asarray(inp["mla_w_o"], f32))
    shared["d_w_in"] = np.ascontiguousarray(np.asarray(inp["diff_w_in"], f32))
    shared["d_wo"] = np.ascontiguousarray(np.asarray(inp["diff_w_o"], f32))
    shared["d_hg"] = np.ascontiguousarray(np.asarray(inp["diff_head_g"], f32).reshape(2, 128).T)
    lq = np.stack([np.stack([np.broadcast_to(np.asarray(inp[n], f32)[j][None, :], (128, 64))
                             for n in ("diff_lq1", "diff_lk1", "diff_lq2", "diff_lk2")]) for j in range(2)])
    shared["d_lq"] = np.ascontiguousarray(lq)
    ind2 = np.zeros((128, 8, 16), f32)
    for hp in range(8):
        ind2[0:64, hp, 2 * hp] = 1.0
        ind2[64:128, hp, 2 * hp + 1] = 1.0
    shared["k_ind2"] = ind2.reshape(128, 128)
    indh = np.zeros((128, 16, 16), f32)
    for h in range(16):
        indh[:, h, h] = 1.0
    shared["k_indh"] = indh.reshape(128, 256)
    inv = (10000.0 ** (-np.arange(0, 32, 2, dtype=np.float32) / 32)).astype(f32)
    p = np.arange(128)
    shared["k_invf"] = inv[p % 16].reshape(128, 1).astype(f32)
    shared["k_sgn"] = np.where((p % 32) < 16, -1.0, 1.0).reshape(128, 1).astype(f32)

    maps = []
    for core in range(NCORE):
        b, r = core // 4, core % 4
        m = dict(shared)
        m["xT_in"] = np.ascontiguousarray(x[b, r * T:(r + 1) * T, :].T)
        m["c_lay"] = lay(c[b], 8)
        m["pos_own"] = np.ascontiguousarray(np.broadcast_to(pos[b, r * T:(r + 1) * T][None, :], (128, T))).astype(np.int32)
        m["pos_all"] = np.ascontiguousarray(pos[b].reshape(64, 128).T).astype(np.int32)
        m["pos_c0"] = np.full((128, 1), pos[b, 0], np.int32)
        maps.append(m)
    return maps


_CACHE = {}


def kernel(**inputs):
    key = tuple(LAYERS)
    if key not in _CACHE:
        _CACHE[key] = build_program(LAYERS)
    nc, cx = _CACHE[key]
    maps = host_inputs(inputs)
    res = run_bass_kernel_spmd(nc, maps, core_ids=list(range(NCORE)))
    out = np.empty((2, S, D), np.float32)
    for core in range(NCORE):
        b, r = core // 4, core % 4
        out[b, r * T:(r + 1) * T, :] = res.results[core]["yT_out"].T
    return out
```
